# Optimizing a Trainium2 kernel written in Bass

```python
import jax, jax.numpy as jnp
from jax import lax
import numpy as np

D_MODEL = 1024
BATCH = 16
SEQ = 256
DEPTH = 1
DEC_BATCH = 8
DEC_SEQ = 1024
PAST_LEN = 256

GRID_W = 64
CHUNK = 128
EPS = 1e-6
SSD_WIDTH = D_MODEL
SSD_HEAD_DIM = 64
SSD_HEADS = SSD_WIDTH // SSD_HEAD_DIM
SSD_GROUPS = 4
SSD_STATE = 128
CONV_K = 3
CONV_CH = SSD_WIDTH + 2 * SSD_GROUPS * SSD_STATE
RET_HEADS = 8
RET_QK_DIM = 64
RET_V_DIM = 128
RET_QK_WIDTH = RET_HEADS * RET_QK_DIM
RET_V_WIDTH = RET_HEADS * RET_V_DIM
MIX_WIDTH = SSD_WIDTH + RET_V_WIDTH
ROPE_BASE = 10000.0
IN_COLS = SSD_WIDTH + CONV_CH + 2 * SSD_HEADS + 2 * RET_QK_WIDTH + 2 * RET_V_WIDTH

kernel_name = 'hybrid_ssd_retention_flow_step'


def _rms(x, w):
    xf = x.astype(jnp.float32)
    y = xf * lax.rsqrt(jnp.mean(xf * xf, axis=-1, keepdims=True) + EPS)
    return (y * w).astype(x.dtype)


def _dwconv(x, w, b):
    y = lax.conv_general_dilated(x, w[:, None, :].astype(x.dtype), (1,),
                                 [(CONV_K // 2, CONV_K // 2)],
                                 dimension_numbers=('NWC', 'WIO', 'NWC'),
                                 feature_group_count=x.shape[-1])
    return y + b.astype(x.dtype)


def _grid_angles(L):
    rows = L // GRID_W
    row = jnp.repeat(jnp.arange(rows, dtype=jnp.float32), GRID_W)
    col = jnp.tile(jnp.arange(GRID_W, dtype=jnp.float32), rows)
    half = RET_QK_DIM // 2
    inv = ROPE_BASE ** (-jnp.arange(0, half, 2, dtype=jnp.float32) / half)
    return row[:, None] * inv, col[:, None] * inv


def _rot(x, ang):
    x1, x2 = jnp.split(x, 2, axis=-1)
    c = jnp.cos(ang)[None, :, None, :]
    s = jnp.sin(ang)[None, :, None, :]
    return jnp.concatenate([x1 * c - x2 * s, x2 * c + x1 * s], axis=-1)


def _grid_rope(x, angles):
    xr, xc = jnp.split(x, 2, axis=-1)
    return jnp.concatenate([_rot(xr, angles[0]), _rot(xc, angles[1])], axis=-1)


def _chunk_scan(q, k, v, log_a, s0):
    Bsz, L, H, _ = q.shape
    P = v.shape[-1]
    nc = L // CHUNK

    def to_chunks(t):
        return jnp.moveaxis(t.reshape((Bsz, nc, CHUNK) + t.shape[2:]), 1, 0)

    causal = jnp.tril(jnp.ones((CHUNK, CHUNK), dtype=bool))[None, :, :, None]

    def step(s, inp):
        qi, ki, vi, ai = inp
        cum = jnp.cumsum(ai, axis=1)
        seg = cum[:, :, None, :] - cum[:, None, :, :]
        decay = jnp.where(causal, jnp.exp(jnp.where(causal, seg, 0.0)), 0.0)
        scores = jnp.einsum('bihn,bjhn->bijh', qi, ki) * decay
        y = (jnp.einsum('bijh,bjhp->bihp', scores, vi)
             + jnp.einsum('bihn,bhnp->bihp', qi, s) * jnp.exp(cum)[..., None])
        tail = jnp.exp(cum[:, -1:, :] - cum)
        s_new = (s * jnp.exp(cum[:, -1, :])[:, :, None, None]
                 + jnp.einsum('bjhn,bjhp,bjh->bhnp', ki, vi, tail))
        return s_new, y

    s_fin, ys = lax.scan(step, s0, (to_chunks(q), to_chunks(k), to_chunks(v), to_chunks(log_a)))
    y = jnp.moveaxis(ys, 0, 1).reshape(Bsz, L, H, P)
    return y, s_fin


def _bidir(q, k_f, k_b, v, la_f, la_b, s0_f, s0_b):
    y_f, s_f = _chunk_scan(q, k_f, v, la_f, s0_f)
    fl = lambda t: jnp.flip(t, axis=1)
    y_b, s_b = _chunk_scan(fl(q), fl(k_b), fl(v), fl(la_b), s0_b)
    return y_f + fl(y_b), s_f, s_b


def _mixer(h, s0_ssd, s0_ret, angles, w_in, conv_w, conv_b, A_log, dt_bias, D_skip,
           ssd_norm_w, ret_decay, ret_norm_w, w_out):
    f32 = jnp.float32
    Bsz, L, _ = h.shape
    u = h @ w_in
    sizes = [SSD_WIDTH, CONV_CH, 2 * SSD_HEADS, RET_QK_WIDTH, RET_QK_WIDTH, RET_V_WIDTH]
    z, xbc, dt_raw, q, k, v, g = jnp.split(u, [int(i) for i in np.cumsum(sizes)], axis=-1)

    xbc = jax.nn.silu(_dwconv(xbc, conv_w, conv_b))
    gn = SSD_GROUPS * SSD_STATE
    xs, Bm, Cm = jnp.split(xbc, [SSD_WIDTH, SSD_WIDTH + gn], axis=-1)
    xs = xs.reshape(Bsz, L, SSD_HEADS, SSD_HEAD_DIM).astype(f32)
    rep = SSD_HEADS // SSD_GROUPS
    Bm = jnp.repeat(Bm.reshape(Bsz, L, SSD_GROUPS, SSD_STATE).astype(f32), rep, axis=2)
    Cm = jnp.repeat(Cm.reshape(Bsz, L, SSD_GROUPS, SSD_STATE).astype(f32), rep, axis=2)
    dt = jax.nn.softplus(dt_raw.reshape(Bsz, L, 2, SSD_HEADS).astype(f32) + dt_bias.astype(f32))
    log_a = dt * (-jnp.exp(A_log.astype(f32)))
    k_dir = Bm[:, :, None] * dt[..., None]
    y_s, sf, sb = _bidir(Cm, k_dir[:, :, 0], k_dir[:, :, 1], xs, log_a[:, :, 0], log_a[:, :, 1],
                         s0_ssd[:, 0].astype(f32), s0_ssd[:, 1].astype(f32))
    y_s = (y_s + D_skip.astype(f32)[:, None] * xs).reshape(Bsz, L, SSD_WIDTH)
    y_s = _rms(y_s * jax.nn.silu(z.astype(f32)), ssd_norm_w)

    qr = q.reshape(Bsz, L, RET_HEADS, RET_QK_DIM).astype(f32)
    kr = k.reshape(Bsz, L, RET_HEADS, RET_QK_DIM).astype(f32) * (RET_QK_DIM ** -0.5)
    if angles is not None:
        qr = _grid_rope(qr, angles)
        kr = _grid_rope(kr, angles)
    vr = v.reshape(Bsz, L, RET_HEADS, RET_V_DIM).astype(f32)
    lam = -jnp.exp(ret_decay.astype(f32))
    la_f = jnp.broadcast_to(lam[0], (Bsz, L, RET_HEADS))
    la_b = jnp.broadcast_to(lam[1], (Bsz, L, RET_HEADS))
    y_r, rf, rb = _bidir(qr, kr, kr, vr, la_f, la_b,
                         s0_ret[:, 0].astype(f32), s0_ret[:, 1].astype(f32))
    mu = jnp.mean(y_r, axis=-1, keepdims=True)
    var = jnp.mean(jnp.square(y_r - mu), axis=-1, keepdims=True)
    y_r = ((y_r - mu) * lax.rsqrt(var + EPS)).reshape(Bsz, L, RET_V_WIDTH)
    y_r = y_r * ret_norm_w * jax.nn.silu(g.astype(f32))

    mix = jnp.concatenate([y_s, y_r], axis=-1).astype(h.dtype)
    out = mix @ w_out
    return out, jnp.stack([sf, sb], axis=1), jnp.stack([rf, rb], axis=1)


def _layer(x, cond, s0_ssd, s0_ret, angles, w_mod, b_mod, norm_pre_w, norm_post_w, w_in,
           conv_w, conv_b, A_log, dt_bias, D_skip, ssd_norm_w, ret_decay, ret_norm_w, w_out):
    mod = jax.nn.silu(cond) @ w_mod + b_mod
    shift, scale, gate = jnp.split(mod, 3, axis=-1)
    h = _rms(x, norm_pre_w) * (1 + scale[:, None, :]) + shift[:, None, :]
    out, s_ssd, s_ret = _mixer(h.astype(x.dtype), s0_ssd, s0_ret, angles, w_in, conv_w, conv_b,
                               A_log, dt_bias, D_skip, ssd_norm_w, ret_decay, ret_norm_w, w_out)
    y = x + gate[:, None, :] * _rms(out, norm_post_w)
    return y.astype(x.dtype), s_ssd, s_ret


def setup_inputs(seed: int = 0) -> dict:
    key = jax.random.key(seed)
    ks = jax.random.split(key, 20)
    f32 = jnp.float32
    nrm = lambda k, s: jax.random.normal(k, s, f32)
    dt = jnp.exp(jax.random.uniform(ks[12], (DEPTH, 2, SSD_HEADS), f32)
                 * (jnp.log(0.1) - jnp.log(0.001)) + jnp.log(0.001))
    gammas = 1.0 - 2.0 ** (-5.0 - jnp.arange(RET_HEADS, dtype=f32))
    ret_base = jnp.log(-jnp.log(gammas))
    return {
        'x_prompt': nrm(ks[0], (BATCH, SEQ, D_MODEL)),
        'x_sample': nrm(ks[1], (DEC_BATCH, DEC_SEQ, D_MODEL)),
        'state_ssd': 0.1 * nrm(ks[2], (DEC_BATCH, DEPTH, 2, SSD_HEADS, SSD_STATE, SSD_HEAD_DIM)),
        'state_ret': 0.1 * nrm(ks[3], (DEC_BATCH, DEPTH, 2, RET_HEADS, RET_QK_DIM, RET_V_DIM)),
        'c': nrm(ks[4], (DEC_BATCH, D_MODEL)),
        'c_ctx': nrm(ks[5], (D_MODEL,)),
        'w_mod': nrm(ks[6], (DEPTH, D_MODEL, 3 * D_MODEL)) * D_MODEL ** -0.5,
        'b_mod': 0.02 * nrm(ks[7], (DEPTH, 3 * D_MODEL)),
        'norm_pre_w': 1.0 + 0.02 * nrm(ks[8], (DEPTH, D_MODEL)),
        'norm_post_w': 1.0 + 0.02 * nrm(ks[9], (DEPTH, D_MODEL)),
        'w_in': nrm(ks[10], (DEPTH, D_MODEL, IN_COLS)) * D_MODEL ** -0.5,
        'conv_w': nrm(ks[11], (DEPTH, CONV_K, CONV_CH)) * CONV_K ** -0.5,
        'conv_b': 0.02 * nrm(ks[13], (DEPTH, CONV_CH)),
        'ssd_A_log': jnp.log(jax.random.uniform(ks[14], (DEPTH, 2, SSD_HEADS), f32, 1.0, 16.0)),
        'ssd_dt_bias': dt + jnp.log(-jnp.expm1(-dt)),
        'ssd_D': 1.0 + 0.02 * nrm(ks[15], (DEPTH, SSD_HEADS)),
        'ssd_norm_w': 1.0 + 0.02 * nrm(ks[16], (DEPTH, SSD_WIDTH)),
        'ret_decay': ret_base + 0.01 * nrm(ks[17], (DEPTH, 2, RET_HEADS)),
        'ret_norm_w': 1.0 + 0.02 * nrm(ks[18], (DEPTH, RET_V_WIDTH)),
        'w_out': nrm(ks[19], (DEPTH, MIX_WIDTH, D_MODEL)) * MIX_WIDTH ** -0.5,
    }


def reference(x_prompt, x_sample, state_ssd, state_ret, c, c_ctx, w_mod, b_mod, norm_pre_w,
              norm_post_w, w_in, conv_w, conv_b, ssd_A_log, ssd_dt_bias, ssd_D, ssd_norm_w,
              ret_decay, ret_norm_w, w_out):
    angles = _grid_angles(x_sample.shape[1])
    nb = x_prompt.shape[0]
    zeros_ssd = jnp.zeros((nb, 2, SSD_HEADS, SSD_STATE, SSD_HEAD_DIM), jnp.float32)
    zeros_ret = jnp.zeros((nb, 2, RET_HEADS, RET_QK_DIM, RET_V_DIM), jnp.float32)
    yp, ys = x_prompt, x_sample
    ssd_states, ret_states = [], []
    for l in range(DEPTH):
        p = (w_mod[l], b_mod[l], norm_pre_w[l], norm_post_w[l], w_in[l], conv_w[l], conv_b[l],
             ssd_A_log[l], ssd_dt_bias[l], ssd_D[l], ssd_norm_w[l], ret_decay[l], ret_norm_w[l],
             w_out[l])
        yp, s_ssd, s_ret = _layer(yp, c_ctx[None, :], zeros_ssd, zeros_ret, None, *p)
        ys, _, _ = _layer(ys, c, state_ssd[:, l], state_ret[:, l], angles, *p)
        ssd_states.append(s_ssd)
        ret_states.append(s_ret)
    new_state_ssd = jnp.stack(ssd_states, axis=1)
    new_state_ret = jnp.stack(ret_states, axis=1)
    return (yp, ys, new_state_ssd, new_state_ret)
```

```python
import numpy as np
from contextlib import ExitStack
import concourse.bass as bass
import concourse.mybir as mybir
from concourse.bass_utils import run_bass_kernel_spmd

F32 = mybir.dt.float32
BF16 = mybir.dt.bfloat16
AF = mybir.ActivationFunctionType
ALU = mybir.AluOpType
AX = mybir.AxisListType

ENGS = ("pe", "act", "dve", "pool", "sp")
EPS = 1e-6
NCORES = 8
C_Q, C_QSW, C_K, C_KSW, C_V, C_G, C_XS, C_B, C_C, C_DT, C_Z = 0, 512, 1024, 1536, 2048, 3072, 4096, 5120, 5632, 6144, 6176
NCOLS = 7200


class Op:
    __slots__ = ("eng", "fn", "deps", "signal", "count", "dma_sem", "dma_count", "cost")

    def __init__(self, eng, fn, cost=0.3):
        self.eng = eng
        self.fn = fn
        self.cost = cost
        self.deps = []
        self.signal = False
        self.count = None
        self.dma_sem = None
        self.dma_count = None


class Sched:
    def __init__(self):
        self.ops = []
        self.last_writer = {}
        self.readers = {}
        self.dma_totals = {}
        self.dma_rr = 0
        self.dma_rr_sw = 0
        self.dma_last = {}

    NDMA = 24
    NDMA_SW = 4

    def add(self, eng, fn, reads=(), writes=(), dma=None, cost=0.3):
        op = Op(eng, fn, cost)
        seen = set()
        xb = [r for r in reads if len(r) == 2 and r[0] == "B" and r[1].isdigit()]
        if xb:
            reads = [r for r in reads if r not in xb]
            writes = list(writes) + xb

        def dep(d):
            if d is not None and id(d) not in seen:
                seen.add(id(d))
                op.deps.append(d)
        for r in reads:
            dep(self.last_writer.get(r))
        for r in writes:
            dep(self.last_writer.get(r))
            for rd in self.readers.get(r, ()):
                dep(rd)
        if dma is not None:
            if eng == "pool":
                dma = "sw%d" % (self.dma_rr_sw % self.NDMA_SW)
                self.dma_rr_sw += 1
            else:
                dma = "hw%d" % (self.dma_rr % self.NDMA)
                self.dma_rr += 1
            dep(self.dma_last.get(dma))
            self.dma_last[dma] = op
            op.dma_sem = dma
            self.dma_totals[dma] = self.dma_totals.get(dma, 0) + 16
            op.dma_count = self.dma_totals[dma]
        for r in reads:
            self.readers.setdefault(r, []).append(op)
        for r in writes:
            self.last_writer[r] = op
            self.readers[r] = []
        self.ops.append(op)
        return op

    def barrier(self):
        lastc = {}
        lastd = {}
        for op in self.ops:
            if op.fn is None:
                continue
            if op.dma_sem is not None:
                lastd[op.dma_sem] = op
            else:
                lastc[op.eng] = op
        allops = list(lastc.values()) + list(lastd.values())
        for e in ENGS:
            op = Op(e, None)
            op.deps = list(allops)
            self.ops.append(op)
        self.last_writer = {}
        self.readers = {}

    def _sched_segment(self, seg):
        n = len(seg)
        idx = {id(op): i for i, op in enumerate(seg)}
        indeg = [0] * n
        succ = [[] for _ in range(n)]
        for i, op in enumerate(seg):
            for d in op.deps:
                j = idx.get(id(d))
                if j is not None:
                    indeg[i] += 1
                    succ[j].append(i)
        tail = [0.0] * n
        for i in range(n - 1, -1, -1):
            m = 0.0
            for s in succ[i]:
                if tail[s] > m:
                    m = tail[s]
            tail[i] = m + seg[i].cost
        ready = [i for i in range(n) if indeg[i] == 0]
        t_eng = {e: 0.0 for e in ENGS}
        rdy = [0.0] * n
        order = []
        while ready:
            bi = None
            bk = None
            for i in ready:
                op = seg[i]
                st = t_eng[op.eng]
                if rdy[i] > st:
                    st = rdy[i]
                key = (st, -tail[i], i)
                if bk is None or key < bk:
                    bk = key
                    bi = i
            ready.remove(bi)
            op = seg[bi]
            st = bk[0]
            if op.dma_sem is not None:
                t_eng[op.eng] = st + 0.08
                fin = st + op.cost
            else:
                t_eng[op.eng] = st + op.cost
                fin = st + op.cost
            order.append(op)
            for s in succ[bi]:
                indeg[s] -= 1
                f2 = fin if seg[s].eng == op.eng and op.dma_sem is None else fin + 0.15
                if f2 > rdy[s]:
                    rdy[s] = f2
                if indeg[s] == 0:
                    ready.append(s)
        assert len(order) == n
        return order

    def schedule(self):
        new_ops = []
        seg = []
        lastc = {}

        def flush():
            for o in self._sched_segment(seg):
                new_ops.append(o)
                if o.dma_sem is None:
                    lastc[o.eng] = o
        for op in self.ops:
            if op.fn is None:
                if seg:
                    flush()
                    seg = []
                op.deps = [d for d in op.deps if d.dma_sem is not None] + list(lastc.values())
                new_ops.append(op)
            else:
                seg.append(op)
        if seg:
            flush()
        self.ops = new_ops

    def finalize(self, reorder=True):
        if reorder:
            self.schedule()
        for op in self.ops:
            for d in op.deps:
                if d.dma_sem is None and not (d.eng == op.eng and d.eng == "pe"):
                    d.signal = True
        counts = {e: 0 for e in ENGS}
        for op in self.ops:
            if op.signal and op.dma_sem is None:
                counts[op.eng] += 1
                op.count = counts[op.eng]
        known = {e: {} for e in ENGS}
        plan = {e: [] for e in ENGS}
        for op in self.ops:
            waits = {}
            for d in op.deps:
                if d.dma_sem is not None:
                    key = ("dma", d.dma_sem)
                    val = d.dma_count
                else:
                    if d.eng == op.eng and d.eng == "pe":
                        continue
                    key = ("eng", d.eng)
                    val = d.count
                if known[op.eng].get(key, 0) >= val:
                    continue
                if waits.get(key, 0) < val:
                    waits[key] = val
            for k, v in waits.items():
                known[op.eng][k] = v
            plan[op.eng].append((op, list(waits.items())))
        return plan


def run_plan(nc, plan, dma_keys):
    with ExitStack() as es:
        sems = {}
        for e in ENGS:
            sems[("eng", e)] = es.enter_context(nc.semaphore("s_" + e))
        for k in dma_keys:
            sems[("dma", k)] = es.enter_context(nc.semaphore("d_" + str(k)))
        block = es.enter_context(nc.Block())

        def body(ename):
            def f(eng):
                for op, waits in plan[ename]:
                    for key, val in waits:
                        eng.wait_ge(sems[key], val)
                    if op.fn is None:
                        continue
                    ins = op.fn(eng)
                    if op.dma_sem is not None:
                        ins.then_inc(sems[("dma", op.dma_sem)], 16)
                    elif op.signal:
                        ins.then_inc(sems[("eng", ename)], 1)
            return f

        block.tensor(body("pe"))
        block.scalar(body("act"))
        block.vector(body("dve"))
        block.gpsimd(body("pool"))
        block.sync(body("sp"))


class Arena:
    def __init__(self, nc, limit=229000):
        self.nc = nc
        self.top = 16640
        self.n = 0
        self.limit = limit
        self.peak = 0

    def alloc(self, name, shape, dt):
        nb = 4 if dt == F32 else 2
        size = nb
        for s in shape[1:]:
            size *= s
        size = (size + 63) // 64 * 64
        off = self.top
        self.top += size
        self.peak = max(self.peak, self.top)
        assert self.top <= self.limit, f"SBUF arena overflow at {name}: {self.top}"
        self.n += 1
        return self.nc.alloc_sbuf_tensor_at(f"{name}_{self.n}", list(shape), dt, offset=off)

    def mark(self):
        return self.top

    def release(self, m):
        self.top = m


class _Stop(Exception):
    pass


def build_nc(dbg=False, stop=None):
    nc = bass.Bass("TRN2", target_bir_lowering=False)

    dumps = {}

    def chk(name):
        if stop is not None and name == stop:
            if name in dumps:
                dumps[name]()
            raise _Stop()

    def din(name, shape):
        return nc.dram_tensor(name, list(shape), F32, kind="ExternalInput").ap()

    def dout(name, shape):
        return nc.dram_tensor(name, list(shape), F32, kind="ExternalOutput").ap()

    x_d = din("x", [1536, 1024])
    condfm_d = din("cond_fm", [128, 16])
    wmod_d = din("w_mod", [1024, 3072])
    bmod_d = din("b_mod", [1, 3072])
    npre_d = din("npre", [1, 1024])
    npost_d = din("npost", [1, 1024])
    win_d = din("w_in", [1024, NCOLS])
    convw_d = din("convw", [128, 48])
    convb_d = din("convb", [128, 16])
    vecs_d = din("vecs", [1, 96])
    snw_d = din("snw", [1, 1024])
    rnw_d = din("rnw", [1, 1024])
    wout_d = din("w_out", [2048, 1024])
    s0s_d = din("s0s", [2, 16, 128, 64])
    s0r_d = din("s0r", [2, 8, 64, 128])
    cm_d = din("cm", [128, 1408])
    cnt_d = din("cnt", [128, 16])
    sel_d = din("sel", [2, 256])
    cos_d = din("cosT", [128, 1024])
    sin_d = din("sinT", [128, 1024])
    y_d = dout("y", [1536, 1024])
    nss_d = dout("ns_s", [2, 2, 16, 128, 64])
    nsr_d = dout("ns_r", [2, 2, 8, 64, 128])

    dbg_d = dout("dbg", [128, 32768]) if dbg else None
    dbg_off = [0]
    S = Sched()
    A = Arena(nc)

    def dump(ap2d, name):
        if not dbg:
            return
        P, N = ap2d.shape
        S.barrier()
        scr = A.alloc("dbgscr", [128, 2048], F32)
        for c0 in range(0, N, 2048):
            n = min(2048, N - c0)
            S.add("dve", (lambda o, i_: lambda e: e.tensor_copy(out=o, in_=i_))(scr[0:P, 0:n], ap2d[:, c0:c0 + n]), [], ["dbgscr"])
            S.add("sp", (lambda o, i_: lambda e: e.dma_start(out=o, in_=i_))(dbg_d[0:P, dbg_off[0]:dbg_off[0] + n], scr[0:P, 0:n]), ["dbgscr"], [], dma="out")
            S.barrier()
            dbg_off[0] += n
        print("DUMP", name, dbg_off[0] - N, N)
    es = ExitStack()
    pf2 = [es.enter_context(nc.psum_tensor(f"pf{i}", [128, 1024], F32)) for i in range(3)]
    pb = [es.enter_context(nc.psum_tensor(f"pb{i}", [128, 1024], BF16)) for i in range(2)]
    banks = [(pf2[i][:, h * 512:(h + 1) * 512], f"B{2 * i + h}") for i in range(3) for h in range(2)]
    bank_rr = [0]

    def next_bank():
        b = banks[bank_rr[0] % 6]
        bank_rr[0] += 1
        return b

    def fsz(ap):
        n = 1
        for s_ in ap.shape[1:]:
            n *= s_
        return n

    def mm(out, lhsT, rhs, start, stop, reads, writes):
        passes = 4 if lhsT.dtype == F32 else 1
        S.add("pe", lambda e: e.matmul(out, lhsT=lhsT, rhs=rhs, start=start, stop=stop), reads, writes,
              cost=0.07 + passes * fsz(out) / 2400.0)

    def tr(out, in_, ident, reads, writes):
        S.add("pe", lambda e: e.transpose(out=out, in_=in_, identity=ident), reads, writes,
              cost=0.12 * (4 if in_.dtype == F32 else 1))

    def act(out, in_, func, reads, writes, bias=None, scale=None, accum_out=None):
        kw = {}
        if bias is not None:
            kw["bias"] = bias
        if scale is not None:
            kw["scale"] = scale
        if accum_out is not None:
            kw["accum_out"] = accum_out
        S.add("act", lambda e: e.activation(out=out, in_=in_, func=func, **kw), reads, writes, cost=0.25 + fsz(out) / 1200.0)

    def ecost(eng, out):
        return (0.08 + fsz(out) / 960.0) if eng == "dve" else (0.15 + fsz(out) / 480.0)

    def tt(eng, out, in0, in1, op, reads, writes):
        S.add(eng, lambda e: e.tensor_tensor(out=out, in0=in0, in1=in1, op=op), reads, writes, cost=ecost(eng, out))

    def ts(eng, out, in0, s1, s2, op0, op1, reads, writes):
        if s2 is None:
            S.add(eng, lambda e: e.tensor_scalar(out=out, in0=in0, scalar1=s1, scalar2=None, op0=op0), reads, writes, cost=ecost(eng, out))
        else:
            S.add(eng, lambda e: e.tensor_scalar(out=out, in0=in0, scalar1=s1, scalar2=s2, op0=op0, op1=op1), reads, writes, cost=ecost(eng, out))

    def stt(eng, out, in0, scalar, in1, op0, op1, reads, writes):
        S.add(eng, lambda e: e.scalar_tensor_tensor(out=out, in0=in0, scalar=scalar, in1=in1, op0=op0, op1=op1), reads, writes, cost=ecost(eng, out))

    def cp(eng, out, in_, reads, writes):
        if eng == "act":
            S.add("act", lambda e: e.copy(out=out, in_=in_), reads, writes, cost=0.25 + fsz(out) / 1200.0)
        else:
            S.add(eng, lambda e: e.tensor_copy(out=out, in_=in_), reads, writes, cost=ecost(eng, out))

    def dma(eng, out, in_, reads, writes, sem):
        nbytes = out.shape[0] * fsz(out) * (4 if out.dtype == F32 else 2)
        S.add(eng, lambda e: e.dma_start(out=out, in_=in_), reads, writes, dma=sem, cost=2.0 + nbytes / 150e3)

    def tred(out, in_, reads, writes):
        S.add("dve", lambda e: e.tensor_reduce(out=out, in_=in_, axis=AX.X, op=ALU.add), reads, writes, cost=ecost("dve", in_))

    def memset(eng, ap, val, writes):
        S.add(eng, lambda e: e.memset(ap, val), (), writes, cost=ecost(eng, ap))

    def bc_last(ap, n):
        return ap.unsqueeze(2).to_broadcast([ap.shape[0], ap.shape[1], n])

    def bc_mid(ap, n):
        return ap.unsqueeze(1).to_broadcast([ap.shape[0], n, ap.shape[1]])

    def body():
        cm = A.alloc("cm", [128, 1408], F32)
        Uincl, Lincl, Ustr, Lstr, ones_f, ident_f, P1, P2, iota1, iota2, Pm_f = [cm[:, i * 128:(i + 1) * 128] for i in range(11)]
        ident_b = A.alloc("identb", [128, 128], BF16)
        Pm_b = A.alloc("Pmb", [128, 128], BF16)
        cnt = A.alloc("cnt", [128, 16], F32)
        sel = A.alloc("sel", [2, 256], F32)
        vecs = A.alloc("vecs", [128, 96], F32)
        convw = A.alloc("convw", [128, 16, 3], F32)
        convb = A.alloc("convb", [128, 16], F32)
        G_fm = A.alloc("Gfm_", [128, 8, 2], F32)
        A_fm = A.alloc("Afm", [128, 8, 2], F32)
        sh_fm = A.alloc("shfm", [128, 8, 2], F32)
        negA = A.alloc("negA", [128, 32], F32)
        lamb = A.alloc("lamb", [128, 16], F32)
        lamq = A.alloc("lamq", [128, 8], F32)
        decq = A.alloc("decq", [128, 8], F32)
        tails = A.alloc("tails", [128, 16], F32)
        Eret = A.alloc("Eret", [128, 8, 128], F32)
        DRm = [[A.alloc(f"DRm{d}{hh}", [128, 4, 128], F32) for hh in range(2)] for d in range(2)]
        st = A.alloc("st", [128, 64], F32)
        Dcol = A.alloc("Dcol", [128, 8], F32)
        Ddiag = A.alloc("Ddiag", [128, 8, 128], BF16)
        hT = A.alloc("hT", [128, 8, 1024], BF16)
        mixT = A.alloc("mixT", [128, 16, 1024], BF16)
        mtmp = A.mark()
        DRf = A.alloc("DRf", [128, 4, 128], F32)
        DRb = A.alloc("DRb", [128, 4, 128], F32)

        dma("sp", cm[:], cm_d, (), ["cm"], "cst")
        dma("sp", cnt[:], cnt_d, (), ["cnt"], "cst")
        dma("sp", sel[:], sel_d, (), ["sel"], "cst")
        dma("sp", vecs[:], vecs_d.partition_broadcast(128), (), ["vecs"], "cst")
        dma("sp", convw[:].rearrange("p a b -> p (a b)"), convw_d, (), ["convw"], "cst")
        dma("sp", convb[:], convb_d, (), ["convb"], "cst")
        S.barrier()
        cp("dve", ident_b[:], ident_f, [], ["identb"])
        cp("dve", Pm_b[:], Pm_f, [], ["Pmb"])
        act(negA[:], vecs[:, 0:32], AF.Exp, [], ["negA"])
        ts("dve", negA[:], negA[:], -1.0, None, ALU.mult, None, ["negA"], ["negA"])
        act(lamb[:], vecs[:, 80:96], AF.Exp, [], ["lamb"])
        ts("dve", lamb[:], lamb[:], -1.0, None, ALU.mult, None, ["lamb"], ["lamb"])
        for d in range(2):
            for t in range(4):
                for hh in range(2):
                    cp("dve", lamq[64 * hh:64 * hh + 64, d * 4 + t:d * 4 + t + 1],
                       lamb[64 * hh:64 * hh + 64, d * 8 + 2 * t + hh:d * 8 + 2 * t + hh + 1], ["lamb"], ["lamq"])
        act(decq[:], lamq[:], AF.Exp, ["lamq"], ["decq"], scale=128.0)
        tt("dve", tails[:], lamb[:], cnt[:], ALU.mult, ["lamb"], ["tails"])
        act(tails[:], tails[:], AF.Exp, ["tails"], ["tails"])
        for h in range(8):
            ts("dve", Eret[:, h, :], P1, lamb[:, h:h + 1], None, ALU.mult, None, ["lamb"], ["Eret"])
            stt("dve", Eret[:, h, :], P2, lamb[:, 8 + h:9 + h], Eret[:, h, :], ALU.mult, ALU.add, ["lamb", "Eret"], ["Eret"])
        act(Eret[:], Eret[:], AF.Exp, ["Eret"], ["Eret"])
        tt("dve", Eret[:], Eret[:], bc_mid(ident_f, 8), ALU.add, ["Eret"], ["Eret"])
        for t in range(4):
            act(DRf[:, t, :], iota1, AF.Exp, ["lamq"], ["DRf"], scale=lamq[:, t:t + 1])
            act(DRb[:, t, :], iota2, AF.Exp, ["lamq"], ["DRb"], scale=lamq[:, 4 + t:5 + t])
        for t in range(8):
            for hh in range(2):
                cp("dve", Dcol[64 * hh:64 * hh + 64, t:t + 1], vecs[64 * hh:64 * hh + 64, 64 + 2 * t + hh:65 + 2 * t + hh], [], ["Dcol"])
        for t in range(8):
            ts("dve", Ddiag[:, t, :], ident_f, Dcol[:, t:t + 1], None, ALU.mult, None, ["Dcol"], ["Ddiag"])
        for d, src, sn_ in [(0, DRf, "DRf"), (1, DRb, "DRb")]:
            for hh in range(2):
                memset("pool", DRm[d][hh][:], 0.0, [f"DRm{d}{hh}"])
                cp("dve", DRm[d][hh][64 * hh:64 * hh + 64, :, :], src[64 * hh:64 * hh + 64, :, :], [sn_, f"DRm{d}{hh}"], [f"DRm{d}{hh}"])

        xbuf = [A.alloc(f"xbuf{i}", [128, 1024], F32) for i in range(3)]
        xnb = [A.alloc(f"xnb{i}", [128, 1024], BF16) for i in range(8)]
        junk = A.alloc("junk", [128, 1024], BF16)
        m0 = A.mark()
        wflat = [A.alloc(f"wbuf{i}", [128, 8448], BF16) for i in range(2)]
        condfm = A.alloc("condfm", [128, 8, 2], F32)
        csil = A.alloc("csil", [128, 8, 2], BF16)
        modrow = A.alloc("modrow", [2, 3072], F32)
        bmod2 = A.alloc("bmod2", [2, 3072], F32)
        np2 = A.alloc("np2", [2, 2048], F32)
        dma("sp", condfm[:].rearrange("p a b -> p (a b)"), condfm_d, (), ["condfm"], "cst")
        dma("sp", bmod2[:], bmod_d.partition_broadcast(2), (), ["bmod2"], "cst")
        dma("sp", np2[:, 0:1024], npre_d.partition_broadcast(2), (), ["np2a"], "cst")
        dma("sp", np2[:, 1024:2048], npost_d.partition_broadcast(2), (), ["np2b"], "cst")
        act(csil[:], condfm[:], AF.Silu, ["condfm"], ["csil"])
        wmod_v = wmod_d.rearrange("(k p) c -> p k c", p=128)
        for s in range(3):
            wv = wflat[s % 2][:, 0:8192].rearrange("p (k c) -> p k c", k=8)
            dma("pool", wv, wmod_v[:, :, s * 1024:(s + 1) * 1024], (), [f"wb{s % 2}"], f"wb{s % 2}")
            for half in range(2):
                bk, bn = next_bank()
                for k in range(8):
                    mm(bk[0:2, :], csil[:, k, :], wv[:, k, half * 512:(half + 1) * 512], k == 0, k == 7,
                       ["csil", f"wb{s % 2}"], [bn])
                c0 = s * 1024 + half * 512
                tt("dve", modrow[:, c0:c0 + 512], bk[0:2, :], bmod2[:, c0:c0 + 512], ALU.add, [bn, "bmod2"], [f"mod{s}"])
        stt("dve", modrow[:, 1024:2048], modrow[:, 1024:2048], 1.0, np2[:, 0:1024], ALU.add, ALU.mult, ["mod1", "np2a"], ["mod1"])
        tt("dve", modrow[:, 2048:3072], modrow[:, 2048:3072], np2[:, 1024:2048], ALU.mult, ["mod2", "np2b"], ["mod2"])
        bk, bn = next_bank()
        for k in range(8):
            tr(bk[:, 2 * k:2 * k + 2], modrow[0:2, 1024 + k * 128:1024 + (k + 1) * 128], ident_f[0:2, 0:2], ["mod1"], [bn])
            tr(bk[:, 16 + 2 * k:16 + 2 * k + 2], modrow[0:2, k * 128:(k + 1) * 128], ident_f[0:2, 0:2], ["mod0"], [bn])
        cp("dve", A_fm[:].rearrange("p a b -> p (a b)"), bk[:, 0:16], [bn], ["Afm"])
        cp("dve", sh_fm[:].rearrange("p a b -> p (a b)"), bk[:, 16:32], [bn], ["shfm"])
        bk, bn = next_bank()
        for kk in range(8):
            tr(bk[:, 2 * kk:2 * kk + 2], modrow[0:2, 2048 + kk * 128:2048 + (kk + 1) * 128], ident_f[0:2, 0:2], ["mod2"], [bn])
        cp("dve", G_fm[:].rearrange("p a b -> p (a b)"), bk[:, 0:16], [bn], ["Gfm_"])
        A.release(mtmp)
        chk("stage0")

        units = [
            dict(tok0=0, T=1024, nseq=1, L=1024, r=0, rope=True, init=True, sout=False),
            dict(tok0=1024, T=512, nseq=2, L=256, r=1, rope=False, init=False, sout=True),
        ]
        win_v = win_d.rearrange("(k p) c -> p k c", p=128)
        wout_v = wout_d.rearrange("(k p) c -> p k c", p=128)
        PRE_A = mixT[:, 0:8, :].rearrange("p a b -> p (a b)")
        PRE_B = mixT[:, 8:16, :].rearrange("p a b -> p (a b)")
        PRE_H = hT[:].rearrange("p a b -> p (a b)")
        pre = {}

        def prefetch(key, flat, pieces, ncols, kdim=8, src=None, after=()):
            src = win_v if src is None else src
            wv = flat[:, 0:kdim * ncols].rearrange("p (k c) -> p k c", k=kdim)
            for (c0, n, d0) in pieces:
                dma("pool", wv[:, :, d0:d0 + n], src[:, :, c0:c0 + n], list(after), [f"pre_{key}"], "pre")
            pre[key] = wv

        prefetch("Rq0", PRE_A, [(C_Q, 512, 0), (C_K, 512, 512)], 1024, after=["wb0", "wb1"])
        prefetch("Rv0", PRE_B, [(C_V, 1024, 0)], 1024, after=["pre_Rq0"])
        if dbg:
            dbg_d = {}

        for ui, U in enumerate(units):
            tok0, T, nseq, L, r = U["tok0"], U["T"], U["nseq"], U["L"], U["r"]
            nch = T // 128
            nchs = L // 128
            ntg = T // 512
            hTr = lambda c: f"hT{c}"

            if ui > 0:
                m1 = A.mark()
                xbuf = [A.alloc(f"xbuf{i}", [128, 1024], F32) for i in range(3)]
                xnb = [A.alloc(f"xnb{i}", [128, 1024], BF16) for i in range(2)]
                junk = A.alloc("junk", [128, 1024], BF16)
            else:
                m1 = mtmp
            for c in range(nch):
                xb = xbuf[c % 3]
                xr = f"xb{c % 3}"
                xn_, xnr = xnb[c % len(xnb)], f"xnb{c % len(xnb)}"
                dma("sp", xb[:], x_d[tok0 + c * 128:tok0 + (c + 1) * 128, :], (), [xr], xr)
                sc = st[:, 4 * (c % 2):4 * (c % 2) + 4]
                sr = f"st{c % 2}"
                act(junk[:], xb[:], AF.Square, [xr], ["junk", sr], accum_out=sc[:, 0:1])
                act(sc[:, 1:2], sc[:, 0:1], AF.Ln, [sr], [sr], scale=1.0 / 1024, bias=EPS)
                act(sc[:, 2:3], sc[:, 1:2], AF.Exp, [sr], [sr], scale=-0.5)
                ts("dve", xn_[:], xb[:], sc[:, 2:3], None, ALU.mult, None, [xr, sr], [xnr])
                for k_ in range(8):
                    half = k_ // 4
                    tr(pb[half][:, (k_ % 4) * 128:(k_ % 4 + 1) * 128], xn_[:, k_ * 128:(k_ + 1) * 128], ident_b[:], [xnr, "identb"], [f"B{6 + half}"])
                for k_ in range(8):
                    half = k_ // 4
                    o = hT[:, k_, c * 128:(c + 1) * 128]
                    i_ = pb[half][:, (k_ % 4) * 128:(k_ % 4 + 1) * 128]
                    if half == 0:
                        act(o, i_, AF.Identity, ["B6", "Afm", "shfm"], [hTr(c)],
                            bias=sh_fm[:, k_, r:r + 1], scale=A_fm[:, k_, r:r + 1])
                    else:
                        ts("dve", o, i_, A_fm[:, k_, r:r + 1], sh_fm[:, k_, r:r + 1], ALU.mult, ALU.add,
                           ["B7", "Afm", "shfm"], [hTr(c)])
            S.barrier()
            A.release(m1)
            dumps[f"hT{ui}"] = lambda: dump(hT[:].rearrange("p a b -> p (a b)"), "hT")
            chk(f"hT{ui}")

            def load_slot(slot, pieces, ncols, key=None):
                if key is not None and key in pre:
                    return pre.pop(key), []
                wv = wflat[slot][:, 0:8 * ncols].rearrange("p (k c) -> p k c", k=8)
                names = []
                for pi, (c0, n, d0) in enumerate(pieces):
                    nm = f"wb{slot}p{pi}"
                    dma("pool", wv[:, :, d0:d0 + n], win_v[:, :, c0:c0 + n], (), [nm], f"wb{slot}")
                    names.append(nm)
                return wv, names

            def fm_tile(wv, wnames, ct, tg):
                bk, bn = next_bank()
                hr = [hTr(c) for c in range(tg * 4, tg * 4 + 4)]
                for k in range(8):
                    mm(bk, wv[:, k, ct * 128:(ct + 1) * 128], hT[:, k, tg * 512:(tg + 1) * 512], k == 0, k == 7,
                       wnames + hr, [bn])
                return bk, bn

            def tm_tile(wv, wnames, c, c0, n):
                bk, bn = next_bank()
                for k in range(8):
                    mm(bk[:, 0:n], hT[:, k, c * 128:(c + 1) * 128], wv[:, k, c0:c0 + n], k == 0, k == 7,
                       wnames + [hTr(c)], [bn])
                return bk, bn

            m2 = A.mark()
            qT = A.alloc("qT", [128, 4, T], BF16)
            kT = A.alloc("kT", [128, 4, T], BF16)
            kTm = [A.alloc(f"kTm{hh}", [128, 4, T], BF16) for hh in range(2)]
            for hh in range(2):
                memset("pool", kTm[hh][:], 0.0, [f"kTm{hh}"])
            v_tok = A.alloc("vtok", [128, nch, 1024], BF16)
            gs = A.alloc("gs", [128, nch, 1024], BF16)
            rnw_b = A.alloc("rnwb", [128, 1024], F32)
            dma("sp", rnw_b[:], rnw_d.partition_broadcast(128), (), ["rnwb"], "cst")
            Srf = A.alloc("Srf", [128, 4, 128], F32)
            Srb = A.alloc("Srb", [128, 4, 128], F32)
            Srb_all = A.alloc("Srball", [128, nch, 512], BF16)
            ktl2 = [A.alloc(f"ktl{i}", [128, 512], BF16) for i in range(2)]
            rstep = [0]

            def ret_state_io(S_t, dram3, load, sname):
                dv = dram3.rearrange("(t hh) n v -> hh n t v", hh=2)
                for hh in range(2):
                    if load:
                        dma("sp", S_t[64 * hh:64 * hh + 64, :, :], dv[hh], (), [sname], "sio")
                    else:
                        dma("sp", dv[hh], S_t[64 * hh:64 * hh + 64, :, :], [sname], (), "out")

            def ret_update(S_t, sname, cu, tail_lo, dq_lo):
                tok = slice(cu * 128, (cu + 1) * 128)
                kp = rstep[0] % 2
                rstep[0] += 1
                ktl, ktn = ktl2[kp], f"ktl{kp}"
                for t in range(4):
                    tr(pb[0][:, t * 128:(t + 1) * 128], kT[:, t, tok], ident_b[:], ["kT", "identb"], ["B6"])
                tt("dve", ktl[:].rearrange("p (h n) -> p h n", h=8), pb[0][:, 0:512].rearrange("p (h n) -> p h n", h=8),
                   bc_last(tails[:, tail_lo:tail_lo + 8], 64), ALU.mult, ["B6", "tails"], [ktn])
                for t in range(4):
                    mm(pf2[2][:, t * 256:(t + 1) * 256], ktl[:, t * 128:(t + 1) * 128], v_tok[:, cu, t * 256:(t + 1) * 256],
                       True, True, [ktn, f"vtok{cu}"], ["B4", "B5"])
                tt("dve", S_t[:], S_t[:], bc_last(decq[:, dq_lo:dq_lo + 4], 128), ALU.mult, [sname, "decq"], [sname])
                pu = pf2[2][:].rearrange("p (t c) -> p t c", t=4)
                tt("dve", S_t[0:64, :, :], S_t[0:64, :, :], pu[0:64, :, 0:128], ALU.add, [sname, "B4", "B5"], [sname])
                tt("dve", S_t[64:128, :, :], S_t[64:128, :, :], pu[64:128, :, 128:256], ALU.add, [sname, "B4", "B5"], [sname])

            def ret_bwd_sweep():
                for s in range(nseq):
                    if U["init"]:
                        ret_state_io(Srb, s0r_d[1], True, "Srb")
                    else:
                        memset("dve", Srb[:], 0.0, ["Srb"])
                    for c in reversed(range(nchs)):
                        cu = s * nchs + c
                        cp("act", Srb_all[:, cu, :], Srb[:].rearrange("p a b -> p (a b)"), ["Srb"], [f"Srball{cu}"])
                        if c > 0 or U["sout"]:
                            ret_update(Srb, "Srb", cu, 8, 4)
                    if U["sout"]:
                        ret_state_io(Srb, nsr_d[s, 1], False, "Srb")
            m3 = A.mark()
            wflat = [A.alloc(f"wbufr{i}", [128, 8448], BF16) for i in range(2)]
            if U["rope"]:
                cosT = A.alloc("cosT", [128, 1024], F32)
                sinT = A.alloc("sinT", [128, 1024], F32)
                rt1_ = [A.alloc(f"rt1{i}", [128, 512], F32) for i in range(2)]
                rt2_ = [A.alloc(f"rt2{i}", [128, 512], F32) for i in range(2)]
                qbf = [A.alloc(f"qbf{i}", [128, 512], BF16) for i in range(2)]
                dma("sp", cosT[:], cos_d, (), ["cosT"], "cst")
                dma("sp", sinT[:], sin_d, (), ["sinT"], "cst")
                wv, wn = load_slot(0, [(C_Q, 512, 0), (C_K, 512, 512)], 1024, key=f"Rq{ui}")
                it_ = 0
                for ti in range(8):
                    dst, dname, kscale = (qT, "qT", None) if ti < 4 else (kT, "kT", 0.125)
                    tl = ti % 4
                    for tg in range(ntg):
                        par = it_ % 2
                        it_ += 1
                        rt1, rt2 = rt1_[par], rt2_[par]
                        r1n, r2n, qbn = f"rt1{par}", f"rt2{par}", f"qbf{par}"
                        ba, bna = fm_tile(wv, wn, ti, tg)
                        cp("act", qbf[par][:], ba, [bna], [qbn])
                        bb, bnb = next_bank()
                        mm(bb, Pm_b[:], qbf[par][:], True, True, [qbn, "Pmb"], [bnb])
                        cs = cosT[:, tg * 512:(tg + 1) * 512]
                        sn = sinT[:, tg * 512:(tg + 1) * 512]
                        if kscale is None:
                            tt("dve", rt1[:], ba, cs, ALU.mult, [bna, "cosT"], [r1n])
                            tt("dve", rt2[:], bb, sn, ALU.mult, [bnb, "sinT"], [r2n])
                        else:
                            stt("dve", rt1[:], ba, kscale, cs, ALU.mult, ALU.mult, [bna, "cosT"], [r1n])
                            stt("dve", rt2[:], bb, kscale, sn, ALU.mult, ALU.mult, [bnb, "sinT"], [r2n])
                        tt("pool", dst[:, tl, tg * 512:(tg + 1) * 512], rt1[:], rt2[:], ALU.add, [r1n, r2n], [dname])
                        if dname == "kT":
                            for hh in range(2):
                                cp("act", kTm[hh][64 * hh:64 * hh + 64, tl, tg * 512:(tg + 1) * 512],
                                   kT[64 * hh:64 * hh + 64, tl, tg * 512:(tg + 1) * 512], ["kT", f"kTm{hh}"], [f"kTm{hh}"])
                nslot = 1
            else:
                wv, wn = load_slot(0, [(C_Q, 512, 0), (C_K, 512, 512)], 1024, key=f"Rqk{ui}")
                for ti in range(8):
                    for tg in range(ntg):
                        bk, bn = fm_tile(wv, wn, ti, tg)
                        if ti < 4:
                            cp("act", qT[:, ti, tg * 512:(tg + 1) * 512], bk, [bn], ["qT"])
                        else:
                            S.add("act", (lambda o, i_: lambda e: e.mul(out=o, in_=i_, mul=0.125))(kT[:, ti - 4, tg * 512:(tg + 1) * 512], bk), [bn], ["kT"])
                            for hh in range(2):
                                cp("dve", kTm[hh][64 * hh:64 * hh + 64, ti - 4, tg * 512:(tg + 1) * 512],
                                   kT[64 * hh:64 * hh + 64, ti - 4, tg * 512:(tg + 1) * 512], ["kT", f"kTm{hh}"], [f"kTm{hh}"])
                nslot = 1
            for (c0, dst, dname, fn) in [(C_V, v_tok, "vtok", None), (C_G, gs, "gs", AF.Silu)]:
                wv, wn = load_slot(nslot % 2, [(c0, 1024, 0)], 1024, key=(f"Rv{ui}" if fn is None else None))
                nslot += 1
                for c in range(nch):
                    for cg in range(2):
                        bk, bn = tm_tile(wv, wn, c, cg * 512, 512)
                        o = dst[:, c, cg * 512:(cg + 1) * 512]
                        if fn is None:
                            cp("act", o, bk, [bn], [f"{dname}{c}"])
                        else:
                            act(o, bk, fn, [bn], [f"{dname}{c}"])
                    if fn is not None:
                        tt("pool", dst[:, c, :], dst[:, c, :], rnw_b[:], ALU.mult, [f"{dname}{c}", "rnwb"], [f"{dname}{c}"])
                if fn is None:
                    ret_bwd_sweep()
            S.barrier()
            A.release(m3)
            dumps[f"Rproj{ui}"] = lambda: (dump(qT[:].rearrange("p a b -> p (a b)"), "qT"), dump(kT[:].rearrange("p a b -> p (a b)"), "kT"),
                                          dump(v_tok[:].rearrange("p a b -> p (a b)"), "v"), dump(gs[:].rearrange("p a b -> p (a b)"), "gs"))
            chk(f"Rproj{ui}")

            prefetch(f"Sxs{ui}", PRE_A, [(C_XS, 1024, 0)], 1024)
            if ui == 1:
                prefetch(f"Sbc{ui}", wpreX[:], [(C_B, 1056, 0)], 1056)
                prefetch(f"Sz{ui}", wpre1[:], [(C_Z, 1024, 0)], 1024)
            Srf_bf = A.alloc("Srfbf", [128, 4, 128], BF16)
            Sm2 = [A.alloc(f"Sm{i}", [128, 1024], BF16) for i in range(2)]
            qfm2 = [[[A.alloc(f"qfm{i}{d}{hh}", [128, 4, 128], BF16) for hh in range(2)] for d in range(2)] for i in range(2)]
            yr2 = [A.alloc(f"yr{i}", [128, 8, 128], F32) for i in range(2)]
            sq2 = [A.alloc(f"sq{i}", [128, 8, 128], F32) for i in range(2)]
            mixr2 = [A.alloc(f"mixr{i}", [128, 1024], BF16) for i in range(2)]
            gst2 = [A.alloc(f"gst{i}", [128, 48], F32) for i in range(2)]

            for s in range(nseq):
                if U["init"]:
                    ret_state_io(Srf, s0r_d[0], True, "Srf")
                else:
                    memset("dve", Srf[:], 0.0, ["Srf"])
                for c in range(nchs):
                    cu = s * nchs + c
                    tok = slice(cu * 128, (cu + 1) * 128)
                    cpar = cu % 2
                    Sm, qfm, yr, sq, mixr, gst = Sm2[cpar], qfm2[cpar], yr2[cpar], sq2[cpar], mixr2[cpar], gst2[cpar]
                    P_ = f"c{cpar}"
                    for h in range(8):
                        t, hh = h // 2, h % 2
                        ps_ = slice(64 * hh, 64 * hh + 64)
                        mm(pf2[0][:, h * 128:(h + 1) * 128], kTm[hh][:, t, tok], qT[:, t, tok], True, True,
                           [f"kTm{hh}", "qT"], [f"B{h // 4}"])
                    for half in range(2):
                        cs_ = slice(half * 512, (half + 1) * 512)
                        tt("dve", Sm[:, cs_], pf2[0][:, cs_], Eret[:].rearrange("p h i -> p (h i)")[:, cs_], ALU.mult,
                           [f"B{half}", "Eret"], [P_ + f"Sm{half}"])
                    chk("Rs_sc")
                    for d in range(2):
                        for hh in range(2):
                            tt("pool", qfm[d][hh][:], qT[:, :, tok], DRm[d][hh][:], ALU.mult, ["qT"], [P_ + f"qfm{d}{hh}"])
                    cp("act", Srf_bf[:], Srf[:], ["Srf"], ["Srfbf"])
                    for h in range(8):
                        t, hh = h // 2, h % 2
                        ps_ = slice(64 * hh, 64 * hh + 64)
                        o = pf2[1][:, h * 128:(h + 1) * 128]
                        wn_ = [f"B{2 + h // 4}"]
                        mm(o, Sm[:, h * 128:(h + 1) * 128], v_tok[:, cu, h * 128:(h + 1) * 128], True, False,
                           [P_ + f"Sm{h // 4}", f"vtok{cu}"], wn_)
                        mm(o, qfm[0][hh][:, t, :], Srf_bf[:, t, :], False, False, [P_ + f"qfm0{hh}", "Srfbf"], wn_)
                        mm(o, qfm[1][hh][:, t, :], Srb_all[:, cu, t * 128:(t + 1) * 128], False, True, [P_ + f"qfm1{hh}", f"Srball{cu}"], wn_)
                    chk("Rs_y")
                    yrf = yr[:].rearrange("p h v -> p (h v)")
                    sqf = sq[:].rearrange("p h v -> p (h v)")
                    for h in range(8):
                        half = h // 4
                        src_ = pf2[1][:, h * 128:(h + 1) * 128]
                        act(yr[:, h, :], src_, AF.Identity, [f"B{2 + half}"], [P_ + f"yr{half}", P_ + "gst0"], accum_out=gst[:, h:h + 1])
                        act(sq[:, h, :], src_, AF.Square, [f"B{2 + half}"], [P_ + "sq", P_ + "gst1"], accum_out=gst[:, 8 + h:9 + h])
                    ts("dve", gst[:, 16:24], gst[:, 0:8], 1.0 / 128, None, ALU.mult, None, [P_ + "gst0"], [P_ + "gst2"])
                    tt("dve", gst[:, 24:32], gst[:, 16:24], gst[:, 16:24], ALU.mult, [P_ + "gst2"], [P_ + "gst3"])
                    stt("dve", gst[:, 32:40], gst[:, 8:16], 1.0 / 128, gst[:, 24:32], ALU.mult, ALU.subtract, [P_ + "gst1", P_ + "gst3"], [P_ + "gst4"])
                    act(gst[:, 40:48], gst[:, 32:40], AF.Ln, [P_ + "gst4"], [P_ + "gst5"], bias=EPS)
                    act(gst[:, 40:48], gst[:, 40:48], AF.Exp, [P_ + "gst5"], [P_ + "gst5"], scale=-0.5)
                    tt("dve", yr[:], yr[:], bc_last(gst[:, 16:24], 128), ALU.subtract, [P_ + "yr0", P_ + "yr1", P_ + "gst2"], [P_ + "yr0", P_ + "yr1"])
                    tt("pool", yr[:], yr[:], bc_last(gst[:, 40:48], 128), ALU.mult, [P_ + "yr0", P_ + "yr1", P_ + "gst5"], [P_ + "yr0", P_ + "yr1"])
                    tt("dve", mixr[:], yrf, gs[:, cu, :], ALU.mult, [P_ + "yr0", P_ + "yr1", f"gs{cu}"], [P_ + "mixr"])
                    chk("Rs_gn")
                    for t in range(8):
                        tr(pb[1][:, t * 128:(t + 1) * 128], mixr[:, t * 128:(t + 1) * 128], ident_b[:], [P_ + "mixr", "identb"], ["B7"])
                    cp("act", mixT[:, 8:16, tok], pb[1][:].rearrange("p (t c) -> p t c", t=8), ["B7"], [f"mixTr{cu}"])
                    if c < nchs - 1 or U["sout"]:
                        ret_update(Srf, "Srf", cu, 0, 0)
                if U["sout"]:
                    ret_state_io(Srf, nsr_d[s, 0], False, "Srf")
            S.barrier()
            A.release(m2)
            dumps[f"Rscan{ui}"] = lambda: dump(mixT[:, 8:16, :].rearrange("p a b -> p (a b)"), "mixTr")
            chk(f"Rscan{ui}")

            m4 = A.mark()
            xbcT = A.alloc("xbcT", [128, 16, T], BF16)
            zs = A.alloc("zs", [128, nch, 1024], BF16)
            dtraw = A.alloc("dtraw", [128, nch, 32], F32)
            dtv = A.alloc("dtv", [128, nch, 32], F32)
            la = A.alloc("la", [128, nch, 32], F32)
            decs = A.alloc("decs", [128, nch, 96], F32)
            wts = A.alloc("wts", [128, nch, 32], F32)
            snw_b = A.alloc("snwb", [128, 1024], F32)
            dma("sp", snw_b[:], snw_d.partition_broadcast(128), (), ["snwb"], "cst")
            Sb = A.alloc("Sb", [128, 16, 64], F32)
            Sb_all = A.alloc("Sball", [128, nch, 1024], BF16)
            xwm2 = [A.alloc(f"xwm{i}", [128, 16, 64], BF16) for i in range(2)]
            Btok2 = [A.alloc(f"Btok{i}", [128, 4, 128], BF16) for i in range(2)]
            tmpa = A.alloc("tmpa", [128, nch, 32], F32)
            tmpb = A.alloc("tmpb", [128, nch, 32], F32)
            step = [0]
            SfN = ["Sf0", "Sf1", "Sf2", "Sf3"]

            def ssd_state_io(S_t, dram3, load, snames):
                dv = dram3.rearrange("h n p -> n h p")
                if load:
                    dma("sp", S_t[:], dv, (), snames, "sio")
                else:
                    dma("sp", dv, S_t[:], snames, (), "out")

            def xs_transposes(cu):
                tok = slice(cu * 128, (cu + 1) * 128)
                for t in range(8):
                    tr(pb[0][:, t * 128:(t + 1) * 128], xbcT[:, t, tok], ident_b[:], [f"xbcT{t}", "identb"], ["B6"])
                return pb[0][:].rearrange("p (h d) -> p h d", h=16)

            def b_transposes(cu, Btok, bname):
                tok = slice(cu * 128, (cu + 1) * 128)
                for g in range(4):
                    tr(pb[1][:, g * 128:(g + 1) * 128], xbcT[:, 8 + g, tok], ident_b[:], [f"xbcT{8 + g}", "identb"], ["B7"])
                cp("act", Btok[:].rearrange("p g n -> p (g n)"), pb[1][:, 0:512], ["B7"], [bname])

            def ssd_prep():
                ts("dve", tmpa[:], dtraw[:], -1.0, None, ALU.mult, None, ["dtraw"], ["tmpa"])
                tt("dve", tmpa[:], tmpa[:], dtraw[:], ALU.min, ["tmpa", "dtraw"], ["tmpa"])
                act(tmpa[:], tmpa[:], AF.Exp, ["tmpa"], ["tmpa"])
                act(tmpa[:], tmpa[:], AF.Ln, ["tmpa"], ["tmpa"], bias=1.0)
                ts("dve", tmpb[:], dtraw[:], 0.0, None, ALU.max, None, ["dtraw"], ["tmpb"])
                tt("dve", dtv[:], tmpa[:], tmpb[:], ALU.add, ["tmpa", "tmpb"], ["dtv"])
                tt("dve", la[:], dtv[:], bc_mid(negA[:], nch), ALU.mult, ["dtv", "negA"], ["la"])
                for c in range(nch):
                    o = pf2[0][:, c * 128:c * 128 + 96]
                    wn_ = [f"B{c // 4}"]
                    mm(o[:, 0:16], Uincl, la[:, c, 0:16], True, True, ["la"], wn_)
                    mm(o[:, 16:32], Lincl, la[:, c, 16:32], True, True, ["la"], wn_)
                    mm(o[:, 32:48], Lstr, la[:, c, 0:16], True, True, ["la"], wn_)
                    mm(o[:, 48:64], Ustr, la[:, c, 16:32], True, True, ["la"], wn_)
                    mm(o[:, 64:96], ones_f, la[:, c, 0:32], True, True, ["la"], wn_)
                for hb in range((nch + 3) // 4):
                    c_lo, c_hi = hb * 4, min(nch, hb * 4 + 4)
                    act(decs[:, c_lo:c_hi, :], pf2[0][:, c_lo * 128:c_hi * 128].rearrange("p (c x) -> p c x", x=128)[:, :, 0:96],
                        AF.Exp, [f"B{hb}"], ["decs"])
                tt("dve", wts[:], decs[:, :, 32:64], dtv[:], ALU.mult, ["decs", "dtv"], ["wts"])

            def ssd_bwd_sweep():
                for s in range(nseq):
                    if U["init"]:
                        ssd_state_io(Sb, s0s_d[1], True, ["Sb"])
                    else:
                        memset("dve", Sb[:], 0.0, ["Sb"])
                    for c in reversed(range(nchs)):
                        cu = s * nchs + c
                        cp("act", Sb_all[:, cu, :], Sb[:].rearrange("p h d -> p (h d)"), ["Sb"], [f"Sball{cu}"])
                        if c > 0 or U["sout"]:
                            par = step[0] % 2
                            step[0] += 1
                            xwm, xwn = xwm2[par], f"xwm{par}"
                            Btok, btn = Btok2[par], f"Btok{par}"
                            xsv = xs_transposes(cu)
                            tt("dve", xwm[:], xsv, bc_last(wts[:, cu, 16:32], 64), ALU.mult, ["B6", "wts"], [xwn])
                            b_transposes(cu, Btok, btn)
                            for g in range(4):
                                mm(pf2[2][:, g * 256:(g + 1) * 256], Btok[:, g, :], xwm[:, 4 * g:4 * g + 4, :].rearrange("p h d -> p (h d)"),
                                   True, True, [btn, xwn], [["B4"], ["B4"], ["B5"], ["B5"]][g])
                            tt("dve", Sb[:], Sb[:], bc_last(decs[:, cu, 80:96], 64), ALU.mult, ["Sb", "decs"], ["Sb"])
                            for half in range(2):
                                tt("dve", Sb[:, 8 * half:8 * half + 8, :], Sb[:, 8 * half:8 * half + 8, :],
                                   pf2[2][:, half * 512:(half + 1) * 512].rearrange("p (h d) -> p h d", h=8), ALU.add,
                                   ["Sb"] + [["B4"], ["B5"]][half], ["Sb"])
                    if U["sout"]:
                        ssd_state_io(Sb, nss_d[s, 1], False, ["Sb"])
            m5 = A.mark()
            wflat = [A.alloc(f"wbufs{i}", [128, 8448], BF16) for i in range(2)]
            raw = [A.alloc(f"raw{i}", [128, nseq, L + 2], F32) for i in range(2)]
            acc = [A.alloc(f"acc{i}", [128, nseq, L], F32) for i in range(2)]
            for i in range(2):
                memset("pool", raw[i][:], 0.0, [f"raw{i}"])
            nslot = 0
            for (c0, ncols, tile0) in [(C_XS, 1024, 0), (C_B, 1056, 8)]:
                wv, wn = load_slot(nslot % 2, [(c0, ncols, 0)], ncols, key=(f"Sxs{ui}" if tile0 == 0 else f"Sbc{ui}"))
                nslot += 1
                for ti in range(8):
                    gi = tile0 + ti
                    rw, ac = raw[gi % 2], acc[gi % 2]
                    rn, an = f"raw{gi % 2}", f"acc{gi % 2}"
                    for tg in range(ntg):
                        bk, bn = fm_tile(wv, wn, ti, tg)
                        if nseq == 1:
                            cp("act", rw[:, 0, 1 + tg * 512:1 + (tg + 1) * 512], bk, [bn], [rn])
                        else:
                            cp("act", rw[:, :, 1:L + 1], bk.rearrange("p (s l) -> p s l", s=nseq), [bn], [rn])
                    act(ac[:], rw[:, :, 1:L + 1], AF.Identity, [rn, "convw", "convb"], [an],
                        bias=convb[:, gi:gi + 1], scale=convw[:, gi, 1:2])
                    stt("dve", ac[:], rw[:, :, 0:L], convw[:, gi, 0:1], ac[:], ALU.mult, ALU.add, [rn, an], [an])
                    stt("dve", ac[:], rw[:, :, 2:L + 2], convw[:, gi, 2:3], ac[:], ALU.mult, ALU.add, [rn, an], [an])
                    act(xbcT[:, gi, :].rearrange("p (s l) -> p s l", s=nseq), ac[:], AF.Silu, [an], [f"xbcT{gi}"])
                if tile0 == 8:
                    for c in range(nch):
                        bk, bn = tm_tile(wv, wn, c, 1024, 32)
                        tt("dve", dtraw[:, c, :], bk[:, 0:32], vecs[:, 32:64], ALU.add, [bn], ["dtraw"])
            ssd_prep()
            ssd_bwd_sweep()
            wv, wn = load_slot(nslot % 2, [(C_Z, 1024, 0)], 1024, key=f"Sz{ui}")
            for c in range(nch):
                for cg in range(2):
                    bk, bn = tm_tile(wv, wn, c, cg * 512, 512)
                    act(zs[:, c, cg * 512:(cg + 1) * 512], bk, AF.Silu, [bn], [f"zs{c}"])
            S.barrier()
            A.release(m5)
            dumps[f"Sproj{ui}"] = lambda: (dump(xbcT[:].rearrange("p a b -> p (a b)"), "xbcT"), dump(zs[:].rearrange("p a b -> p (a b)"), "zs"),
                                          dump(dtraw[:].rearrange("p a b -> p (a b)"), "dtraw"))
            chk(f"Sproj{ui}")

            prefetch(f"O{ui}", PRE_H, [(0, 512, 0)], 512, kdim=16, src=wout_v)
            if ui == 1:
                prefetch(f"Ob{ui}", wpre1[:], [(512, 512, 0)], 512, kdim=16, src=wout_v)
            Sf = A.alloc("Sf", [128, 16, 64], F32)
            Sf_bf = A.alloc("Sfbf", [128, 1024], BF16)
            xfm2 = [A.alloc(f"xfm{i}", [128, 16, 64], BF16) for i in range(2)]
            xbm2 = [A.alloc(f"xbm{i}", [128, 16, 64], BF16) for i in range(2)]
            Gfm = A.alloc("Gfm", [128, 4, 128], BF16)
            Gbm = A.alloc("Gbm", [128, 4, 128], BF16)
            RFf2 = [A.alloc(f"RFf{i}", [128, 4, 128], F32) for i in range(2)]
            RFb2 = [A.alloc(f"RFb{i}", [128, 4, 128], F32) for i in range(2)]
            Ef2 = [A.alloc(f"Ef{i}", [128, 4, 128], BF16) for i in range(2)]
            Eb2 = [A.alloc(f"Eb{i}", [128, 4, 128], BF16) for i in range(2)]
            SSf2 = [A.alloc(f"SSf{i}", [128, 4, 128], BF16) for i in range(2)]
            SSb2 = [A.alloc(f"SSb{i}", [128, 4, 128], BF16) for i in range(2)]
            t12 = [A.alloc(f"t1{i}", [128, 4, 64], F32) for i in range(2)]
            t22 = [A.alloc(f"t2{i}", [128, 4, 64], F32) for i in range(2)]
            ys = A.alloc("ys", [128, 16, 64], F32)
            mixs = A.alloc("mixs", [128, 1024], BF16)
            junk2 = A.alloc("junk2", [128, 1024], BF16)
            dsk = vecs[:, 64:80]
            bankA = [(pf2[1][:, 0:512], "B2"), (pf2[2][:, 0:512], "B4")]
            bankB = [(pf2[1][:, 512:1024], "B3"), (pf2[2][:, 512:1024], "B5")]

            for s in range(nseq):
                if U["init"]:
                    ssd_state_io(Sf, s0s_d[0], True, SfN)
                else:
                    memset("dve", Sf[:], 0.0, SfN)
                for c in range(nchs):
                    cu = s * nchs + c
                    tok = slice(cu * 128, (cu + 1) * 128)
                    upd = (c < nchs - 1) or U["sout"]
                    par = step[0] % 2
                    step[0] += 1
                    xfm, xbm, xwm = xfm2[par], xbm2[par], xwm2[par]
                    xfn, xbn, xwn = f"xfm{par}", f"xbm{par}", f"xwm{par}"
                    Btok, btn = Btok2[par], f"Btok{par}"
                    xsv = xs_transposes(cu)
                    tt("dve", xfm[:], xsv, bc_last(dtv[:, cu, 0:16], 64), ALU.mult, ["B6", "dtv"], [xfn])
                    tt("dve", xbm[:], xsv, bc_last(dtv[:, cu, 16:32], 64), ALU.mult, ["B6", "dtv"], [xbn])
                    if upd:
                        tt("dve", xwm[:], xsv, bc_last(wts[:, cu, 0:16], 64), ALU.mult, ["B6", "wts"], [xwn])
                        b_transposes(cu, Btok, btn)
                    pG = pf2[0][:, 0:512]
                    for g in range(4):
                        mm(pG[:, g * 128:(g + 1) * 128], xbcT[:, 8 + g, tok], xbcT[:, 12 + g, tok], True, True,
                           [f"xbcT{8 + g}", f"xbcT{12 + g}"], ["B0"])
                    pG3 = pG.rearrange("p (g i) -> p g i", g=4)
                    tt("dve", Gfm[:], pG3, bc_mid(Uincl, 4), ALU.mult, ["B0"], ["Gfm"])
                    tt("dve", Gbm[:], pG3, bc_mid(Lincl, 4), ALU.mult, ["B0"], ["Gbm"])
                    cp("act", Sf_bf[:], Sf[:].rearrange("p h d -> p (h d)"), SfN, ["Sfbf"])
                    for g in range(4):
                        gp = g % 2
                        hs = slice(4 * g, 4 * g + 4)
                        RFf, RFb, Ef, Eb, SSf, SSb, t1, t2 = RFf2[gp], RFb2[gp], Ef2[gp], Eb2[gp], SSf2[gp], SSb2[gp], t12[gp], t22[gp]
                        nRFf, nRFb, nEf, nEb, nSSf, nSSb, nt1, nt2 = [f"{n}{gp}" for n in ("RFf", "RFb", "Ef", "Eb", "SSf", "SSb", "t1", "t2")]
                        tt("pool", RFf[:], bc_mid(Uincl, 4), bc_last(la[:, cu, 4 * g:4 * g + 4], 128), ALU.mult, ["la"], [nRFf])
                        tt("pool", RFb[:], bc_mid(Lincl, 4), bc_last(la[:, cu, 16 + 4 * g:16 + 4 * g + 4], 128), ALU.mult, ["la"], [nRFb])
                        pAf = pf2[0][:, 0:512]
                        pAb = pf2[0][:, 512:1024]
                        mm(pAf, Lstr, RFf[:].rearrange("p h i -> p (h i)"), True, True, [nRFf], ["B0"])
                        mm(pAb, Ustr, RFb[:].rearrange("p h i -> p (h i)"), True, True, [nRFb], ["B1"])
                        act(Ef[:].rearrange("p h i -> p (h i)"), pAf, AF.Exp, ["B0"], [nEf])
                        act(Eb[:].rearrange("p h i -> p (h i)"), pAb, AF.Exp, ["B1"], [nEb])
                        tt("dve", SSf[:], Ef[:], bc_mid(Gfm[:, g, :], 4), ALU.mult, [nEf, "Gfm"], [nSSf])
                        tt("pool", SSb[:], Eb[:], bc_mid(Gbm[:, g, :], 4), ALU.mult, [nEb, "Gbm"], [nSSb])
                        (bA, nA), (bB, nB) = bankA[gp], bankB[gp]
                        pY, pYf, pYb, pU = bA[:, 0:256], bA[:, 256:512], bB[:, 0:256], bB[:, 256:512]
                        for hl in range(4):
                            h = 4 * g + hl
                            o = pY[:, hl * 64:(hl + 1) * 64]
                            mm(o, SSf[:, hl, :], xfm[:, h, :], True, False, [nSSf, xfn], [nA])
                            mm(o, SSb[:, hl, :], xbm[:, h, :], False, False, [nSSb, xbn], [nA])
                            mm(o, xbcT[:, h // 2, tok], Ddiag[:, h // 2, (h % 2) * 64:(h % 2) * 64 + 64], False, True, [f"xbcT{h // 2}", "Ddiag"], [nA])
                        mm(pYf, xbcT[:, 12 + g, tok], Sf_bf[:, g * 256:(g + 1) * 256], True, True, [f"xbcT{12 + g}", "Sfbf"], [nA])
                        mm(pYb, xbcT[:, 12 + g, tok], Sb_all[:, cu, g * 256:(g + 1) * 256], True, True, [f"xbcT{12 + g}", f"Sball{cu}"], [nB])
                        if upd:
                            mm(pU, Btok[:, g, :], xwm[:, hs, :].rearrange("p h d -> p (h d)"), True, True, [btn, xwn], [nB])
                        tt("dve", t1[:], pYf.rearrange("p (h d) -> p h d", h=4), bc_last(decs[:, cu, 4 * g:4 * g + 4], 64), ALU.mult,
                           [nA, "decs"], [nt1])
                        tt("dve", t2[:], pYb.rearrange("p (h d) -> p h d", h=4), bc_last(decs[:, cu, 16 + 4 * g:16 + 4 * g + 4], 64), ALU.mult,
                           [nB, "decs"], [nt2])
                        tt("pool", t1[:], t1[:], t2[:], ALU.add, [nt1, nt2], [nt1])
                        tt("dve", ys[:, hs, :], pY.rearrange("p (h d) -> p h d", h=4), t1[:], ALU.add, [nA, nt1], [f"ys{g}"])
                        if upd:
                            for hl_ in range(4):
                                h_ = 4 * g + hl_
                                act(Sf[:, h_, :], Sf[:, h_, :], AF.Identity, [f"Sf{g}", "decs", "Sfbf"], [f"Sf{g}"],
                                    scale=decs[:, cu, 64 + h_:65 + h_])
                            tt("dve", Sf[:, hs, :], Sf[:, hs, :], pU.rearrange("p (h d) -> p h d", h=4), ALU.add, [f"Sf{g}", nB], [f"Sf{g}"])
                    ysf = ys[:].rearrange("p h d -> p (h d)")
                    ysn = [f"ys{g}" for g in range(4)]
                    tt("dve", ysf, ysf, zs[:, cu, :], ALU.mult, ysn + [f"zs{cu}"], ysn)
                    act(junk2[:], ysf, AF.Square, ysn, ["junk2", "sst"], accum_out=st[:, 16:17])
                    act(st[:, 17:18], st[:, 16:17], AF.Ln, ["sst"], ["sst"], scale=1.0 / 1024, bias=EPS)
                    act(st[:, 18:19], st[:, 17:18], AF.Exp, ["sst"], ["sst"], scale=-0.5)
                    stt("dve", mixs[:], ysf, st[:, 18:19], snw_b[:], ALU.mult, ALU.mult, ysn + ["sst", "snwb"], ["mixs"])
                    for t in range(8):
                        tr(pb[1][:, t * 128:(t + 1) * 128], mixs[:, t * 128:(t + 1) * 128], ident_b[:], ["mixs", "identb"], ["B7"])
                    cp("act", mixT[:, 0:8, tok], pb[1][:].rearrange("p (t c) -> p t c", t=8), ["B7"], [f"mixTs{cu}"])
                if U["sout"]:
                    ssd_state_io(Sf, nss_d[s, 0], False, SfN)
            S.barrier()
            A.release(m4)
            dumps[f"Sscan{ui}"] = lambda: dump(mixT[:, 0:8, :].rearrange("p a b -> p (a b)"), "mixTs")
            chk(f"Sscan{ui}")

            if ui == 0:
                wpre1 = A.alloc("wpre1", [128, 8192], BF16)
                wpreX = A.alloc("wpreX", [128, 8448], BF16)
                prefetch("Rqk1", wpre1[:], [(C_Q, 512, 0), (C_K, 512, 512)], 1024)
                prefetch("Rv1", wpreX[:], [(C_V, 1024, 0)], 1024)
            m6 = A.mark()
            wflat_o = [A.alloc(f"wbufo{i}", [128, 8448], BF16) for i in range(2)]
            xbuf = [A.alloc(f"xbufo{i}", [128, 1024], F32) for i in range(2)]
            yo = [A.alloc(f"yo{i}", [128, 1024], F32) for i in range(2)]
            junk3 = A.alloc("junk3", [128, 1024], BF16)
            Gbu = A.alloc("Gbu", [128, 1024], F32)
            Dg = A.alloc("Dg", [128, 8, 128], F32)
            for kk in range(8):
                ts("dve", Dg[:, kk, :], ident_f, G_fm[:, kk, r:r + 1], None, ALU.mult, None, ["Gfm_"], ["Dg"])
            for half in range(2):
                bk, bn = next_bank()
                mm(bk, ones_f, Dg[:, 4 * half:4 * half + 4, :].rearrange("p a b -> p (a b)"), True, True, ["Dg"], [bn])
                cp("act", Gbu[:, half * 512:(half + 1) * 512], bk, [bn], ["Gbu"])
            wo = []
            for half in range(2):
                if half == 0 and f"O{ui}" in pre:
                    wo.append(pre.pop(f"O{ui}"))
                    continue
                if half == 1 and f"Ob{ui}" in pre:
                    wo.append(pre.pop(f"Ob{ui}"))
                    continue
                wv = wflat_o[half][:, 0:8192].rearrange("p (k c) -> p k c", k=16)
                dma("pool", wv, wout_v[:, :, half * 512:(half + 1) * 512], (), [f"wo{half}"], f"wb{half}")
                wo.append(wv)
            for c in range(nch):
                tok = slice(c * 128, (c + 1) * 128)
                xb, xr = xbuf[c % 2], f"xbo{c % 2}"
                yb, yn = yo[c % 2], f"yo{c % 2}"
                dma("sp", xb[:], x_d[tok0 + c * 128:tok0 + (c + 1) * 128, :], (), [xr], xr)
                pt = pf2[c % 2]
                for half in range(2):
                    for k in range(16):
                        mm(pt[:, half * 512:(half + 1) * 512], mixT[:, k, tok], wo[half][:, k, :], k == 0, k == 15,
                           [f"wo{half}"], [f"B{2 * (c % 2) + half}"])
                sc = st[:, 24 + 4 * (c % 2):28 + 4 * (c % 2)]
                sr = f"sto{c % 2}"
                for half in range(2):
                    act(junk3[:, half * 512:(half + 1) * 512], pt[:, half * 512:(half + 1) * 512], AF.Square,
                        [f"B{2 * (c % 2) + half}"], ["junk3", sr + str(half)], accum_out=sc[:, half:half + 1])
                tt("dve", sc[:, 2:3], sc[:, 0:1], sc[:, 1:2], ALU.add, [sr + "0", sr + "1"], [sr])
                act(sc[:, 3:4], sc[:, 2:3], AF.Ln, [sr], [sr], scale=1.0 / 1024, bias=EPS)
                act(sc[:, 3:4], sc[:, 3:4], AF.Exp, [sr], [sr], scale=-0.5)
                for half in range(2):
                    cs_ = slice(half * 512, (half + 1) * 512)
                    stt("dve", yb[:, cs_], pt[:, cs_], sc[:, 3:4], Gbu[:, cs_], ALU.mult, ALU.mult,
                        [f"B{2 * (c % 2) + half}", sr, "Gbu"], [yn + str(half)])
                tt("pool", yb[:], yb[:], xb[:], ALU.add, [yn + "0", yn + "1", xr], [yn + "0", yn + "1"])
                dma("sp", y_d[tok0 + c * 128:tok0 + (c + 1) * 128, :], yb[:], [yn + "0", yn + "1"], (), "out")
            S.barrier()
            A.release(m6)
            dumps[f"Oproj{ui}"] = lambda: dump(mixT[:].rearrange("p a b -> p (a b)"), "mixT")
            chk(f"Oproj{ui}")


    try:
        body()
    except _Stop:
        pass
    S.barrier()
    plan = S.finalize()
    run_plan(nc, plan, list(S.dma_totals.keys()))
    es.close()
    return nc


def _consts():
    t = np.arange(128)
    T_, I_ = t[:, None], t[None, :]
    mats = [
        (T_ <= I_), (T_ >= I_), (T_ < I_), (T_ > I_), np.ones((128, 128)), np.eye(128),
        np.maximum(I_ - T_, 0), np.maximum(T_ - I_, 0),
        np.broadcast_to(I_ + 1, (128, 128)), np.broadcast_to(128 - I_, (128, 128)),
    ]
    nn = np.arange(64)
    partner = np.where((nn % 32) < 16, nn + 16, nn - 16)
    Pm = np.zeros((128, 128), np.float32)
    for n2 in range(128):
        Pm[(n2 // 64) * 64 + partner[n2 % 64], n2] = 1.0
    mats.append(Pm)
    cm = np.concatenate([np.asarray(m, dtype=np.float32) for m in mats], axis=1)
    cnt = np.zeros((128, 16), np.float32)
    cnt[:, 0:8] = (127 - t)[:, None]
    cnt[:, 8:16] = t[:, None]
    sel = np.zeros((2, 256), np.float32)
    sel[0, 0:128] = 1.0
    sel[1, 128:256] = 1.0
    L = 1024
    half = 32
    inv = (10000.0 ** (-np.arange(0, half, 2, dtype=np.float32) / half)).astype(np.float32)
    row = (np.arange(L) // 64).astype(np.float32)
    col = (np.arange(L) % 64).astype(np.float32)
    ang_r = (row[:, None] * inv[None, :]).astype(np.float32)
    ang_c = (col[:, None] * inv[None, :]).astype(np.float32)
    cosT = np.zeros((128, L), np.float32)
    sinT = np.zeros((128, L), np.float32)
    for p in range(128):
        n = p % 64
        f = n % 16
        ang = ang_r[:, f] if n < 32 else ang_c[:, f]
        sign = -1.0 if (n % 32) < 16 else 1.0
        cosT[p] = np.cos(ang)
        sinT[p] = sign * np.sin(ang)
    return cm, cnt, sel, cosT, sinT


def _win_dev(w_in):
    z = w_in[:, 0:1024]
    xs = w_in[:, 1024:2048]
    B = w_in[:, 2048:2560]
    C = w_in[:, 2560:3072]
    dt = w_in[:, 3072:3104]
    q = w_in[:, 3104:3616]
    k = w_in[:, 3616:4128]
    v = w_in[:, 4128:5152]
    g = w_in[:, 5152:6176]
    n = np.arange(64)
    partner = np.where((n % 32) < 16, n + 16, n - 16)
    perm = (np.arange(8)[:, None] * 64 + partner[None, :]).reshape(-1)
    return np.ascontiguousarray(np.concatenate([q, q[:, perm], k, k[:, perm], v, g, xs, B, C, dt, z], axis=1))


_NC_CACHE = {}


def kernel(x_prompt, x_sample, state_ssd, state_ret, c, c_ctx, w_mod, b_mod, norm_pre_w,
           norm_post_w, w_in, conv_w, conv_b, ssd_A_log, ssd_dt_bias, ssd_D, ssd_norm_w,
           ret_decay, ret_norm_w, w_out):
    f = lambda a: np.ascontiguousarray(np.asarray(a, dtype=np.float32))
    x_prompt, x_sample, state_ssd, state_ret, c, c_ctx = map(f, (x_prompt, x_sample, state_ssd, state_ret, c, c_ctx))
    cm, cnt, sel, cosT, sinT = _consts()
    win = _win_dev(f(w_in)[0])
    cw = f(conv_w)[0]
    convw = np.ascontiguousarray(cw.reshape(3, 16, 128).transpose(2, 1, 0).reshape(128, 48))
    convb = np.ascontiguousarray(f(conv_b)[0].reshape(16, 128).T)
    vecs = np.concatenate([f(ssd_A_log)[0].reshape(-1), f(ssd_dt_bias)[0].reshape(-1), f(ssd_D)[0].reshape(-1),
                           f(ret_decay)[0].reshape(-1)])[None, :]
    shared = {
        "w_mod": f(w_mod)[0], "b_mod": f(b_mod)[0][None, :], "npre": f(norm_pre_w)[0][None, :],
        "npost": f(norm_post_w)[0][None, :], "w_in": win, "convw": convw, "convb": convb,
        "vecs": np.ascontiguousarray(vecs), "snw": f(ssd_norm_w)[0][None, :], "rnw": f(ret_norm_w)[0][None, :],
        "w_out": f(w_out)[0], "cm": cm, "cnt": cnt, "sel": sel, "cosT": cosT, "sinT": sinT,
    }
    in_maps = []
    for i in range(NCORES):
        xs_ = np.concatenate([x_sample[i], x_prompt[2 * i], x_prompt[2 * i + 1]], axis=0)
        cond = np.stack([c[i], c_ctx], axis=0)
        cond_fm = np.ascontiguousarray(cond.reshape(2, 8, 128).transpose(2, 1, 0).reshape(128, 16))
        m = dict(shared)
        m.update({"x": np.ascontiguousarray(xs_), "cond_fm": cond_fm,
                  "s0s": np.ascontiguousarray(state_ssd[i, 0]), "s0r": np.ascontiguousarray(state_ret[i, 0])})
        in_maps.append(m)
    if "nc" not in _NC_CACHE:
        _NC_CACHE["nc"] = build_nc()
    res = run_bass_kernel_spmd(_NC_CACHE["nc"], in_maps, core_ids=list(range(NCORES)))
    y_prompt = np.zeros((16, 256, 1024), np.float32)
    y_sample = np.zeros((8, 1024, 1024), np.float32)
    ns_s = np.zeros((16, 1, 2, 16, 128, 64), np.float32)
    ns_r = np.zeros((16, 1, 2, 8, 64, 128), np.float32)
    for i in range(NCORES):
        rr = res.results[i]
        y = rr["y"]
        y_sample[i] = y[0:1024]
        y_prompt[2 * i] = y[1024:1280]
        y_prompt[2 * i + 1] = y[1280:1536]
        ns_s[2 * i:2 * i + 2, 0] = rr["ns_s"]
        ns_r[2 * i:2 * i + 2, 0] = rr["ns_r"]
    return (y_prompt, y_sample, ns_s, ns_r)
```

```python
import numpy as np
from contextlib import ExitStack
import concourse.bass as bass
import concourse.mybir as mybir
from concourse.bass_utils import run_bass_kernel_spmd

F32 = mybir.dt.float32
BF16 = mybir.dt.bfloat16
AF = mybir.ActivationFunctionType
ALU = mybir.AluOpType
AX = mybir.AxisListType

ENGS = ("pe", "act", "dve", "pool", "sp")
EPS = 1e-6
NCORES = 8
C_Q, C_QSW, C_K, C_KSW, C_V, C_G, C_XS, C_B, C_C, C_DT, C_Z = 0, 512, 1024, 1536, 2048, 3072, 4096, 5120, 5632, 6144, 6176
NCOLS = 7200


class Op:
    __slots__ = ("eng", "fn", "deps", "signal", "count", "dma_sem", "dma_count", "cost")

    def __init__(self, eng, fn, cost=0.3):
        self.eng = eng
        self.fn = fn
        self.cost = cost
        self.deps = []
        self.signal = False
        self.count = None
        self.dma_sem = None
        self.dma_count = None


class Sched:
    def __init__(self):
        self.ops = []
        self.last_writer = {}
        self.readers = {}
        self.dma_totals = {}
        self.dma_rr = 0
        self.dma_rr_sw = 0
        self.dma_last = {}

    NDMA = 24
    NDMA_SW = 4

    def add(self, eng, fn, reads=(), writes=(), dma=None, cost=0.3):
        op = Op(eng, fn, cost)
        seen = set()
        xb = [r for r in reads if len(r) == 2 and r[0] == "B" and r[1].isdigit()]
        if xb:
            reads = [r for r in reads if r not in xb]
            writes = list(writes) + xb

        def dep(d):
            if d is not None and id(d) not in seen:
                seen.add(id(d))
                op.deps.append(d)
        for r in reads:
            dep(self.last_writer.get(r))
        for r in writes:
            dep(self.last_writer.get(r))
            for rd in self.readers.get(r, ()):
                dep(rd)
        if dma is not None:
            if eng == "pool":
                dma = "sw%d" % (self.dma_rr_sw % self.NDMA_SW)
                self.dma_rr_sw += 1
            else:
                dma = "hw%d" % (self.dma_rr % self.NDMA)
                self.dma_rr += 1
            dep(self.dma_last.get(dma))
            self.dma_last[dma] = op
            op.dma_sem = dma
            self.dma_totals[dma] = self.dma_totals.get(dma, 0) + 16
            op.dma_count = self.dma_totals[dma]
        for r in reads:
            self.readers.setdefault(r, []).append(op)
        for r in writes:
            self.last_writer[r] = op
            self.readers[r] = []
        self.ops.append(op)
        return op

    def barrier(self):
        lastc = {}
        lastd = {}
        for op in self.ops:
            if op.fn is None:
                continue
            if op.dma_sem is not None:
                lastd[op.dma_sem] = op
            else:
                lastc[op.eng] = op
        allops = list(lastc.values()) + list(lastd.values())
        for e in ENGS:
            op = Op(e, None)
            op.deps = list(allops)
            self.ops.append(op)
        self.last_writer = {}
        self.readers = {}

    def _sched_segment(self, seg):
        n = len(seg)
        idx = {id(op): i for i, op in enumerate(seg)}
        indeg = [0] * n
        succ = [[] for _ in range(n)]
        for i, op in enumerate(seg):
            for d in op.deps:
                j = idx.get(id(d))
                if j is not None:
                    indeg[i] += 1
                    succ[j].append(i)
        tail = [0.0] * n
        for i in range(n - 1, -1, -1):
            m = 0.0
            for s in succ[i]:
                if tail[s] > m:
                    m = tail[s]
            tail[i] = m + seg[i].cost
        ready = [i for i in range(n) if indeg[i] == 0]
        t_eng = {e: 0.0 for e in ENGS}
        rdy = [0.0] * n
        order = []
        while ready:
            bi = None
            bk = None
            for i in ready:
                op = seg[i]
                st = t_eng[op.eng]
                if rdy[i] > st:
                    st = rdy[i]
                key = (st, -tail[i], i)
                if bk is None or key < bk:
                    bk = key
                    bi = i
            ready.remove(bi)
            op = seg[bi]
            st = bk[0]
            if op.dma_sem is not None:
                t_eng[op.eng] = st + 0.08
                fin = st + op.cost
            else:
                t_eng[op.eng] = st + op.cost
                fin = st + op.cost
            order.append(op)
            for s in succ[bi]:
                indeg[s] -= 1
                f2 = fin if seg[s].eng == op.eng and op.dma_sem is None else fin + 0.15
                if f2 > rdy[s]:
                    rdy[s] = f2
                if indeg[s] == 0:
                    ready.append(s)
        assert len(order) == n
        return order

    def schedule(self):
        new_ops = []
        seg = []
        lastc = {}

        def flush():
            for o in self._sched_segment(seg):
                new_ops.append(o)
                if o.dma_sem is None:
                    lastc[o.eng] = o
        for op in self.ops:
            if op.fn is None:
                if seg:
                    flush()
                    seg = []
                op.deps = [d for d in op.deps if d.dma_sem is not None] + list(lastc.values())
                new_ops.append(op)
            else:
                seg.append(op)
        if seg:
            flush()
        self.ops = new_ops

    def finalize(self, reorder=True):
        if reorder:
            self.schedule()
        for op in self.ops:
            for d in op.deps:
                if d.dma_sem is None and not (d.eng == op.eng and d.eng == "pe"):
                    d.signal = True
        counts = {e: 0 for e in ENGS}
        for op in self.ops:
            if op.signal and op.dma_sem is None:
                counts[op.eng] += 1
                op.count = counts[op.eng]
        known = {e: {} for e in ENGS}
        plan = {e: [] for e in ENGS}
        for op in self.ops:
            waits = {}
            for d in op.deps:
                if d.dma_sem is not None:
                    key = ("dma", d.dma_sem)
                    val = d.dma_count
                else:
                    if d.eng == op.eng and d.eng == "pe":
                        continue
                    key = ("eng", d.eng)
                    val = d.count
                if known[op.eng].get(key, 0) >= val:
                    continue
                if waits.get(key, 0) < val:
                    waits[key] = val
            for k, v in waits.items():
                known[op.eng][k] = v
            plan[op.eng].append((op, list(waits.items())))
        return plan


def run_plan(nc, plan, dma_keys):
    with ExitStack() as es:
        sems = {}
        for e in ENGS:
            sems[("eng", e)] = es.enter_context(nc.semaphore("s_" + e))
        for k in dma_keys:
            sems[("dma", k)] = es.enter_context(nc.semaphore("d_" + str(k)))
        block = es.enter_context(nc.Block())

        def body(ename):
            def f(eng):
                for op, waits in plan[ename]:
                    for key, val in waits:
                        eng.wait_ge(sems[key], val)
                    if op.fn is None:
                        continue
                    ins = op.fn(eng)
                    if op.dma_sem is not None:
                        ins.then_inc(sems[("dma", op.dma_sem)], 16)
                    elif op.signal:
                        ins.then_inc(sems[("eng", ename)], 1)
            return f

        block.tensor(body("pe"))
        block.scalar(body("act"))
        block.vector(body("dve"))
        block.gpsimd(body("pool"))
        block.sync(body("sp"))


class Arena:
    def __init__(self, nc, limit=229000):
        self.nc = nc
        self.top = 16640
        self.n = 0
        self.limit = limit
        self.peak = 0

    def alloc(self, name, shape, dt):
        nb = 4 if dt == F32 else 2
        size = nb
        for s in shape[1:]:
            size *= s
        size = (size + 63) // 64 * 64
        off = self.top
        self.top += size
        self.peak = max(self.peak, self.top)
        assert self.top <= self.limit, f"SBUF arena overflow at {name}: {self.top}"
        self.n += 1
        return self.nc.alloc_sbuf_tensor_at(f"{name}_{self.n}", list(shape), dt, offset=off)

    def mark(self):
        return self.top

    def release(self, m):
        self.top = m


class _Stop(Exception):
    pass


def build_nc(dbg=False, stop=None):
    nc = bass.Bass("TRN2", target_bir_lowering=False)

    dumps = {}

    def chk(name):
        if stop is not None and name == stop:
            if name in dumps:
                dumps[name]()
            raise _Stop()

    def din(name, shape):
        return nc.dram_tensor(name, list(shape), F32, kind="ExternalInput").ap()

    def dout(name, shape):
        return nc.dram_tensor(name, list(shape), F32, kind="ExternalOutput").ap()

    x_d = din("x", [1536, 1024])
    condfm_d = din("cond_fm", [128, 16])
    wmod_d = din("w_mod", [1024, 3072])
    bmod_d = din("b_mod", [1, 3072])
    npre_d = din("npre", [1, 1024])
    npost_d = din("npost", [1, 1024])
    win_d = din("w_in", [1024, NCOLS])
    convw_d = din("convw", [128, 48])
    convb_d = din("convb", [128, 16])
    vecs_d = din("vecs", [1, 96])
    snw_d = din("snw", [1, 1024])
    rnw_d = din("rnw", [1, 1024])
    wout_d = din("w_out", [2048, 1024])
    s0s_d = din("s0s", [2, 16, 128, 64])
    s0r_d = din("s0r", [2, 8, 64, 128])
    cm_d = din("cm", [128, 1408])
    cnt_d = din("cnt", [128, 16])
    sel_d = din("sel", [2, 256])
    cos_d = din("cosT", [128, 1024])
    sin_d = din("sinT", [128, 1024])
    y_d = dout("y", [1536, 1024])
    nss_d = dout("ns_s", [2, 2, 16, 128, 64])
    nsr_d = dout("ns_r", [2, 2, 8, 64, 128])

    dbg_d = dout("dbg", [128, 32768]) if dbg else None
    dbg_off = [0]
    S = Sched()
    A = Arena(nc)

    def dump(ap2d, name):
        if not dbg:
            return
        P, N = ap2d.shape
        S.barrier()
        scr = A.alloc("dbgscr", [128, 2048], F32)
        for c0 in range(0, N, 2048):
            n = min(2048, N - c0)
            S.add("dve", (lambda o, i_: lambda e: e.tensor_copy(out=o, in_=i_))(scr[0:P, 0:n], ap2d[:, c0:c0 + n]), [], ["dbgscr"])
            S.add("sp", (lambda o, i_: lambda e: e.dma_start(out=o, in_=i_))(dbg_d[0:P, dbg_off[0]:dbg_off[0] + n], scr[0:P, 0:n]), ["dbgscr"], [], dma="out")
            S.barrier()
            dbg_off[0] += n
        print("DUMP", name, dbg_off[0] - N, N)
    es = ExitStack()
    pf2 = [es.enter_context(nc.psum_tensor(f"pf{i}", [128, 1024], F32)) for i in range(3)]
    pb = [es.enter_context(nc.psum_tensor(f"pb{i}", [128, 1024], BF16)) for i in range(2)]
    banks = [(pf2[i][:, h * 512:(h + 1) * 512], f"B{2 * i + h}") for i in range(3) for h in range(2)]
    bank_rr = [0]

    def next_bank():
        b = banks[bank_rr[0] % 6]
        bank_rr[0] += 1
        return b

    def fsz(ap):
        n = 1
        for s_ in ap.shape[1:]:
            n *= s_
        return n

    def mm(out, lhsT, rhs, start, stop, reads, writes):
        passes = 4 if lhsT.dtype == F32 else 1
        S.add("pe", lambda e: e.matmul(out, lhsT=lhsT, rhs=rhs, start=start, stop=stop), reads, writes,
              cost=0.07 + passes * fsz(out) / 2400.0)

    def tr(out, in_, ident, reads, writes):
        S.add("pe", lambda e: e.transpose(out=out, in_=in_, identity=ident), reads, writes,
              cost=0.12 * (4 if in_.dtype == F32 else 1))

    def act(out, in_, func, reads, writes, bias=None, scale=None, accum_out=None):
        kw = {}
        if bias is not None:
            kw["bias"] = bias
        if scale is not None:
            kw["scale"] = scale
        if accum_out is not None:
            kw["accum_out"] = accum_out
        S.add("act", lambda e: e.activation(out=out, in_=in_, func=func, **kw), reads, writes, cost=0.25 + fsz(out) / 1200.0)

    def ecost(eng, out):
        return (0.08 + fsz(out) / 960.0) if eng == "dve" else (0.15 + fsz(out) / 480.0)

    def tt(eng, out, in0, in1, op, reads, writes):
        S.add(eng, lambda e: e.tensor_tensor(out=out, in0=in0, in1=in1, op=op), reads, writes, cost=ecost(eng, out))

    def ts(eng, out, in0, s1, s2, op0, op1, reads, writes):
        if s2 is None:
            S.add(eng, lambda e: e.tensor_scalar(out=out, in0=in0, scalar1=s1, scalar2=None, op0=op0), reads, writes, cost=ecost(eng, out))
        else:
            S.add(eng, lambda e: e.tensor_scalar(out=out, in0=in0, scalar1=s1, scalar2=s2, op0=op0, op1=op1), reads, writes, cost=ecost(eng, out))

    def stt(eng, out, in0, scalar, in1, op0, op1, reads, writes):
        S.add(eng, lambda e: e.scalar_tensor_tensor(out=out, in0=in0, scalar=scalar, in1=in1, op0=op0, op1=op1), reads, writes, cost=ecost(eng, out))

    def cp(eng, out, in_, reads, writes):
        if eng == "act":
            S.add("act", lambda e: e.copy(out=out, in_=in_), reads, writes, cost=0.25 + fsz(out) / 1200.0)
        else:
            S.add(eng, lambda e: e.tensor_copy(out=out, in_=in_), reads, writes, cost=ecost(eng, out))

    def dma(eng, out, in_, reads, writes, sem):
        nbytes = out.shape[0] * fsz(out) * (4 if out.dtype == F32 else 2)
        S.add(eng, lambda e: e.dma_start(out=out, in_=in_), reads, writes, dma=sem, cost=2.0 + nbytes / 150e3)

    def tred(out, in_, reads, writes):
        S.add("dve", lambda e: e.tensor_reduce(out=out, in_=in_, axis=AX.X, op=ALU.add), reads, writes, cost=ecost("dve", in_))

    def memset(eng, ap, val, writes):
        S.add(eng, lambda e: e.memset(ap, val), (), writes, cost=ecost(eng, ap))

    def bc_last(ap, n):
        return ap.unsqueeze(2).to_broadcast([ap.shape[0], ap.shape[1], n])

    def bc_mid(ap, n):
        return ap.unsqueeze(1).to_broadcast([ap.shape[0], n, ap.shape[1]])

    def body():
        cm = A.alloc("cm", [128, 1408], F32)
        Uincl, Lincl, Ustr, Lstr, ones_f, ident_f, P1, P2, iota1, iota2, Pm_f = [cm[:, i * 128:(i + 1) * 128] for i in range(11)]
        ident_b = A.alloc("identb", [128, 128], BF16)
        Pm_b = A.alloc("Pmb", [128, 128], BF16)
        cnt = A.alloc("cnt", [128, 16], F32)
        sel = A.alloc("sel", [2, 256], F32)
        vecs = A.alloc("vecs", [128, 96], F32)
        convw = A.alloc("convw", [128, 16, 3], F32)
        convb = A.alloc("convb", [128, 16], F32)
        G_fm = A.alloc("Gfm_", [128, 8, 2], F32)
        A_fm = A.alloc("Afm", [128, 8, 2], F32)
        sh_fm = A.alloc("shfm", [128, 8, 2], F32)
        negA = A.alloc("negA", [128, 32], F32)
        lamb = A.alloc("lamb", [128, 16], F32)
        lamq = A.alloc("lamq", [128, 8], F32)
        decq = A.alloc("decq", [128, 8], F32)
        tails = A.alloc("tails", [128, 16], F32)
        Eret = A.alloc("Eret", [128, 8, 128], F32)
        DRm = [[A.alloc(f"DRm{d}{hh}", [128, 4, 128], F32) for hh in range(2)] for d in range(2)]
        st = A.alloc("st", [128, 64], F32)
        Dcol = A.alloc("Dcol", [128, 8], F32)
        Ddiag = A.alloc("Ddiag", [128, 8, 128], BF16)
        hT = A.alloc("hT", [128, 8, 1024], BF16)
        mixT = A.alloc("mixT", [128, 16, 1024], BF16)
        mtmp = A.mark()
        DRf = A.alloc("DRf", [128, 4, 128], F32)
        DRb = A.alloc("DRb", [128, 4, 128], F32)

        dma("sp", cm[:], cm_d, (), ["cm"], "cst")
        dma("sp", cnt[:], cnt_d, (), ["cnt"], "cst")
        dma("sp", sel[:], sel_d, (), ["sel"], "cst")
        dma("sp", vecs[:], vecs_d.partition_broadcast(128), (), ["vecs"], "cst")
        dma("sp", convw[:].rearrange("p a b -> p (a b)"), convw_d, (), ["convw"], "cst")
        dma("sp", convb[:], convb_d, (), ["convb"], "cst")
        S.barrier()
        cp("dve", ident_b[:], ident_f, [], ["identb"])
        cp("dve", Pm_b[:], Pm_f, [], ["Pmb"])
        act(negA[:], vecs[:, 0:32], AF.Exp, [], ["negA"])
        ts("dve", negA[:], negA[:], -1.0, None, ALU.mult, None, ["negA"], ["negA"])
        act(lamb[:], vecs[:, 80:96], AF.Exp, [], ["lamb"])
        ts("dve", lamb[:], lamb[:], -1.0, None, ALU.mult, None, ["lamb"], ["lamb"])
        for d in range(2):
            for t in range(4):
                for hh in range(2):
                    cp("dve", lamq[64 * hh:64 * hh + 64, d * 4 + t:d * 4 + t + 1],
                       lamb[64 * hh:64 * hh + 64, d * 8 + 2 * t + hh:d * 8 + 2 * t + hh + 1], ["lamb"], ["lamq"])
        act(decq[:], lamq[:], AF.Exp, ["lamq"], ["decq"], scale=128.0)
        tt("dve", tails[:], lamb[:], cnt[:], ALU.mult, ["lamb"], ["tails"])
        act(tails[:], tails[:], AF.Exp, ["tails"], ["tails"])
        for h in range(8):
            ts("dve", Eret[:, h, :], P1, lamb[:, h:h + 1], None, ALU.mult, None, ["lamb"], ["Eret"])
            stt("dve", Eret[:, h, :], P2, lamb[:, 8 + h:9 + h], Eret[:, h, :], ALU.mult, ALU.add, ["lamb", "Eret"], ["Eret"])
        act(Eret[:], Eret[:], AF.Exp, ["Eret"], ["Eret"])
        tt("dve", Eret[:], Eret[:], bc_mid(ident_f, 8), ALU.add, ["Eret"], ["Eret"])
        for t in range(4):
            act(DRf[:, t, :], iota1, AF.Exp, ["lamq"], ["DRf"], scale=lamq[:, t:t + 1])
            act(DRb[:, t, :], iota2, AF.Exp, ["lamq"], ["DRb"], scale=lamq[:, 4 + t:5 + t])
        for t in range(8):
            for hh in range(2):
                cp("dve", Dcol[64 * hh:64 * hh + 64, t:t + 1], vecs[64 * hh:64 * hh + 64, 64 + 2 * t + hh:65 + 2 * t + hh], [], ["Dcol"])
        for t in range(8):
            ts("dve", Ddiag[:, t, :], ident_f, Dcol[:, t:t + 1], None, ALU.mult, None, ["Dcol"], ["Ddiag"])
        for d, src, sn_ in [(0, DRf, "DRf"), (1, DRb, "DRb")]:
            for hh in range(2):
                memset("pool", DRm[d][hh][:], 0.0, [f"DRm{d}{hh}"])
                cp("dve", DRm[d][hh][64 * hh:64 * hh + 64, :, :], src[64 * hh:64 * hh + 64, :, :], [sn_, f"DRm{d}{hh}"], [f"DRm{d}{hh}"])

        xbuf = [A.alloc(f"xbuf{i}", [128, 1024], F32) for i in range(3)]
        xnb = [A.alloc(f"xnb{i}", [128, 1024], BF16) for i in range(8)]
        junk = A.alloc("junk", [128, 1024], BF16)
        m0 = A.mark()
        wflat = [A.alloc(f"wbuf{i}", [128, 8448], BF16) for i in range(2)]
        condfm = A.alloc("condfm", [128, 8, 2], F32)
        csil = A.alloc("csil", [128, 8, 2], BF16)
        modrow = A.alloc("modrow", [2, 3072], F32)
        bmod2 = A.alloc("bmod2", [2, 3072], F32)
        np2 = A.alloc("np2", [2, 2048], F32)
        dma("sp", condfm[:].rearrange("p a b -> p (a b)"), condfm_d, (), ["condfm"], "cst")
        dma("sp", bmod2[:], bmod_d.partition_broadcast(2), (), ["bmod2"], "cst")
        dma("sp", np2[:, 0:1024], npre_d.partition_broadcast(2), (), ["np2a"], "cst")
        dma("sp", np2[:, 1024:2048], npost_d.partition_broadcast(2), (), ["np2b"], "cst")
        act(csil[:], condfm[:], AF.Silu, ["condfm"], ["csil"])
        wmod_v = wmod_d.rearrange("(k p) c -> p k c", p=128)
        for s in range(3):
            wv = wflat[s % 2][:, 0:8192].rearrange("p (k c) -> p k c", k=8)
            dma("pool", wv, wmod_v[:, :, s * 1024:(s + 1) * 1024], (), [f"wb{s % 2}"], f"wb{s % 2}")
            for half in range(2):
                bk, bn = next_bank()
                for k in range(8):
                    mm(bk[0:2, :], csil[:, k, :], wv[:, k, half * 512:(half + 1) * 512], k == 0, k == 7,
                       ["csil", f"wb{s % 2}"], [bn])
                c0 = s * 1024 + half * 512
                tt("dve", modrow[:, c0:c0 + 512], bk[0:2, :], bmod2[:, c0:c0 + 512], ALU.add, [bn, "bmod2"], [f"mod{s}"])
        stt("dve", modrow[:, 1024:2048], modrow[:, 1024:2048], 1.0, np2[:, 0:1024], ALU.add, ALU.mult, ["mod1", "np2a"], ["mod1"])
        tt("dve", modrow[:, 2048:3072], modrow[:, 2048:3072], np2[:, 1024:2048], ALU.mult, ["mod2", "np2b"], ["mod2"])
        bk, bn = next_bank()
        for k in range(8):
            tr(bk[:, 2 * k:2 * k + 2], modrow[0:2, 1024 + k * 128:1024 + (k + 1) * 128], ident_f[0:2, 0:2], ["mod1"], [bn])
            tr(bk[:, 16 + 2 * k:16 + 2 * k + 2], modrow[0:2, k * 128:(k + 1) * 128], ident_f[0:2, 0:2], ["mod0"], [bn])
        cp("dve", A_fm[:].rearrange("p a b -> p (a b)"), bk[:, 0:16], [bn], ["Afm"])
        cp("dve", sh_fm[:].rearrange("p a b -> p (a b)"), bk[:, 16:32], [bn], ["shfm"])
        bk, bn = next_bank()
        for kk in range(8):
            tr(bk[:, 2 * kk:2 * kk + 2], modrow[0:2, 2048 + kk * 128:2048 + (kk + 1) * 128], ident_f[0:2, 0:2], ["mod2"], [bn])
        cp("dve", G_fm[:].rearrange("p a b -> p (a b)"), bk[:, 0:16], [bn], ["Gfm_"])
        A.release(mtmp)
        chk("stage0")

        units = [
            dict(tok0=0, T=1024, nseq=1, L=1024, r=0, rope=True, init=True, sout=False),
            dict(tok0=1024, T=512, nseq=2, L=256, r=1, rope=False, init=False, sout=True),
        ]
        win_v = win_d.rearrange("(k p) c -> p k c", p=128)
        wout_v = wout_d.rearrange("(k p) c -> p k c", p=128)
        PRE_A = mixT[:, 0:8, :].rearrange("p a b -> p (a b)")
        PRE_B = mixT[:, 8:16, :].rearrange("p a b -> p (a b)")
        PRE_H = hT[:].rearrange("p a b -> p (a b)")
        pre = {}

        def prefetch(key, flat, pieces, ncols, kdim=8, src=None, after=()):
            src = win_v if src is None else src
            wv = flat[:, 0:kdim * ncols].rearrange("p (k c) -> p k c", k=kdim)
            for (c0, n, d0) in pieces:
                dma("pool", wv[:, :, d0:d0 + n], src[:, :, c0:c0 + n], list(after), [f"pre_{key}"], "pre")
            pre[key] = wv

        prefetch("Rq0", PRE_A, [(C_Q, 512, 0), (C_K, 512, 512)], 1024, after=["wb0", "wb1"])
        prefetch("Rv0", PRE_B, [(C_V, 1024, 0)], 1024, after=["pre_Rq0"])
        if dbg:
            dbg_d = {}

        for ui, U in enumerate(units):
            tok0, T, nseq, L, r = U["tok0"], U["T"], U["nseq"], U["L"], U["r"]
            nch = T // 128
            nchs = L // 128
            ntg = T // 512
            hTr = lambda c: f"hT{c}"

            if ui > 0:
                m1 = A.mark()
                xbuf = [A.alloc(f"xbuf{i}", [128, 1024], F32) for i in range(3)]
                xnb = [A.alloc(f"xnb{i}", [128, 1024], BF16) for i in range(2)]
                junk = A.alloc("junk", [128, 1024], BF16)
            else:
                m1 = mtmp
            for c in range(nch):
                xb = xbuf[c % 3]
                xr = f"xb{c % 3}"
                xn_, xnr = xnb[c % len(xnb)], f"xnb{c % len(xnb)}"
                dma("sp", xb[:], x_d[tok0 + c * 128:tok0 + (c + 1) * 128, :], (), [xr], xr)
                sc = st[:, 4 * (c % 2):4 * (c % 2) + 4]
                sr = f"st{c % 2}"
                act(junk[:], xb[:], AF.Square, [xr], ["junk", sr], accum_out=sc[:, 0:1])
                act(sc[:, 1:2], sc[:, 0:1], AF.Ln, [sr], [sr], scale=1.0 / 1024, bias=EPS)
                act(sc[:, 2:3], sc[:, 1:2], AF.Exp, [sr], [sr], scale=-0.5)
                ts("dve", xn_[:], xb[:], sc[:, 2:3], None, ALU.mult, None, [xr, sr], [xnr])
                for k_ in range(8):
                    half = k_ // 4
                    tr(pb[half][:, (k_ % 4) * 128:(k_ % 4 + 1) * 128], xn_[:, k_ * 128:(k_ + 1) * 128], ident_b[:], [xnr, "identb"], [f"B{6 + half}"])
                for k_ in range(8):
                    half = k_ // 4
                    o = hT[:, k_, c * 128:(c + 1) * 128]
                    i_ = pb[half][:, (k_ % 4) * 128:(k_ % 4 + 1) * 128]
                    if half == 0:
                        act(o, i_, AF.Identity, ["B6", "Afm", "shfm"], [hTr(c)],
                            bias=sh_fm[:, k_, r:r + 1], scale=A_fm[:, k_, r:r + 1])
                    else:
                        ts("dve", o, i_, A_fm[:, k_, r:r + 1], sh_fm[:, k_, r:r + 1], ALU.mult, ALU.add,
                           ["B7", "Afm", "shfm"], [hTr(c)])
            S.barrier()
            A.release(m1)
            dumps[f"hT{ui}"] = lambda: dump(hT[:].rearrange("p a b -> p (a b)"), "hT")
            chk(f"hT{ui}")

            def load_slot(slot, pieces, ncols, key=None):
                if key is not None and key in pre:
                    return pre.pop(key), []
                wv = wflat[slot][:, 0:8 * ncols].rearrange("p (k c) -> p k c", k=8)
                names = []
                for pi, (c0, n, d0) in enumerate(pieces):
                    nm = f"wb{slot}p{pi}"
                    dma("pool", wv[:, :, d0:d0 + n], win_v[:, :, c0:c0 + n], (), [nm], f"wb{slot}")
                    names.append(nm)
                return wv, names

            def fm_tile(wv, wnames, ct, tg):
                bk, bn = next_bank()
                hr = [hTr(c) for c in range(tg * 4, tg * 4 + 4)]
                for k in range(8):
                    mm(bk, wv[:, k, ct * 128:(ct + 1) * 128], hT[:, k, tg * 512:(tg + 1) * 512], k == 0, k == 7,
                       wnames + hr, [bn])
                return bk, bn

            def tm_tile(wv, wnames, c, c0, n):
                bk, bn = next_bank()
                for k in range(8):
                    mm(bk[:, 0:n], hT[:, k, c * 128:(c + 1) * 128], wv[:, k, c0:c0 + n], k == 0, k == 7,
                       wnames + [hTr(c)], [bn])
                return bk, bn

            m2 = A.mark()
            qT = A.alloc("qT", [128, 4, T], BF16)
            kT = A.alloc("kT", [128, 4, T], BF16)
            kTm = [A.alloc(f"kTm{hh}", [128, 4, T], BF16) for hh in range(2)]
            for hh in range(2):
                memset("pool", kTm[hh][:], 0.0, [f"kTm{hh}"])
            v_tok = A.alloc("vtok", [128, nch, 1024], BF16)
            gs = A.alloc("gs", [128, nch, 1024], BF16)
            rnw_b = A.alloc("rnwb", [128, 1024], F32)
            dma("sp", rnw_b[:], rnw_d.partition_broadcast(128), (), ["rnwb"], "cst")
            Srf = A.alloc("Srf", [128, 4, 128], F32)
            Srb = A.alloc("Srb", [128, 4, 128], F32)
            Srb_all = A.alloc("Srball", [128, nch, 512], BF16)
            ktl2 = [A.alloc(f"ktl{i}", [128, 512], BF16) for i in range(2)]
            rstep = [0]

            def ret_state_io(S_t, dram3, load, sname):
                dv = dram3.rearrange("(t hh) n v -> hh n t v", hh=2)
                for hh in range(2):
                    if load:
                        dma("sp", S_t[64 * hh:64 * hh + 64, :, :], dv[hh], (), [sname], "sio")
                    else:
                        dma("sp", dv[hh], S_t[64 * hh:64 * hh + 64, :, :], [sname], (), "out")

            def ret_update(S_t, sname, cu, tail_lo, dq_lo):
                tok = slice(cu * 128, (cu + 1) * 128)
                kp = rstep[0] % 2
                rstep[0] += 1
                ktl, ktn = ktl2[kp], f"ktl{kp}"
                for t in range(4):
                    tr(pb[0][:, t * 128:(t + 1) * 128], kT[:, t, tok], ident_b[:], ["kT", "identb"], ["B6"])
                tt("dve", ktl[:].rearrange("p (h n) -> p h n", h=8), pb[0][:, 0:512].rearrange("p (h n) -> p h n", h=8),
                   bc_last(tails[:, tail_lo:tail_lo + 8], 64), ALU.mult, ["B6", "tails"], [ktn])
                for t in range(4):
                    mm(pf2[2][:, t * 256:(t + 1) * 256], ktl[:, t * 128:(t + 1) * 128], v_tok[:, cu, t * 256:(t + 1) * 256],
                       True, True, [ktn, f"vtok{cu}"], ["B4", "B5"])
                tt("dve", S_t[:], S_t[:], bc_last(decq[:, dq_lo:dq_lo + 4], 128), ALU.mult, [sname, "decq"], [sname])
                pu = pf2[2][:].rearrange("p (t c) -> p t c", t=4)
                tt("dve", S_t[0:64, :, :], S_t[0:64, :, :], pu[0:64, :, 0:128], ALU.add, [sname, "B4", "B5"], [sname])
                tt("dve", S_t[64:128, :, :], S_t[64:128, :, :], pu[64:128, :, 128:256], ALU.add, [sname, "B4", "B5"], [sname])

            def ret_bwd_sweep():
                for s in range(nseq):
                    if U["init"]:
                        ret_state_io(Srb, s0r_d[1], True, "Srb")
                    else:
                        memset("dve", Srb[:], 0.0, ["Srb"])
                    for c in reversed(range(nchs)):
                        cu = s * nchs + c
                        cp("act", Srb_all[:, cu, :], Srb[:].rearrange("p a b -> p (a b)"), ["Srb"], [f"Srball{cu}"])
                        if c > 0 or U["sout"]:
                            ret_update(Srb, "Srb", cu, 8, 4)
                    if U["sout"]:
                        ret_state_io(Srb, nsr_d[s, 1], False, "Srb")
            m3 = A.mark()
            wflat = [A.alloc(f"wbufr{i}", [128, 8448], BF16) for i in range(2)]
            if U["rope"]:
                cosT = A.alloc("cosT", [128, 1024], F32)
                sinT = A.alloc("sinT", [128, 1024], F32)
                rt1_ = [A.alloc(f"rt1{i}", [128, 512], F32) for i in range(2)]
                rt2_ = [A.alloc(f"rt2{i}", [128, 512], F32) for i in range(2)]
                qbf = [A.alloc(f"qbf{i}", [128, 512], BF16) for i in range(2)]
                dma("sp", cosT[:], cos_d, (), ["cosT"], "cst")
                dma("sp", sinT[:], sin_d, (), ["sinT"], "cst")
                wv, wn = load_slot(0, [(C_Q, 512, 0), (C_K, 512, 512)], 1024, key=f"Rq{ui}")
                it_ = 0
                for ti in range(8):
                    dst, dname, kscale = (qT, "qT", None) if ti < 4 else (kT, "kT", 0.125)
                    tl = ti % 4
                    for tg in range(ntg):
                        par = it_ % 2
                        it_ += 1
                        rt1, rt2 = rt1_[par], rt2_[par]
                        r1n, r2n, qbn = f"rt1{par}", f"rt2{par}", f"qbf{par}"
                        ba, bna = fm_tile(wv, wn, ti, tg)
                        cp("act", qbf[par][:], ba, [bna], [qbn])
                        bb, bnb = next_bank()
                        mm(bb, Pm_b[:], qbf[par][:], True, True, [qbn, "Pmb"], [bnb])
                        cs = cosT[:, tg * 512:(tg + 1) * 512]
                        sn = sinT[:, tg * 512:(tg + 1) * 512]
                        if kscale is None:
                            tt("dve", rt1[:], ba, cs, ALU.mult, [bna, "cosT"], [r1n])
                            tt("dve", rt2[:], bb, sn, ALU.mult, [bnb, "sinT"], [r2n])
                        else:
                            stt("dve", rt1[:], ba, kscale, cs, ALU.mult, ALU.mult, [bna, "cosT"], [r1n])
                            stt("dve", rt2[:], bb, kscale, sn, ALU.mult, ALU.mult, [bnb, "sinT"], [r2n])
                        tt("pool", dst[:, tl, tg * 512:(tg + 1) * 512], rt1[:], rt2[:], ALU.add, [r1n, r2n], [dname])
                        if dname == "kT":
                            for hh in range(2):
                                cp("act", kTm[hh][64 * hh:64 * hh + 64, tl, tg * 512:(tg + 1) * 512],
                                   kT[64 * hh:64 * hh + 64, tl, tg * 512:(tg + 1) * 512], ["kT", f"kTm{hh}"], [f"kTm{hh}"])
                nslot = 1
            else:
                wv, wn = load_slot(0, [(C_Q, 512, 0), (C_K, 512, 512)], 1024, key=f"Rqk{ui}")
                for ti in range(8):
                    for tg in range(ntg):
                        bk, bn = fm_tile(wv, wn, ti, tg)
                        if ti < 4:
                            cp("act", qT[:, ti, tg * 512:(tg + 1) * 512], bk, [bn], ["qT"])
                        else:
                            S.add("act", (lambda o, i_: lambda e: e.mul(out=o, in_=i_, mul=0.125))(kT[:, ti - 4, tg * 512:(tg + 1) * 512], bk), [bn], ["kT"])
                            for hh in range(2):
                                cp("dve", kTm[hh][64 * hh:64 * hh + 64, ti - 4, tg * 512:(tg + 1) * 512],
                                   kT[64 * hh:64 * hh + 64, ti - 4, tg * 512:(tg + 1) * 512], ["kT", f"kTm{hh}"], [f"kTm{hh}"])
                nslot = 1
            for (c0, dst, dname, fn) in [(C_V, v_tok, "vtok", None), (C_G, gs, "gs", AF.Silu)]:
                wv, wn = load_slot(nslot % 2, [(c0, 1024, 0)], 1024, key=(f"Rv{ui}" if fn is None else None))
                nslot += 1
                for c in range(nch):
                    for cg in range(2):
                        bk, bn = tm_tile(wv, wn, c, cg * 512, 512)
                        o = dst[:, c, cg * 512:(cg + 1) * 512]
                        if fn is None:
                            cp("act", o, bk, [bn], [f"{dname}{c}"])
                        else:
                            act(o, bk, fn, [bn], [f"{dname}{c}"])
                    if fn is not None:
                        tt("pool", dst[:, c, :], dst[:, c, :], rnw_b[:], ALU.mult, [f"{dname}{c}", "rnwb"], [f"{dname}{c}"])
                if fn is None:
                    ret_bwd_sweep()
            S.barrier()
            A.release(m3)
            dumps[f"Rproj{ui}"] = lambda: (dump(qT[:].rearrange("p a b -> p (a b)"), "qT"), dump(kT[:].rearrange("p a b -> p (a b)"), "kT"),
                                          dump(v_tok[:].rearrange("p a b -> p (a b)"), "v"), dump(gs[:].rearrange("p a b -> p (a b)"), "gs"))
            chk(f"Rproj{ui}")

            prefetch(f"Sxs{ui}", PRE_A, [(C_XS, 1024, 0)], 1024)
            if ui == 1:
                prefetch(f"Sbc{ui}", wpreX[:], [(C_B, 1056, 0)], 1056)
                prefetch(f"Sz{ui}", wpre1[:], [(C_Z, 1024, 0)], 1024)
            Srf_bf = A.alloc("Srfbf", [128, 4, 128], BF16)
            Sm2 = [A.alloc(f"Sm{i}", [128, 1024], BF16) for i in range(2)]
            qfm2 = [[[A.alloc(f"qfm{i}{d}{hh}", [128, 4, 128], BF16) for hh in range(2)] for d in range(2)] for i in range(2)]
            yr2 = [A.alloc(f"yr{i}", [128, 8, 128], F32) for i in range(2)]
            sq2 = [A.alloc(f"sq{i}", [128, 8, 128], F32) for i in range(2)]
            mixr2 = [A.alloc(f"mixr{i}", [128, 1024], BF16) for i in range(2)]
            gst2 = [A.alloc(f"gst{i}", [128, 48], F32) for i in range(2)]

            for s in range(nseq):
                if U["init"]:
                    ret_state_io(Srf, s0r_d[0], True, "Srf")
                else:
                    memset("dve", Srf[:], 0.0, ["Srf"])
                for c in range(nchs):
                    cu = s * nchs + c
                    tok = slice(cu * 128, (cu + 1) * 128)
                    cpar = cu % 2
                    Sm, qfm, yr, sq, mixr, gst = Sm2[cpar], qfm2[cpar], yr2[cpar], sq2[cpar], mixr2[cpar], gst2[cpar]
                    P_ = f"c{cpar}"
                    for h in range(8):
                        t, hh = h // 2, h % 2
                        ps_ = slice(64 * hh, 64 * hh + 64)
                        mm(pf2[0][:, h * 128:(h + 1) * 128], kTm[hh][:, t, tok], qT[:, t, tok], True, True,
                           [f"kTm{hh}", "qT"], [f"B{h // 4}"])
                    for half in range(2):
                        cs_ = slice(half * 512, (half + 1) * 512)
                        tt("dve", Sm[:, cs_], pf2[0][:, cs_], Eret[:].rearrange("p h i -> p (h i)")[:, cs_], ALU.mult,
                           [f"B{half}", "Eret"], [P_ + f"Sm{half}"])
                    chk("Rs_sc")
                    for d in range(2):
                        for hh in range(2):
                            tt("pool", qfm[d][hh][:], qT[:, :, tok], DRm[d][hh][:], ALU.mult, ["qT"], [P_ + f"qfm{d}{hh}"])
                    cp("act", Srf_bf[:], Srf[:], ["Srf"], ["Srfbf"])
                    for h in range(8):
                        t, hh = h // 2, h % 2
                        ps_ = slice(64 * hh, 64 * hh + 64)
                        o = pf2[1][:, h * 128:(h + 1) * 128]
                        wn_ = [f"B{2 + h // 4}"]
                        mm(o, Sm[:, h * 128:(h + 1) * 128], v_tok[:, cu, h * 128:(h + 1) * 128], True, False,
                           [P_ + f"Sm{h // 4}", f"vtok{cu}"], wn_)
                        mm(o, qfm[0][hh][:, t, :], Srf_bf[:, t, :], False, False, [P_ + f"qfm0{hh}", "Srfbf"], wn_)
                        mm(o, qfm[1][hh][:, t, :], Srb_all[:, cu, t * 128:(t + 1) * 128], False, True, [P_ + f"qfm1{hh}", f"Srball{cu}"], wn_)
                    chk("Rs_y")
                    yrf = yr[:].rearrange("p h v -> p (h v)")
                    sqf = sq[:].rearrange("p h v -> p (h v)")
                    for h in range(8):
                        half = h // 4
                        src_ = pf2[1][:, h * 128:(h + 1) * 128]
                        act(yr[:, h, :], src_, AF.Identity, [f"B{2 + half}"], [P_ + f"yr{half}", P_ + "gst0"], accum_out=gst[:, h:h + 1])
                        act(sq[:, h, :], src_, AF.Square, [f"B{2 + half}"], [P_ + "sq", P_ + "gst1"], accum_out=gst[:, 8 + h:9 + h])
                    ts("dve", gst[:, 16:24], gst[:, 0:8], 1.0 / 128, None, ALU.mult, None, [P_ + "gst0"], [P_ + "gst2"])
                    tt("dve", gst[:, 24:32], gst[:, 16:24], gst[:, 16:24], ALU.mult, [P_ + "gst2"], [P_ + "gst3"])
                    stt("dve", gst[:, 32:40], gst[:, 8:16], 1.0 / 128, gst[:, 24:32], ALU.mult, ALU.subtract, [P_ + "gst1", P_ + "gst3"], [P_ + "gst4"])
                    act(gst[:, 40:48], gst[:, 32:40], AF.Ln, [P_ + "gst4"], [P_ + "gst5"], bias=EPS)
                    act(gst[:, 40:48], gst[:, 40:48], AF.Exp, [P_ + "gst5"], [P_ + "gst5"], scale=-0.5)
                    tt("dve", yr[:], yr[:], bc_last(gst[:, 16:24], 128), ALU.subtract, [P_ + "yr0", P_ + "yr1", P_ + "gst2"], [P_ + "yr0", P_ + "yr1"])
                    tt("pool", yr[:], yr[:], bc_last(gst[:, 40:48], 128), ALU.mult, [P_ + "yr0", P_ + "yr1", P_ + "gst5"], [P_ + "yr0", P_ + "yr1"])
                    tt("dve", mixr[:], yrf, gs[:, cu, :], ALU.mult, [P_ + "yr0", P_ + "yr1", f"gs{cu}"], [P_ + "mixr"])
                    chk("Rs_gn")
                    for t in range(8):
                        tr(pb[1][:, t * 128:(t + 1) * 128], mixr[:, t * 128:(t + 1) * 128], ident_b[:], [P_ + "mixr", "identb"], ["B7"])
                    cp("act", mixT[:, 8:16, tok], pb[1][:].rearrange("p (t c) -> p t c", t=8), ["B7"], [f"mixTr{cu}"])
                    if c < nchs - 1 or U["sout"]:
                        ret_update(Srf, "Srf", cu, 0, 0)
                if U["sout"]:
                    ret_state_io(Srf, nsr_d[s, 0], False, "Srf")
            S.barrier()
            A.release(m2)
            dumps[f"Rscan{ui}"] = lambda: dump(mixT[:, 8:16, :].rearrange("p a b -> p (a b)"), "mixTr")
            chk(f"Rscan{ui}")

            m4 = A.mark()
            xbcT = A.alloc("xbcT", [128, 16, T], BF16)
            zs = A.alloc("zs", [128, nch, 1024], BF16)
            dtraw = A.alloc("dtraw", [128, nch, 32], F32)
            dtv = A.alloc("dtv", [128, nch, 32], F32)
            la = A.alloc("la", [128, nch, 32], F32)
            decs = A.alloc("decs", [128, nch, 96], F32)
            wts = A.alloc("wts", [128, nch, 32], F32)
            snw_b = A.alloc("snwb", [128, 1024], F32)
            dma("sp", snw_b[:], snw_d.partition_broadcast(128), (), ["snwb"], "cst")
            Sb = A.alloc("Sb", [128, 16, 64], F32)
            Sb_all = A.alloc("Sball", [128, nch, 1024], BF16)
            xwm2 = [A.alloc(f"xwm{i}", [128, 16, 64], BF16) for i in range(2)]
            Btok2 = [A.alloc(f"Btok{i}", [128, 4, 128], BF16) for i in range(2)]
            tmpa = A.alloc("tmpa", [128, nch, 32], F32)
            tmpb = A.alloc("tmpb", [128, nch, 32], F32)
            step = [0]
            SfN = ["Sf0", "Sf1", "Sf2", "Sf3"]

            def ssd_state_io(S_t, dram3, load, snames):
                dv = dram3.rearrange("h n p -> n h p")
                if load:
                    dma("sp", S_t[:], dv, (), snames, "sio")
                else:
                    dma("sp", dv, S_t[:], snames, (), "out")

            def xs_transposes(cu):
                tok = slice(cu * 128, (cu + 1) * 128)
                for t in range(8):
                    tr(pb[0][:, t * 128:(t + 1) * 128], xbcT[:, t, tok], ident_b[:], [f"xbcT{t}", "identb"], ["B6"])
                return pb[0][:].rearrange("p (h d) -> p h d", h=16)

            def b_transposes(cu, Btok, bname):
                tok = slice(cu * 128, (cu + 1) * 128)
                for g in range(4):
                    tr(pb[1][:, g * 128:(g + 1) * 128], xbcT[:, 8 + g, tok], ident_b[:], [f"xbcT{8 + g}", "identb"], ["B7"])
                cp("act", Btok[:].rearrange("p g n -> p (g n)"), pb[1][:, 0:512], ["B7"], [bname])

            def ssd_prep():
                ts("dve", tmpa[:], dtraw[:], -1.0, None, ALU.mult, None, ["dtraw"], ["tmpa"])
                tt("dve", tmpa[:], tmpa[:], dtraw[:], ALU.min, ["tmpa", "dtraw"], ["tmpa"])
                act(tmpa[:], tmpa[:], AF.Exp, ["tmpa"], ["tmpa"])
                act(tmpa[:], tmpa[:], AF.Ln, ["tmpa"], ["tmpa"], bias=1.0)
                ts("dve", tmpb[:], dtraw[:], 0.0, None, ALU.max, None, ["dtraw"], ["tmpb"])
                tt("dve", dtv[:], tmpa[:], tmpb[:], ALU.add, ["tmpa", "tmpb"], ["dtv"])
                tt("dve", la[:], dtv[:], bc_mid(negA[:], nch), ALU.mult, ["dtv", "negA"], ["la"])
                for c in range(nch):
                    o = pf2[0][:, c * 128:c * 128 + 96]
                    wn_ = [f"B{c // 4}"]
                    mm(o[:, 0:16], Uincl, la[:, c, 0:16], True, True, ["la"], wn_)
                    mm(o[:, 16:32], Lincl, la[:, c, 16:32], True, True, ["la"], wn_)
                    mm(o[:, 32:48], Lstr, la[:, c, 0:16], True, True, ["la"], wn_)
                    mm(o[:, 48:64], Ustr, la[:, c, 16:32], True, True, ["la"], wn_)
                    mm(o[:, 64:96], ones_f, la[:, c, 0:32], True, True, ["la"], wn_)
                for hb in range((nch + 3) // 4):
                    c_lo, c_hi = hb * 4, min(nch, hb * 4 + 4)
                    act(decs[:, c_lo:c_hi, :], pf2[0][:, c_lo * 128:c_hi * 128].rearrange("p (c x) -> p c x", x=128)[:, :, 0:96],
                        AF.Exp, [f"B{hb}"], ["decs"])
                tt("dve", wts[:], decs[:, :, 32:64], dtv[:], ALU.mult, ["decs", "dtv"], ["wts"])

            def ssd_bwd_sweep():
                for s in range(nseq):
                    if U["init"]:
                        ssd_state_io(Sb, s0s_d[1], True, ["Sb"])
                    else:
                        memset("dve", Sb[:], 0.0, ["Sb"])
                    for c in reversed(range(nchs)):
                        cu = s * nchs + c
                        cp("act", Sb_all[:, cu, :], Sb[:].rearrange("p h d -> p (h d)"), ["Sb"], [f"Sball{cu}"])
                        if c > 0 or U["sout"]:
                            par = step[0] % 2
                            step[0] += 1
                            xwm, xwn = xwm2[par], f"xwm{par}"
                            Btok, btn = Btok2[par], f"Btok{par}"
                            xsv = xs_transposes(cu)
                            tt("dve", xwm[:], xsv, bc_last(wts[:, cu, 16:32], 64), ALU.mult, ["B6", "wts"], [xwn])
                            b_transposes(cu, Btok, btn)
                            for g in range(4):
                                mm(pf2[2][:, g * 256:(g + 1) * 256], Btok[:, g, :], xwm[:, 4 * g:4 * g + 4, :].rearrange("p h d -> p (h d)"),
                                   True, True, [btn, xwn], [["B4"], ["B4"], ["B5"], ["B5"]][g])
                            tt("dve", Sb[:], Sb[:], bc_last(decs[:, cu, 80:96], 64), ALU.mult, ["Sb", "decs"], ["Sb"])
                            for half in range(2):
                                tt("dve", Sb[:, 8 * half:8 * half + 8, :], Sb[:, 8 * half:8 * half + 8, :],
                                   pf2[2][:, half * 512:(half + 1) * 512].rearrange("p (h d) -> p h d", h=8), ALU.add,
                                   ["Sb"] + [["B4"], ["B5"]][half], ["Sb"])
                    if U["sout"]:
                        ssd_state_io(Sb, nss_d[s, 1], False, ["Sb"])
            m5 = A.mark()
            wflat = [A.alloc(f"wbufs{i}", [128, 8448], BF16) for i in range(2)]
            raw = [A.alloc(f"raw{i}", [128, nseq, L + 2], F32) for i in range(2)]
            acc = [A.alloc(f"acc{i}", [128, nseq, L], F32) for i in range(2)]
            for i in range(2):
                memset("pool", raw[i][:], 0.0, [f"raw{i}"])
            nslot = 0
            for (c0, ncols, tile0) in [(C_XS, 1024, 0), (C_B, 1056, 8)]:
                wv, wn = load_slot(nslot % 2, [(c0, ncols, 0)], ncols, key=(f"Sxs{ui}" if tile0 == 0 else f"Sbc{ui}"))
                nslot += 1
                for ti in range(8):
                    gi = tile0 + ti
                    rw, ac = raw[gi % 2], acc[gi % 2]
                    rn, an = f"raw{gi % 2}", f"acc{gi % 2}"
                    for tg in range(ntg):
                        bk, bn = fm_tile(wv, wn, ti, tg)
                        if nseq == 1:
                            cp("act", rw[:, 0, 1 + tg * 512:1 + (tg + 1) * 512], bk, [bn], [rn])
                        else:
                            cp("act", rw[:, :, 1:L + 1], bk.rearrange("p (s l) -> p s l", s=nseq), [bn], [rn])
                    act(ac[:], rw[:, :, 1:L + 1], AF.Identity, [rn, "convw", "convb"], [an],
                        bias=convb[:, gi:gi + 1], scale=convw[:, gi, 1:2])
                    stt("dve", ac[:], rw[:, :, 0:L], convw[:, gi, 0:1], ac[:], ALU.mult, ALU.add, [rn, an], [an])
                    stt("dve", ac[:], rw[:, :, 2:L + 2], convw[:, gi, 2:3], ac[:], ALU.mult, ALU.add, [rn, an], [an])
                    act(xbcT[:, gi, :].rearrange("p (s l) -> p s l", s=nseq), ac[:], AF.Silu, [an], [f"xbcT{gi}"])
                if tile0 == 8:
                    for c in range(nch):
                        bk, bn = tm_tile(wv, wn, c, 1024, 32)
                        tt("dve", dtraw[:, c, :], bk[:, 0:32], vecs[:, 32:64], ALU.add, [bn], ["dtraw"])
            ssd_prep()
            ssd_bwd_sweep()
            wv, wn = load_slot(nslot % 2, [(C_Z, 1024, 0)], 1024, key=f"Sz{ui}")
            for c in range(nch):
                for cg in range(2):
                    bk, bn = tm_tile(wv, wn, c, cg * 512, 512)
                    act(zs[:, c, cg * 512:(cg + 1) * 512], bk, AF.Silu, [bn], [f"zs{c}"])
            S.barrier()
            A.release(m5)
            dumps[f"Sproj{ui}"] = lambda: (dump(xbcT[:].rearrange("p a b -> p (a b)"), "xbcT"), dump(zs[:].rearrange("p a b -> p (a b)"), "zs"),
                                          dump(dtraw[:].rearrange("p a b -> p (a b)"), "dtraw"))
            chk(f"Sproj{ui}")

            prefetch(f"O{ui}", PRE_H, [(0, 512, 0)], 512, kdim=16, src=wout_v)
            if ui == 1:
                prefetch(f"Ob{ui}", wpre1[:], [(512, 512, 0)], 512, kdim=16, src=wout_v)
            Sf = A.alloc("Sf", [128, 16, 64], F32)
            Sf_bf = A.alloc("Sfbf", [128, 1024], BF16)
            xfm2 = [A.alloc(f"xfm{i}", [128, 16, 64], BF16) for i in range(2)]
            xbm2 = [A.alloc(f"xbm{i}", [128, 16, 64], BF16) for i in range(2)]
            Gfm = A.alloc("Gfm", [128, 4, 128], BF16)
            Gbm = A.alloc("Gbm", [128, 4, 128], BF16)
            RFf2 = [A.alloc(f"RFf{i}", [128, 4, 128], F32) for i in range(2)]
            RFb2 = [A.alloc(f"RFb{i}", [128, 4, 128], F32) for i in range(2)]
            Ef2 = [A.alloc(f"Ef{i}", [128, 4, 128], BF16) for i in range(2)]
            Eb2 = [A.alloc(f"Eb{i}", [128, 4, 128], BF16) for i in range(2)]
            SSf2 = [A.alloc(f"SSf{i}", [128, 4, 128], BF16) for i in range(2)]
            SSb2 = [A.alloc(f"SSb{i}", [128, 4, 128], BF16) for i in range(2)]
            t12 = [A.alloc(f"t1{i}", [128, 4, 64], F32) for i in range(2)]
            t22 = [A.alloc(f"t2{i}", [128, 4, 64], F32) for i in range(2)]
            ys = A.alloc("ys", [128, 16, 64], F32)
            mixs = A.alloc("mixs", [128, 1024], BF16)
            junk2 = A.alloc("junk2", [128, 1024], BF16)
            dsk = vecs[:, 64:80]
            bankA = [(pf2[1][:, 0:512], "B2"), (pf2[2][:, 0:512], "B4")]
            bankB = [(pf2[1][:, 512:1024], "B3"), (pf2[2][:, 512:1024], "B5")]

            for s in range(nseq):
                if U["init"]:
                    ssd_state_io(Sf, s0s_d[0], True, SfN)
                else:
                    memset("dve", Sf[:], 0.0, SfN)
                for c in range(nchs):
                    cu = s * nchs + c
                    tok = slice(cu * 128, (cu + 1) * 128)
                    upd = (c < nchs - 1) or U["sout"]
                    par = step[0] % 2
                    step[0] += 1
                    xfm, xbm, xwm = xfm2[par], xbm2[par], xwm2[par]
                    xfn, xbn, xwn = f"xfm{par}", f"xbm{par}", f"xwm{par}"
                    Btok, btn = Btok2[par], f"Btok{par}"
                    xsv = xs_transposes(cu)
                    tt("dve", xfm[:], xsv, bc_last(dtv[:, cu, 0:16], 64), ALU.mult, ["B6", "dtv"], [xfn])
                    tt("dve", xbm[:], xsv, bc_last(dtv[:, cu, 16:32], 64), ALU.mult, ["B6", "dtv"], [xbn])
                    if upd:
                        tt("dve", xwm[:], xsv, bc_last(wts[:, cu, 0:16], 64), ALU.mult, ["B6", "wts"], [xwn])
                        b_transposes(cu, Btok, btn)
                    pG = pf2[0][:, 0:512]
                    for g in range(4):
                        mm(pG[:, g * 128:(g + 1) * 128], xbcT[:, 8 + g, tok], xbcT[:, 12 + g, tok], True, True,
                           [f"xbcT{8 + g}", f"xbcT{12 + g}"], ["B0"])
                    pG3 = pG.rearrange("p (g i) -> p g i", g=4)
                    tt("dve", Gfm[:], pG3, bc_mid(Uincl, 4), ALU.mult, ["B0"], ["Gfm"])
                    tt("dve", Gbm[:], pG3, bc_mid(Lincl, 4), ALU.mult, ["B0"], ["Gbm"])
                    cp("act", Sf_bf[:], Sf[:].rearrange("p h d -> p (h d)"), SfN, ["Sfbf"])
                    for g in range(4):
                        gp = g % 2
                        hs = slice(4 * g, 4 * g + 4)
                        RFf, RFb, Ef, Eb, SSf, SSb, t1, t2 = RFf2[gp], RFb2[gp], Ef2[gp], Eb2[gp], SSf2[gp], SSb2[gp], t12[gp], t22[gp]
                        nRFf, nRFb, nEf, nEb, nSSf, nSSb, nt1, nt2 = [f"{n}{gp}" for n in ("RFf", "RFb", "Ef", "Eb", "SSf", "SSb", "t1", "t2")]
                        tt("pool", RFf[:], bc_mid(Uincl, 4), bc_last(la[:, cu, 4 * g:4 * g + 4], 128), ALU.mult, ["la"], [nRFf])
                        tt("pool", RFb[:], bc_mid(Lincl, 4), bc_last(la[:, cu, 16 + 4 * g:16 + 4 * g + 4], 128), ALU.mult, ["la"], [nRFb])
                        pAf = pf2[0][:, 0:512]
                        pAb = pf2[0][:, 512:1024]
                        mm(pAf, Lstr, RFf[:].rearrange("p h i -> p (h i)"), True, True, [nRFf], ["B0"])
                        mm(pAb, Ustr, RFb[:].rearrange("p h i -> p (h i)"), True, True, [nRFb], ["B1"])
                        act(Ef[:].rearrange("p h i -> p (h i)"), pAf, AF.Exp, ["B0"], [nEf])
                        act(Eb[:].rearrange("p h i -> p (h i)"), pAb, AF.Exp, ["B1"], [nEb])
                        tt("dve", SSf[:], Ef[:], bc_mid(Gfm[:, g, :], 4), ALU.mult, [nEf, "Gfm"], [nSSf])
                        tt("pool", SSb[:], Eb[:], bc_mid(Gbm[:, g, :], 4), ALU.mult, [nEb, "Gbm"], [nSSb])
                        (bA, nA), (bB, nB) = bankA[gp], bankB[gp]
                        pY, pYf, pYb, pU = bA[:, 0:256], bA[:, 256:512], bB[:, 0:256], bB[:, 256:512]
                        for hl in range(4):
                            h = 4 * g + hl
                            o = pY[:, hl * 64:(hl + 1) * 64]
                            mm(o, SSf[:, hl, :], xfm[:, h, :], True, False, [nSSf, xfn], [nA])
                            mm(o, SSb[:, hl, :], xbm[:, h, :], False, False, [nSSb, xbn], [nA])
                            mm(o, xbcT[:, h // 2, tok], Ddiag[:, h // 2, (h % 2) * 64:(h % 2) * 64 + 64], False, True, [f"xbcT{h // 2}", "Ddiag"], [nA])
                        mm(pYf, xbcT[:, 12 + g, tok], Sf_bf[:, g * 256:(g + 1) * 256], True, True, [f"xbcT{12 + g}", "Sfbf"], [nA])
                        mm(pYb, xbcT[:, 12 + g, tok], Sb_all[:, cu, g * 256:(g + 1) * 256], True, True, [f"xbcT{12 + g}", f"Sball{cu}"], [nB])
                        if upd:
                            mm(pU, Btok[:, g, :], xwm[:, hs, :].rearrange("p h d -> p (h d)"), True, True, [btn, xwn], [nB])
                        tt("dve", t1[:], pYf.rearrange("p (h d) -> p h d", h=4), bc_last(decs[:, cu, 4 * g:4 * g + 4], 64), ALU.mult,
                           [nA, "decs"], [nt1])
                        tt("dve", t2[:], pYb.rearrange("p (h d) -> p h d", h=4), bc_last(decs[:, cu, 16 + 4 * g:16 + 4 * g + 4], 64), ALU.mult,
                           [nB, "decs"], [nt2])
                        tt("pool", t1[:], t1[:], t2[:], ALU.add, [nt1, nt2], [nt1])
                        tt("dve", ys[:, hs, :], pY.rearrange("p (h d) -> p h d", h=4), t1[:], ALU.add, [nA, nt1], [f"ys{g}"])
                        if upd:
                            for hl_ in range(4):
                                h_ = 4 * g + hl_
                                act(Sf[:, h_, :], Sf[:, h_, :], AF.Identity, [f"Sf{g}", "decs", "Sfbf"], [f"Sf{g}"],
                                    scale=decs[:, cu, 64 + h_:65 + h_])
                            tt("dve", Sf[:, hs, :], Sf[:, hs, :], pU.rearrange("p (h d) -> p h d", h=4), ALU.add, [f"Sf{g}", nB], [f"Sf{g}"])
                    ysf = ys[:].rearrange("p h d -> p (h d)")
                    ysn = [f"ys{g}" for g in range(4)]
                    tt("dve", ysf, ysf, zs[:, cu, :], ALU.mult, ysn + [f"zs{cu}"], ysn)
                    act(junk2[:], ysf, AF.Square, ysn, ["junk2", "sst"], accum_out=st[:, 16:17])
                    act(st[:, 17:18], st[:, 16:17], AF.Ln, ["sst"], ["sst"], scale=1.0 / 1024, bias=EPS)
                    act(st[:, 18:19], st[:, 17:18], AF.Exp, ["sst"], ["sst"], scale=-0.5)
                    stt("dve", mixs[:], ysf, st[:, 18:19], snw_b[:], ALU.mult, ALU.mult, ysn + ["sst", "snwb"], ["mixs"])
                    for t in range(8):
                        tr(pb[1][:, t * 128:(t + 1) * 128], mixs[:, t * 128:(t + 1) * 128], ident_b[:], ["mixs", "identb"], ["B7"])
                    cp("act", mixT[:, 0:8, tok], pb[1][:].rearrange("p (t c) -> p t c", t=8), ["B7"], [f"mixTs{cu}"])
                if U["sout"]:
                    ssd_state_io(Sf, nss_d[s, 0], False, SfN)
            S.barrier()
            A.release(m4)
            dumps[f"Sscan{ui}"] = lambda: dump(mixT[:, 0:8, :].rearrange("p a b -> p (a b)"), "mixTs")
            chk(f"Sscan{ui}")

            if ui == 0:
                wpre1 = A.alloc("wpre1", [128, 8192], BF16)
                wpreX = A.alloc("wpreX", [128, 8448], BF16)
            m6 = A.mark()
            wflat_o = [A.alloc(f"wbufo{i}", [128, 8448], BF16) for i in range(2)]
            xbuf = [A.alloc(f"xbufo{i}", [128, 1024], F32) for i in range(2)]
            yo = [A.alloc(f"yo{i}", [128, 1024], F32) for i in range(2)]
            junk3 = A.alloc("junk3", [128, 1024], BF16)
            Gbu = A.alloc("Gbu", [128, 1024], F32)
            Dg = A.alloc("Dg", [128, 8, 128], F32)
            for kk in range(8):
                ts("dve", Dg[:, kk, :], ident_f, G_fm[:, kk, r:r + 1], None, ALU.mult, None, ["Gfm_"], ["Dg"])
            for half in range(2):
                bk, bn = next_bank()
                mm(bk, ones_f, Dg[:, 4 * half:4 * half + 4, :].rearrange("p a b -> p (a b)"), True, True, ["Dg"], [bn])
                cp("act", Gbu[:, half * 512:(half + 1) * 512], bk, [bn], ["Gbu"])
            wo = []
            for half in range(2):
                if half == 0 and f"O{ui}" in pre:
                    wo.append(pre.pop(f"O{ui}"))
                    continue
                if half == 1 and f"Ob{ui}" in pre:
                    wo.append(pre.pop(f"Ob{ui}"))
                    continue
                wv = wflat_o[half][:, 0:8192].rearrange("p (k c) -> p k c", k=16)
                dma("pool", wv, wout_v[:, :, half * 512:(half + 1) * 512], (), [f"wo{half}"], f"wb{half}")
                wo.append(wv)
            if ui == 0:
                prefetch("Rqk1", wpre1[:], [(C_Q, 512, 0), (C_K, 512, 512)], 1024, after=["wo1"])
                prefetch("Rv1", wpreX[:], [(C_V, 1024, 0)], 1024, after=["pre_Rqk1"])
            for c in range(nch):
                tok = slice(c * 128, (c + 1) * 128)
                xb, xr = xbuf[c % 2], f"xbo{c % 2}"
                yb, yn = yo[c % 2], f"yo{c % 2}"
                dma("sp", xb[:], x_d[tok0 + c * 128:tok0 + (c + 1) * 128, :], (), [xr], xr)
                pt = pf2[c % 2]
                for half in range(2):
                    for k in range(16):
                        mm(pt[:, half * 512:(half + 1) * 512], mixT[:, k, tok], wo[half][:, k, :], k == 0, k == 15,
                           [f"wo{half}"], [f"B{2 * (c % 2) + half}"])
                sc = st[:, 24 + 4 * (c % 2):28 + 4 * (c % 2)]
                sr = f"sto{c % 2}"
                for half in range(2):
                    act(junk3[:, half * 512:(half + 1) * 512], pt[:, half * 512:(half + 1) * 512], AF.Square,
                        [f"B{2 * (c % 2) + half}"], ["junk3", sr + str(half)], accum_out=sc[:, half:half + 1])
                tt("dve", sc[:, 2:3], sc[:, 0:1], sc[:, 1:2], ALU.add, [sr + "0", sr + "1"], [sr])
                act(sc[:, 3:4], sc[:, 2:3], AF.Ln, [sr], [sr], scale=1.0 / 1024, bias=EPS)
                act(sc[:, 3:4], sc[:, 3:4], AF.Exp, [sr], [sr], scale=-0.5)
                for half in range(2):
                    cs_ = slice(half * 512, (half + 1) * 512)
                    stt("dve", yb[:, cs_], pt[:, cs_], sc[:, 3:4], Gbu[:, cs_], ALU.mult, ALU.mult,
                        [f"B{2 * (c % 2) + half}", sr, "Gbu"], [yn + str(half)])
                tt("pool", yb[:], yb[:], xb[:], ALU.add, [yn + "0", yn + "1", xr], [yn + "0", yn + "1"])
                dma("sp", y_d[tok0 + c * 128:tok0 + (c + 1) * 128, :], yb[:], [yn + "0", yn + "1"], (), "out")
            S.barrier()
            A.release(m6)
            dumps[f"Oproj{ui}"] = lambda: dump(mixT[:].rearrange("p a b -> p (a b)"), "mixT")
            chk(f"Oproj{ui}")


    try:
        body()
    except _Stop:
        pass
    S.barrier()
    plan = S.finalize()
    run_plan(nc, plan, list(S.dma_totals.keys()))
    es.close()
    return nc


def _consts():
    t = np.arange(128)
    T_, I_ = t[:, None], t[None, :]
    mats = [
        (T_ <= I_), (T_ >= I_), (T_ < I_), (T_ > I_), np.ones((128, 128)), np.eye(128),
        np.maximum(I_ - T_, 0), np.maximum(T_ - I_, 0),
        np.broadcast_to(I_ + 1, (128, 128)), np.broadcast_to(128 - I_, (128, 128)),
    ]
    nn = np.arange(64)
    partner = np.where((nn % 32) < 16, nn + 16, nn - 16)
    Pm = np.zeros((128, 128), np.float32)
    for n2 in range(128):
        Pm[(n2 // 64) * 64 + partner[n2 % 64], n2] = 1.0
    mats.append(Pm)
    cm = np.concatenate([np.asarray(m, dtype=np.float32) for m in mats], axis=1)
    cnt = np.zeros((128, 16), np.float32)
    cnt[:, 0:8] = (127 - t)[:, None]
    cnt[:, 8:16] = t[:, None]
    sel = np.zeros((2, 256), np.float32)
    sel[0, 0:128] = 1.0
    sel[1, 128:256] = 1.0
    L = 1024
    half = 32
    inv = (10000.0 ** (-np.arange(0, half, 2, dtype=np.float32) / half)).astype(np.float32)
    row = (np.arange(L) // 64).astype(np.float32)
    col = (np.arange(L) % 64).astype(np.float32)
    ang_r = (row[:, None] * inv[None, :]).astype(np.float32)
    ang_c = (col[:, None] * inv[None, :]).astype(np.float32)
    cosT = np.zeros((128, L), np.float32)
    sinT = np.zeros((128, L), np.float32)
    for p in range(128):
        n = p % 64
        f = n % 16
        ang = ang_r[:, f] if n < 32 else ang_c[:, f]
        sign = -1.0 if (n % 32) < 16 else 1.0
        cosT[p] = np.cos(ang)
        sinT[p] = sign * np.sin(ang)
    return cm, cnt, sel, cosT, sinT


def _win_dev(w_in):
    z = w_in[:, 0:1024]
    xs = w_in[:, 1024:2048]
    B = w_in[:, 2048:2560]
    C = w_in[:, 2560:3072]
    dt = w_in[:, 3072:3104]
    q = w_in[:, 3104:3616]
    k = w_in[:, 3616:4128]
    v = w_in[:, 4128:5152]
    g = w_in[:, 5152:6176]
    n = np.arange(64)
    partner = np.where((n % 32) < 16, n + 16, n - 16)
    perm = (np.arange(8)[:, None] * 64 + partner[None, :]).reshape(-1)
    return np.ascontiguousarray(np.concatenate([q, q[:, perm], k, k[:, perm], v, g, xs, B, C, dt, z], axis=1))


_NC_CACHE = {}


def kernel(x_prompt, x_sample, state_ssd, state_ret, c, c_ctx, w_mod, b_mod, norm_pre_w,
           norm_post_w, w_in, conv_w, conv_b, ssd_A_log, ssd_dt_bias, ssd_D, ssd_norm_w,
           ret_decay, ret_norm_w, w_out):
    f = lambda a: np.ascontiguousarray(np.asarray(a, dtype=np.float32))
    x_prompt, x_sample, state_ssd, state_ret, c, c_ctx = map(f, (x_prompt, x_sample, state_ssd, state_ret, c, c_ctx))
    cm, cnt, sel, cosT, sinT = _consts()
    win = _win_dev(f(w_in)[0])
    cw = f(conv_w)[0]
    convw = np.ascontiguousarray(cw.reshape(3, 16, 128).transpose(2, 1, 0).reshape(128, 48))
    convb = np.ascontiguousarray(f(conv_b)[0].reshape(16, 128).T)
    vecs = np.concatenate([f(ssd_A_log)[0].reshape(-1), f(ssd_dt_bias)[0].reshape(-1), f(ssd_D)[0].reshape(-1),
                           f(ret_decay)[0].reshape(-1)])[None, :]
    shared = {
        "w_mod": f(w_mod)[0], "b_mod": f(b_mod)[0][None, :], "npre": f(norm_pre_w)[0][None, :],
        "npost": f(norm_post_w)[0][None, :], "w_in": win, "convw": convw, "convb": convb,
        "vecs": np.ascontiguousarray(vecs), "snw": f(ssd_norm_w)[0][None, :], "rnw": f(ret_norm_w)[0][None, :],
        "w_out": f(w_out)[0], "cm": cm, "cnt": cnt, "sel": sel, "cosT": cosT, "sinT": sinT,
    }
    in_maps = []
    for i in range(NCORES):
        xs_ = np.concatenate([x_sample[i], x_prompt[2 * i], x_prompt[2 * i + 1]], axis=0)
        cond = np.stack([c[i], c_ctx], axis=0)
        cond_fm = np.ascontiguousarray(cond.reshape(2, 8, 128).transpose(2, 1, 0).reshape(128, 16))
        m = dict(shared)
        m.update({"x": np.ascontiguousarray(xs_), "cond_fm": cond_fm,
                  "s0s": np.ascontiguousarray(state_ssd[i, 0]), "s0r": np.ascontiguousarray(state_ret[i, 0])})
        in_maps.append(m)
    if "nc" not in _NC_CACHE:
        _NC_CACHE["nc"] = build_nc()
    res = run_bass_kernel_spmd(_NC_CACHE["nc"], in_maps, core_ids=list(range(NCORES)))
    y_prompt = np.zeros((16, 256, 1024), np.float32)
    y_sample = np.zeros((8, 1024, 1024), np.float32)
    ns_s = np.zeros((16, 1, 2, 16, 128, 64), np.float32)
    ns_r = np.zeros((16, 1, 2, 8, 64, 128), np.float32)
    for i in range(NCORES):
        rr = res.results[i]
        y = rr["y"]
        y_sample[i] = y[0:1024]
        y_prompt[2 * i] = y[1024:1280]
        y_prompt[2 * i + 1] = y[1280:1536]
        ns_s[2 * i:2 * i + 2, 0] = rr["ns_s"]
        ns_r[2 * i:2 * i + 2, 0] = rr["ns_r"]
    return (y_prompt, y_sample, ns_s, ns_r)
```

```python
import numpy as np
from contextlib import ExitStack
import concourse.bass as bass
import concourse.mybir as mybir
from concourse.bass_utils import run_bass_kernel_spmd

F32 = mybir.dt.float32
BF16 = mybir.dt.bfloat16
AF = mybir.ActivationFunctionType
ALU = mybir.AluOpType
AX = mybir.AxisListType

ENGS = ("pe", "act", "dve", "pool", "sp")
EPS = 1e-6
NCORES = 8
C_Q, C_QSW, C_K, C_KSW, C_V, C_G, C_XS, C_B, C_C, C_DT, C_Z = 0, 512, 1024, 1536, 2048, 3072, 4096, 5120, 5632, 6144, 6176
NCOLS = 7200


class Op:
    __slots__ = ("eng", "fn", "deps", "signal", "count", "dma_sem", "dma_count", "cost")

    def __init__(self, eng, fn, cost=0.3):
        self.eng = eng
        self.fn = fn
        self.cost = cost
        self.deps = []
        self.signal = False
        self.count = None
        self.dma_sem = None
        self.dma_count = None


class Sched:
    def __init__(self):
        self.ops = []
        self.last_writer = {}
        self.readers = {}
        self.dma_totals = {}
        self.dma_rr = 0
        self.dma_rr_sw = 0
        self.dma_last = {}

    NDMA = 24
    NDMA_SW = 4

    def add(self, eng, fn, reads=(), writes=(), dma=None, cost=0.3):
        op = Op(eng, fn, cost)
        seen = set()
        xb = [r for r in reads if len(r) == 2 and r[0] == "B" and r[1].isdigit()]
        if xb:
            reads = [r for r in reads if r not in xb]
            writes = list(writes) + xb

        def dep(d):
            if d is not None and id(d) not in seen:
                seen.add(id(d))
                op.deps.append(d)
        for r in reads:
            dep(self.last_writer.get(r))
        for r in writes:
            dep(self.last_writer.get(r))
            for rd in self.readers.get(r, ()):
                dep(rd)
        if dma is not None:
            if eng == "pool":
                dma = "sw%d" % (self.dma_rr_sw % self.NDMA_SW)
                self.dma_rr_sw += 1
            else:
                dma = "hw%d" % (self.dma_rr % self.NDMA)
                self.dma_rr += 1
            dep(self.dma_last.get(dma))
            self.dma_last[dma] = op
            op.dma_sem = dma
            self.dma_totals[dma] = self.dma_totals.get(dma, 0) + 16
            op.dma_count = self.dma_totals[dma]
        for r in reads:
            self.readers.setdefault(r, []).append(op)
        for r in writes:
            self.last_writer[r] = op
            self.readers[r] = []
        self.ops.append(op)
        return op

    def barrier(self):
        lastc = {}
        lastd = {}
        for op in self.ops:
            if op.fn is None:
                continue
            if op.dma_sem is not None:
                lastd[op.dma_sem] = op
            else:
                lastc[op.eng] = op
        allops = list(lastc.values()) + list(lastd.values())
        for e in ENGS:
            op = Op(e, None)
            op.deps = list(allops)
            self.ops.append(op)
        self.last_writer = {}
        self.readers = {}

    def _sched_segment(self, seg):
        n = len(seg)
        idx = {id(op): i for i, op in enumerate(seg)}
        indeg = [0] * n
        succ = [[] for _ in range(n)]
        for i, op in enumerate(seg):
            for d in op.deps:
                j = idx.get(id(d))
                if j is not None:
                    indeg[i] += 1
                    succ[j].append(i)
        tail = [0.0] * n
        for i in range(n - 1, -1, -1):
            m = 0.0
            for s in succ[i]:
                if tail[s] > m:
                    m = tail[s]
            tail[i] = m + seg[i].cost
        ready = [i for i in range(n) if indeg[i] == 0]
        t_eng = {e: 0.0 for e in ENGS}
        rdy = [0.0] * n
        order = []
        while ready:
            bi = None
            bk = None
            for i in ready:
                op = seg[i]
                st = t_eng[op.eng]
                if rdy[i] > st:
                    st = rdy[i]
                key = (st, -tail[i], i)
                if bk is None or key < bk:
                    bk = key
                    bi = i
            ready.remove(bi)
            op = seg[bi]
            st = bk[0]
            if op.dma_sem is not None:
                t_eng[op.eng] = st + 0.08
                fin = st + op.cost
            else:
                t_eng[op.eng] = st + op.cost
                fin = st + op.cost
            order.append(op)
            for s in succ[bi]:
                indeg[s] -= 1
                f2 = fin if seg[s].eng == op.eng and op.dma_sem is None else fin + 0.15
                if f2 > rdy[s]:
                    rdy[s] = f2
                if indeg[s] == 0:
                    ready.append(s)
        assert len(order) == n
        return order

    def schedule(self):
        new_ops = []
        seg = []
        lastc = {}

        def flush():
            for o in self._sched_segment(seg):
                new_ops.append(o)
                if o.dma_sem is None:
                    lastc[o.eng] = o
        for op in self.ops:
            if op.fn is None:
                if seg:
                    flush()
                    seg = []
                op.deps = [d for d in op.deps if d.dma_sem is not None] + list(lastc.values())
                new_ops.append(op)
            else:
                seg.append(op)
        if seg:
            flush()
        self.ops = new_ops

    def finalize(self, reorder=True):
        if reorder:
            self.schedule()
        for op in self.ops:
            for d in op.deps:
                if d.dma_sem is None and not (d.eng == op.eng and d.eng == "pe"):
                    d.signal = True
        counts = {e: 0 for e in ENGS}
        for op in self.ops:
            if op.signal and op.dma_sem is None:
                counts[op.eng] += 1
                op.count = counts[op.eng]
        known = {e: {} for e in ENGS}
        plan = {e: [] for e in ENGS}
        for op in self.ops:
            waits = {}
            for d in op.deps:
                if d.dma_sem is not None:
                    key = ("dma", d.dma_sem)
                    val = d.dma_count
                else:
                    if d.eng == op.eng and d.eng == "pe":
                        continue
                    key = ("eng", d.eng)
                    val = d.count
                if known[op.eng].get(key, 0) >= val:
                    continue
                if waits.get(key, 0) < val:
                    waits[key] = val
            for k, v in waits.items():
                known[op.eng][k] = v
            plan[op.eng].append((op, list(waits.items())))
        return plan


def run_plan(nc, plan, dma_keys):
    with ExitStack() as es:
        sems = {}
        for e in ENGS:
            sems[("eng", e)] = es.enter_context(nc.semaphore("s_" + e))
        for k in dma_keys:
            sems[("dma", k)] = es.enter_context(nc.semaphore("d_" + str(k)))
        block = es.enter_context(nc.Block())

        def body(ename):
            def f(eng):
                for op, waits in plan[ename]:
                    for key, val in waits:
                        eng.wait_ge(sems[key], val)
                    if op.fn is None:
                        continue
                    ins = op.fn(eng)
                    if op.dma_sem is not None:
                        ins.then_inc(sems[("dma", op.dma_sem)], 16)
                    elif op.signal:
                        ins.then_inc(sems[("eng", ename)], 1)
            return f

        block.tensor(body("pe"))
        block.scalar(body("act"))
        block.vector(body("dve"))
        block.gpsimd(body("pool"))
        block.sync(body("sp"))


class Arena:
    def __init__(self, nc, limit=229000):
        self.nc = nc
        self.top = 16640
        self.n = 0
        self.limit = limit
        self.peak = 0

    def alloc(self, name, shape, dt):
        nb = 4 if dt == F32 else 2
        size = nb
        for s in shape[1:]:
            size *= s
        size = (size + 63) // 64 * 64
        off = self.top
        self.top += size
        self.peak = max(self.peak, self.top)
        assert self.top <= self.limit, f"SBUF arena overflow at {name}: {self.top}"
        self.n += 1
        return self.nc.alloc_sbuf_tensor_at(f"{name}_{self.n}", list(shape), dt, offset=off)

    def mark(self):
        return self.top

    def release(self, m):
        self.top = m


class _Stop(Exception):
    pass


def build_nc(dbg=False, stop=None):
    nc = bass.Bass("TRN2", target_bir_lowering=False)

    dumps = {}

    def chk(name):
        if stop is not None and name == stop:
            if name in dumps:
                dumps[name]()
            raise _Stop()

    def din(name, shape):
        return nc.dram_tensor(name, list(shape), F32, kind="ExternalInput").ap()

    def dout(name, shape):
        return nc.dram_tensor(name, list(shape), F32, kind="ExternalOutput").ap()

    x_d = din("x", [1536, 1024])
    condfm_d = din("cond_fm", [128, 16])
    wmod_d = din("w_mod", [1024, 3072])
    bmod_d = din("b_mod", [1, 3072])
    npre_d = din("npre", [1, 1024])
    npost_d = din("npost", [1, 1024])
    win_d = din("w_in", [1024, NCOLS])
    convw_d = din("convw", [128, 48])
    convb_d = din("convb", [128, 16])
    vecs_d = din("vecs", [1, 96])
    snw_d = din("snw", [1, 1024])
    rnw_d = din("rnw", [1, 1024])
    wout_d = din("w_out", [2048, 1024])
    s0s_d = din("s0s", [2, 16, 128, 64])
    s0r_d = din("s0r", [2, 8, 64, 128])
    cm_d = din("cm", [128, 1408])
    cnt_d = din("cnt", [128, 16])
    sel_d = din("sel", [2, 256])
    cos_d = din("cosT", [128, 1024])
    sin_d = din("sinT", [128, 1024])
    y_d = dout("y", [1536, 1024])
    nss_d = dout("ns_s", [2, 2, 16, 128, 64])
    nsr_d = dout("ns_r", [2, 2, 8, 64, 128])

    dbg_d = dout("dbg", [128, 32768]) if dbg else None
    dbg_off = [0]
    S = Sched()
    A = Arena(nc)

    def dump(ap2d, name):
        if not dbg:
            return
        P, N = ap2d.shape
        S.barrier()
        scr = A.alloc("dbgscr", [128, 2048], F32)
        for c0 in range(0, N, 2048):
            n = min(2048, N - c0)
            S.add("dve", (lambda o, i_: lambda e: e.tensor_copy(out=o, in_=i_))(scr[0:P, 0:n], ap2d[:, c0:c0 + n]), [], ["dbgscr"])
            S.add("sp", (lambda o, i_: lambda e: e.dma_start(out=o, in_=i_))(dbg_d[0:P, dbg_off[0]:dbg_off[0] + n], scr[0:P, 0:n]), ["dbgscr"], [], dma="out")
            S.barrier()
            dbg_off[0] += n
        print("DUMP", name, dbg_off[0] - N, N)
    es = ExitStack()
    pf2 = [es.enter_context(nc.psum_tensor(f"pf{i}", [128, 1024], F32)) for i in range(3)]
    pb = [es.enter_context(nc.psum_tensor(f"pb{i}", [128, 1024], BF16)) for i in range(2)]
    banks = [(pf2[i][:, h * 512:(h + 1) * 512], f"B{2 * i + h}") for i in range(3) for h in range(2)]
    bank_rr = [0]

    def next_bank():
        b = banks[bank_rr[0] % 6]
        bank_rr[0] += 1
        return b

    def fsz(ap):
        n = 1
        for s_ in ap.shape[1:]:
            n *= s_
        return n

    def mm(out, lhsT, rhs, start, stop, reads, writes):
        passes = 4 if lhsT.dtype == F32 else 1
        S.add("pe", lambda e: e.matmul(out, lhsT=lhsT, rhs=rhs, start=start, stop=stop), reads, writes,
              cost=0.07 + passes * fsz(out) / 2400.0)

    def tr(out, in_, ident, reads, writes):
        S.add("pe", lambda e: e.transpose(out=out, in_=in_, identity=ident), reads, writes,
              cost=0.12 * (4 if in_.dtype == F32 else 1))

    def act(out, in_, func, reads, writes, bias=None, scale=None, accum_out=None):
        kw = {}
        if bias is not None:
            kw["bias"] = bias
        if scale is not None:
            kw["scale"] = scale
        if accum_out is not None:
            kw["accum_out"] = accum_out
        S.add("act", lambda e: e.activation(out=out, in_=in_, func=func, **kw), reads, writes, cost=0.25 + fsz(out) / 1200.0)

    def ecost(eng, out):
        return (0.08 + fsz(out) / 960.0) if eng == "dve" else (0.15 + fsz(out) / 480.0)

    def tt(eng, out, in0, in1, op, reads, writes):
        S.add(eng, lambda e: e.tensor_tensor(out=out, in0=in0, in1=in1, op=op), reads, writes, cost=ecost(eng, out))

    def ts(eng, out, in0, s1, s2, op0, op1, reads, writes):
        if s2 is None:
            S.add(eng, lambda e: e.tensor_scalar(out=out, in0=in0, scalar1=s1, scalar2=None, op0=op0), reads, writes, cost=ecost(eng, out))
        else:
            S.add(eng, lambda e: e.tensor_scalar(out=out, in0=in0, scalar1=s1, scalar2=s2, op0=op0, op1=op1), reads, writes, cost=ecost(eng, out))

    def stt(eng, out, in0, scalar, in1, op0, op1, reads, writes):
        S.add(eng, lambda e: e.scalar_tensor_tensor(out=out, in0=in0, scalar=scalar, in1=in1, op0=op0, op1=op1), reads, writes, cost=ecost(eng, out))

    def cp(eng, out, in_, reads, writes):
        if eng == "act":
            S.add("act", lambda e: e.copy(out=out, in_=in_), reads, writes, cost=0.25 + fsz(out) / 1200.0)
        else:
            S.add(eng, lambda e: e.tensor_copy(out=out, in_=in_), reads, writes, cost=ecost(eng, out))

    def dma(eng, out, in_, reads, writes, sem):
        nbytes = out.shape[0] * fsz(out) * (4 if out.dtype == F32 else 2)
        S.add(eng, lambda e: e.dma_start(out=out, in_=in_), reads, writes, dma=sem, cost=2.0 + nbytes / 150e3)

    def tred(out, in_, reads, writes):
        S.add("dve", lambda e: e.tensor_reduce(out=out, in_=in_, axis=AX.X, op=ALU.add), reads, writes, cost=ecost("dve", in_))

    def memset(eng, ap, val, writes):
        S.add(eng, lambda e: e.memset(ap, val), (), writes, cost=ecost(eng, ap))

    def bc_last(ap, n):
        return ap.unsqueeze(2).to_broadcast([ap.shape[0], ap.shape[1], n])

    def bc_mid(ap, n):
        return ap.unsqueeze(1).to_broadcast([ap.shape[0], n, ap.shape[1]])

    def body():
        cm = A.alloc("cm", [128, 1408], F32)
        Uincl, Lincl, Ustr, Lstr, ones_f, ident_f, P1, P2, iota1, iota2, Pm_f = [cm[:, i * 128:(i + 1) * 128] for i in range(11)]
        ident_b = A.alloc("identb", [128, 128], BF16)
        Pm_b = A.alloc("Pmb", [128, 128], BF16)
        cnt = A.alloc("cnt", [128, 16], F32)
        sel = A.alloc("sel", [2, 256], F32)
        vecs = A.alloc("vecs", [128, 96], F32)
        convw = A.alloc("convw", [128, 16, 3], F32)
        convb = A.alloc("convb", [128, 16], F32)
        G_fm = A.alloc("Gfm_", [128, 8, 2], F32)
        A_fm = A.alloc("Afm", [128, 8, 2], F32)
        sh_fm = A.alloc("shfm", [128, 8, 2], F32)
        negA = A.alloc("negA", [128, 32], F32)
        lamb = A.alloc("lamb", [128, 16], F32)
        lamq = A.alloc("lamq", [128, 8], F32)
        decq = A.alloc("decq", [128, 8], F32)
        tails = A.alloc("tails", [128, 16], F32)
        Eret = A.alloc("Eret", [128, 8, 128], F32)
        DRm = [[A.alloc(f"DRm{d}{hh}", [128, 4, 128], F32) for hh in range(2)] for d in range(2)]
        st = A.alloc("st", [128, 64], F32)
        Dcol = A.alloc("Dcol", [128, 8], F32)
        Ddiag = A.alloc("Ddiag", [128, 8, 128], BF16)
        hT = A.alloc("hT", [128, 8, 1024], BF16)
        mixT = A.alloc("mixT", [128, 16, 1024], BF16)
        mtmp = A.mark()
        DRf = A.alloc("DRf", [128, 4, 128], F32)
        DRb = A.alloc("DRb", [128, 4, 128], F32)

        dma("sp", cm[:], cm_d, (), ["cm"], "cst")
        dma("sp", cnt[:], cnt_d, (), ["cnt"], "cst")
        dma("sp", sel[:], sel_d, (), ["sel"], "cst")
        dma("sp", vecs[:], vecs_d.partition_broadcast(128), (), ["vecs"], "cst")
        dma("sp", convw[:].rearrange("p a b -> p (a b)"), convw_d, (), ["convw"], "cst")
        dma("sp", convb[:], convb_d, (), ["convb"], "cst")
        S.barrier()
        cp("dve", ident_b[:], ident_f, [], ["identb"])
        cp("dve", Pm_b[:], Pm_f, [], ["Pmb"])
        act(negA[:], vecs[:, 0:32], AF.Exp, [], ["negA"])
        ts("dve", negA[:], negA[:], -1.0, None, ALU.mult, None, ["negA"], ["negA"])
        act(lamb[:], vecs[:, 80:96], AF.Exp, [], ["lamb"])
        ts("dve", lamb[:], lamb[:], -1.0, None, ALU.mult, None, ["lamb"], ["lamb"])
        for d in range(2):
            for t in range(4):
                for hh in range(2):
                    cp("dve", lamq[64 * hh:64 * hh + 64, d * 4 + t:d * 4 + t + 1],
                       lamb[64 * hh:64 * hh + 64, d * 8 + 2 * t + hh:d * 8 + 2 * t + hh + 1], ["lamb"], ["lamq"])
        act(decq[:], lamq[:], AF.Exp, ["lamq"], ["decq"], scale=128.0)
        tt("dve", tails[:], lamb[:], cnt[:], ALU.mult, ["lamb"], ["tails"])
        act(tails[:], tails[:], AF.Exp, ["tails"], ["tails"])
        for h in range(8):
            ts("dve", Eret[:, h, :], P1, lamb[:, h:h + 1], None, ALU.mult, None, ["lamb"], ["Eret"])
            stt("dve", Eret[:, h, :], P2, lamb[:, 8 + h:9 + h], Eret[:, h, :], ALU.mult, ALU.add, ["lamb", "Eret"], ["Eret"])
        act(Eret[:], Eret[:], AF.Exp, ["Eret"], ["Eret"])
        tt("dve", Eret[:], Eret[:], bc_mid(ident_f, 8), ALU.add, ["Eret"], ["Eret"])
        for t in range(4):
            act(DRf[:, t, :], iota1, AF.Exp, ["lamq"], ["DRf"], scale=lamq[:, t:t + 1])
            act(DRb[:, t, :], iota2, AF.Exp, ["lamq"], ["DRb"], scale=lamq[:, 4 + t:5 + t])
        for t in range(8):
            for hh in range(2):
                cp("dve", Dcol[64 * hh:64 * hh + 64, t:t + 1], vecs[64 * hh:64 * hh + 64, 64 + 2 * t + hh:65 + 2 * t + hh], [], ["Dcol"])
        for t in range(8):
            ts("dve", Ddiag[:, t, :], ident_f, Dcol[:, t:t + 1], None, ALU.mult, None, ["Dcol"], ["Ddiag"])
        for d, src, sn_ in [(0, DRf, "DRf"), (1, DRb, "DRb")]:
            for hh in range(2):
                memset("pool", DRm[d][hh][:], 0.0, [f"DRm{d}{hh}"])
                cp("dve", DRm[d][hh][64 * hh:64 * hh + 64, :, :], src[64 * hh:64 * hh + 64, :, :], [sn_, f"DRm{d}{hh}"], [f"DRm{d}{hh}"])

        xbuf = [A.alloc(f"xbuf{i}", [128, 1024], F32) for i in range(3)]
        xnb = [A.alloc(f"xnb{i}", [128, 1024], BF16) for i in range(8)]
        junk = A.alloc("junk", [128, 1024], BF16)
        m0 = A.mark()
        wflat = [A.alloc(f"wbuf{i}", [128, 8448], BF16) for i in range(2)]
        condfm = A.alloc("condfm", [128, 8, 2], F32)
        csil = A.alloc("csil", [128, 8, 2], BF16)
        modrow = A.alloc("modrow", [2, 3072], F32)
        bmod2 = A.alloc("bmod2", [2, 3072], F32)
        np2 = A.alloc("np2", [2, 2048], F32)
        dma("sp", condfm[:].rearrange("p a b -> p (a b)"), condfm_d, (), ["condfm"], "cst")
        dma("sp", bmod2[:], bmod_d.partition_broadcast(2), (), ["bmod2"], "cst")
        dma("sp", np2[:, 0:1024], npre_d.partition_broadcast(2), (), ["np2a"], "cst")
        dma("sp", np2[:, 1024:2048], npost_d.partition_broadcast(2), (), ["np2b"], "cst")
        act(csil[:], condfm[:], AF.Silu, ["condfm"], ["csil"])
        wmod_v = wmod_d.rearrange("(k p) c -> p k c", p=128)
        for s in range(3):
            wv = wflat[s % 2][:, 0:8192].rearrange("p (k c) -> p k c", k=8)
            dma("pool", wv, wmod_v[:, :, s * 1024:(s + 1) * 1024], (), [f"wb{s % 2}"], f"wb{s % 2}")
            for half in range(2):
                bk, bn = next_bank()
                for k in range(8):
                    mm(bk[0:2, :], csil[:, k, :], wv[:, k, half * 512:(half + 1) * 512], k == 0, k == 7,
                       ["csil", f"wb{s % 2}"], [bn])
                c0 = s * 1024 + half * 512
                tt("dve", modrow[:, c0:c0 + 512], bk[0:2, :], bmod2[:, c0:c0 + 512], ALU.add, [bn, "bmod2"], [f"mod{s}"])
        stt("dve", modrow[:, 1024:2048], modrow[:, 1024:2048], 1.0, np2[:, 0:1024], ALU.add, ALU.mult, ["mod1", "np2a"], ["mod1"])
        tt("dve", modrow[:, 2048:3072], modrow[:, 2048:3072], np2[:, 1024:2048], ALU.mult, ["mod2", "np2b"], ["mod2"])
        bk, bn = next_bank()
        for k in range(8):
            tr(bk[:, 2 * k:2 * k + 2], modrow[0:2, 1024 + k * 128:1024 + (k + 1) * 128], ident_f[0:2, 0:2], ["mod1"], [bn])
            tr(bk[:, 16 + 2 * k:16 + 2 * k + 2], modrow[0:2, k * 128:(k + 1) * 128], ident_f[0:2, 0:2], ["mod0"], [bn])
        cp("dve", A_fm[:].rearrange("p a b -> p (a b)"), bk[:, 0:16], [bn], ["Afm"])
        cp("dve", sh_fm[:].rearrange("p a b -> p (a b)"), bk[:, 16:32], [bn], ["shfm"])
        bk, bn = next_bank()
        for kk in range(8):
            tr(bk[:, 2 * kk:2 * kk + 2], modrow[0:2, 2048 + kk * 128:2048 + (kk + 1) * 128], ident_f[0:2, 0:2], ["mod2"], [bn])
        cp("dve", G_fm[:].rearrange("p a b -> p (a b)"), bk[:, 0:16], [bn], ["Gfm_"])
        A.release(mtmp)
        chk("stage0")

        units = [
            dict(tok0=0, T=1024, nseq=1, L=1024, r=0, rope=True, init=True, sout=False),
            dict(tok0=1024, T=512, nseq=2, L=256, r=1, rope=False, init=False, sout=True),
        ]
        win_v = win_d.rearrange("(k p) c -> p k c", p=128)
        wout_v = wout_d.rearrange("(k p) c -> p k c", p=128)
        PRE_A = mixT[:, 0:8, :].rearrange("p a b -> p (a b)")
        PRE_B = mixT[:, 8:16, :].rearrange("p a b -> p (a b)")
        PRE_H = hT[:].rearrange("p a b -> p (a b)")
        pre = {}

        def prefetch(key, flat, pieces, ncols, kdim=8, src=None, after=()):
            src = win_v if src is None else src
            wv = flat[:, 0:kdim * ncols].rearrange("p (k c) -> p k c", k=kdim)
            for (c0, n, d0) in pieces:
                dma("pool", wv[:, :, d0:d0 + n], src[:, :, c0:c0 + n], list(after), [f"pre_{key}"], "pre")
            pre[key] = wv

        def prenorm(c, row0, xbuf_, xnb_, junk_, tag):
            xb = xbuf_[c % len(xbuf_)]
            xr = f"{tag}xb{c % len(xbuf_)}"
            xn_, xnr = xnb_[c % len(xnb_)], f"{tag}xnb{c % len(xnb_)}"
            dma("sp", xb[:], x_d[row0 + c * 128:row0 + (c + 1) * 128, :], (), [xr], xr)
            sc = st[:, 4 * (c % 2):4 * (c % 2) + 4]
            sr = f"st{c % 2}"
            act(junk_[:], xb[:], AF.Square, [xr], [tag + "junk", sr], accum_out=sc[:, 0:1])
            act(sc[:, 1:2], sc[:, 0:1], AF.Ln, [sr], [sr], scale=1.0 / 1024, bias=EPS)
            act(sc[:, 2:3], sc[:, 1:2], AF.Exp, [sr], [sr], scale=-0.5)
            ts("dve", xn_[:], xb[:], sc[:, 2:3], None, ALU.mult, None, [xr, sr], [xnr])
            return xn_, xnr

        pre_xn = {}
        prefetch("Rq0", PRE_A, [(C_Q, 512, 0), (C_K, 512, 512)], 1024, after=["wb0", "wb1"])
        prefetch("Rv0", PRE_B, [(C_V, 1024, 0)], 1024, after=["pre_Rq0"])
        if dbg:
            dbg_d = {}

        for ui, U in enumerate(units):
            tok0, T, nseq, L, r = U["tok0"], U["T"], U["nseq"], U["L"], U["r"]
            nch = T // 128
            nchs = L // 128
            ntg = T // 512
            hTr = lambda c: f"hT{c}"

            have_xn = ui in pre_xn
            if have_xn:
                m1 = pre_xn[ui]["mark"]
                xnb = pre_xn[ui]["xnb"]
            elif ui > 0:
                m1 = A.mark()
                xbuf = [A.alloc(f"xbuf{i}", [128, 1024], F32) for i in range(3)]
                xnb = [A.alloc(f"xnb{i}", [128, 1024], BF16) for i in range(2)]
                junk = A.alloc("junk", [128, 1024], BF16)
            else:
                m1 = mtmp
            for c in range(nch):
                if have_xn:
                    xn_, xnr = xnb[c % len(xnb)], f"u{ui}xnb{c % len(xnb)}"
                else:
                    xn_, xnr = prenorm(c, tok0, xbuf, xnb, junk, "")
                for k_ in range(8):
                    half = k_ // 4
                    tr(pb[half][:, (k_ % 4) * 128:(k_ % 4 + 1) * 128], xn_[:, k_ * 128:(k_ + 1) * 128], ident_b[:], [xnr, "identb"], [f"B{6 + half}"])
                for k_ in range(8):
                    half = k_ // 4
                    o = hT[:, k_, c * 128:(c + 1) * 128]
                    i_ = pb[half][:, (k_ % 4) * 128:(k_ % 4 + 1) * 128]
                    if half == 0:
                        act(o, i_, AF.Identity, ["B6", "Afm", "shfm"], [hTr(c)],
                            bias=sh_fm[:, k_, r:r + 1], scale=A_fm[:, k_, r:r + 1])
                    else:
                        ts("dve", o, i_, A_fm[:, k_, r:r + 1], sh_fm[:, k_, r:r + 1], ALU.mult, ALU.add,
                           ["B7", "Afm", "shfm"], [hTr(c)])
            S.barrier()
            A.release(m1)
            dumps[f"hT{ui}"] = lambda: dump(hT[:].rearrange("p a b -> p (a b)"), "hT")
            chk(f"hT{ui}")

            def load_slot(slot, pieces, ncols, key=None):
                if key is not None and key in pre:
                    return pre.pop(key), []
                wv = wflat[slot][:, 0:8 * ncols].rearrange("p (k c) -> p k c", k=8)
                names = []
                for pi, (c0, n, d0) in enumerate(pieces):
                    nm = f"wb{slot}p{pi}"
                    dma("pool", wv[:, :, d0:d0 + n], win_v[:, :, c0:c0 + n], (), [nm], f"wb{slot}")
                    names.append(nm)
                return wv, names

            def fm_tile(wv, wnames, ct, tg):
                bk, bn = next_bank()
                hr = [hTr(c) for c in range(tg * 4, tg * 4 + 4)]
                for k in range(8):
                    mm(bk, wv[:, k, ct * 128:(ct + 1) * 128], hT[:, k, tg * 512:(tg + 1) * 512], k == 0, k == 7,
                       wnames + hr, [bn])
                return bk, bn

            def tm_tile(wv, wnames, c, c0, n):
                bk, bn = next_bank()
                for k in range(8):
                    mm(bk[:, 0:n], hT[:, k, c * 128:(c + 1) * 128], wv[:, k, c0:c0 + n], k == 0, k == 7,
                       wnames + [hTr(c)], [bn])
                return bk, bn

            m2 = A.mark()
            qT = A.alloc("qT", [128, 4, T], BF16)
            kT = A.alloc("kT", [128, 4, T], BF16)
            kTm = [A.alloc(f"kTm{hh}", [128, 4, T], BF16) for hh in range(2)]
            for hh in range(2):
                memset("pool", kTm[hh][:], 0.0, [f"kTm{hh}"])
            v_tok = A.alloc("vtok", [128, nch, 1024], BF16)
            gs = A.alloc("gs", [128, nch, 1024], BF16)
            rnw_b = A.alloc("rnwb", [128, 1024], F32)
            dma("sp", rnw_b[:], rnw_d.partition_broadcast(128), (), ["rnwb"], "cst")
            Srf = A.alloc("Srf", [128, 4, 128], F32)
            Srb = A.alloc("Srb", [128, 4, 128], F32)
            Srb_all = A.alloc("Srball", [128, nch, 512], BF16)
            ktl2 = [A.alloc(f"ktl{i}", [128, 512], BF16) for i in range(2)]
            rstep = [0]

            def ret_state_io(S_t, dram3, load, sname):
                dv = dram3.rearrange("(t hh) n v -> hh n t v", hh=2)
                for hh in range(2):
                    if load:
                        dma("sp", S_t[64 * hh:64 * hh + 64, :, :], dv[hh], (), [sname], "sio")
                    else:
                        dma("sp", dv[hh], S_t[64 * hh:64 * hh + 64, :, :], [sname], (), "out")

            def ret_update(S_t, sname, cu, tail_lo, dq_lo):
                tok = slice(cu * 128, (cu + 1) * 128)
                kp = rstep[0] % 2
                rstep[0] += 1
                ktl, ktn = ktl2[kp], f"ktl{kp}"
                for t in range(4):
                    tr(pb[0][:, t * 128:(t + 1) * 128], kT[:, t, tok], ident_b[:], ["kT", "identb"], ["B6"])
                tt("dve", ktl[:].rearrange("p (h n) -> p h n", h=8), pb[0][:, 0:512].rearrange("p (h n) -> p h n", h=8),
                   bc_last(tails[:, tail_lo:tail_lo + 8], 64), ALU.mult, ["B6", "tails"], [ktn])
                for t in range(4):
                    mm(pf2[2][:, t * 256:(t + 1) * 256], ktl[:, t * 128:(t + 1) * 128], v_tok[:, cu, t * 256:(t + 1) * 256],
                       True, True, [ktn, f"vtok{cu}"], ["B4", "B5"])
                tt("dve", S_t[:], S_t[:], bc_last(decq[:, dq_lo:dq_lo + 4], 128), ALU.mult, [sname, "decq"], [sname])
                pu = pf2[2][:].rearrange("p (t c) -> p t c", t=4)
                tt("dve", S_t[0:64, :, :], S_t[0:64, :, :], pu[0:64, :, 0:128], ALU.add, [sname, "B4", "B5"], [sname])
                tt("dve", S_t[64:128, :, :], S_t[64:128, :, :], pu[64:128, :, 128:256], ALU.add, [sname, "B4", "B5"], [sname])

            def ret_bwd_sweep():
                for s in range(nseq):
                    if U["init"]:
                        ret_state_io(Srb, s0r_d[1], True, "Srb")
                    else:
                        memset("dve", Srb[:], 0.0, ["Srb"])
                    for c in reversed(range(nchs)):
                        cu = s * nchs + c
                        cp("act", Srb_all[:, cu, :], Srb[:].rearrange("p a b -> p (a b)"), ["Srb"], [f"Srball{cu}"])
                        if c > 0 or U["sout"]:
                            ret_update(Srb, "Srb", cu, 8, 4)
                    if U["sout"]:
                        ret_state_io(Srb, nsr_d[s, 1], False, "Srb")
            m3 = A.mark()
            wflat = [A.alloc(f"wbufr{i}", [128, 8448], BF16) for i in range(2)]
            if U["rope"]:
                cosT = A.alloc("cosT", [128, 1024], F32)
                sinT = A.alloc("sinT", [128, 1024], F32)
                rt1_ = [A.alloc(f"rt1{i}", [128, 512], F32) for i in range(2)]
                rt2_ = [A.alloc(f"rt2{i}", [128, 512], F32) for i in range(2)]
                qbf = [A.alloc(f"qbf{i}", [128, 512], BF16) for i in range(2)]
                dma("sp", cosT[:], cos_d, (), ["cosT"], "cst")
                dma("sp", sinT[:], sin_d, (), ["sinT"], "cst")
                wv, wn = load_slot(0, [(C_Q, 512, 0), (C_K, 512, 512)], 1024, key=f"Rq{ui}")
                it_ = 0
                for ti in range(8):
                    dst, dname, kscale = (qT, "qT", None) if ti < 4 else (kT, "kT", 0.125)
                    tl = ti % 4
                    for tg in range(ntg):
                        par = it_ % 2
                        it_ += 1
                        rt1, rt2 = rt1_[par], rt2_[par]
                        r1n, r2n, qbn = f"rt1{par}", f"rt2{par}", f"qbf{par}"
                        ba, bna = fm_tile(wv, wn, ti, tg)
                        cp("act", qbf[par][:], ba, [bna], [qbn])
                        bb, bnb = next_bank()
                        mm(bb, Pm_b[:], qbf[par][:], True, True, [qbn, "Pmb"], [bnb])
                        cs = cosT[:, tg * 512:(tg + 1) * 512]
                        sn = sinT[:, tg * 512:(tg + 1) * 512]
                        if kscale is None:
                            tt("dve", rt1[:], ba, cs, ALU.mult, [bna, "cosT"], [r1n])
                            tt("dve", rt2[:], bb, sn, ALU.mult, [bnb, "sinT"], [r2n])
                        else:
                            stt("dve", rt1[:], ba, kscale, cs, ALU.mult, ALU.mult, [bna, "cosT"], [r1n])
                            stt("dve", rt2[:], bb, kscale, sn, ALU.mult, ALU.mult, [bnb, "sinT"], [r2n])
                        tt("pool", dst[:, tl, tg * 512:(tg + 1) * 512], rt1[:], rt2[:], ALU.add, [r1n, r2n], [dname])
                        if dname == "kT":
                            for hh in range(2):
                                cp("act", kTm[hh][64 * hh:64 * hh + 64, tl, tg * 512:(tg + 1) * 512],
                                   kT[64 * hh:64 * hh + 64, tl, tg * 512:(tg + 1) * 512], ["kT", f"kTm{hh}"], [f"kTm{hh}"])
                nslot = 1
            else:
                wv, wn = load_slot(0, [(C_Q, 512, 0), (C_K, 512, 512)], 1024, key=f"Rqk{ui}")
                for ti in range(8):
                    for tg in range(ntg):
                        bk, bn = fm_tile(wv, wn, ti, tg)
                        if ti < 4:
                            cp("act", qT[:, ti, tg * 512:(tg + 1) * 512], bk, [bn], ["qT"])
                        else:
                            S.add("act", (lambda o, i_: lambda e: e.mul(out=o, in_=i_, mul=0.125))(kT[:, ti - 4, tg * 512:(tg + 1) * 512], bk), [bn], ["kT"])
                            for hh in range(2):
                                cp("dve", kTm[hh][64 * hh:64 * hh + 64, ti - 4, tg * 512:(tg + 1) * 512],
                                   kT[64 * hh:64 * hh + 64, ti - 4, tg * 512:(tg + 1) * 512], ["kT", f"kTm{hh}"], [f"kTm{hh}"])
                nslot = 1
            for (c0, dst, dname, fn) in [(C_V, v_tok, "vtok", None), (C_G, gs, "gs", AF.Silu)]:
                wv, wn = load_slot(nslot % 2, [(c0, 1024, 0)], 1024, key=(f"Rv{ui}" if fn is None else None))
                nslot += 1
                for c in range(nch):
                    for cg in range(2):
                        bk, bn = tm_tile(wv, wn, c, cg * 512, 512)
                        o = dst[:, c, cg * 512:(cg + 1) * 512]
                        if fn is None:
                            cp("act", o, bk, [bn], [f"{dname}{c}"])
                        else:
                            act(o, bk, fn, [bn], [f"{dname}{c}"])
                    if fn is not None:
                        tt("pool", dst[:, c, :], dst[:, c, :], rnw_b[:], ALU.mult, [f"{dname}{c}", "rnwb"], [f"{dname}{c}"])
                if fn is None:
                    ret_bwd_sweep()
            S.barrier()
            A.release(m3)
            dumps[f"Rproj{ui}"] = lambda: (dump(qT[:].rearrange("p a b -> p (a b)"), "qT"), dump(kT[:].rearrange("p a b -> p (a b)"), "kT"),
                                          dump(v_tok[:].rearrange("p a b -> p (a b)"), "v"), dump(gs[:].rearrange("p a b -> p (a b)"), "gs"))
            chk(f"Rproj{ui}")

            prefetch(f"Sxs{ui}", PRE_A, [(C_XS, 1024, 0)], 1024)
            if ui == 1:
                prefetch(f"Sbc{ui}", wpreX[:], [(C_B, 1056, 0)], 1056)
                prefetch(f"Sz{ui}", wpre1[:], [(C_Z, 1024, 0)], 1024)
            Srf_bf = A.alloc("Srfbf", [128, 4, 128], BF16)
            Sm2 = [A.alloc(f"Sm{i}", [128, 1024], BF16) for i in range(2)]
            qfm2 = [[[A.alloc(f"qfm{i}{d}{hh}", [128, 4, 128], BF16) for hh in range(2)] for d in range(2)] for i in range(2)]
            yr2 = [A.alloc(f"yr{i}", [128, 8, 128], F32) for i in range(2)]
            sq2 = [A.alloc(f"sq{i}", [128, 8, 128], F32) for i in range(2)]
            mixr2 = [A.alloc(f"mixr{i}", [128, 1024], BF16) for i in range(2)]
            gst2 = [A.alloc(f"gst{i}", [128, 48], F32) for i in range(2)]

            for s in range(nseq):
                if U["init"]:
                    ret_state_io(Srf, s0r_d[0], True, "Srf")
                else:
                    memset("dve", Srf[:], 0.0, ["Srf"])
                for c in range(nchs):
                    cu = s * nchs + c
                    tok = slice(cu * 128, (cu + 1) * 128)
                    cpar = cu % 2
                    Sm, qfm, yr, sq, mixr, gst = Sm2[cpar], qfm2[cpar], yr2[cpar], sq2[cpar], mixr2[cpar], gst2[cpar]
                    P_ = f"c{cpar}"
                    for h in range(8):
                        t, hh = h // 2, h % 2
                        ps_ = slice(64 * hh, 64 * hh + 64)
                        mm(pf2[0][:, h * 128:(h + 1) * 128], kTm[hh][:, t, tok], qT[:, t, tok], True, True,
                           [f"kTm{hh}", "qT"], [f"B{h // 4}"])
                    for half in range(2):
                        cs_ = slice(half * 512, (half + 1) * 512)
                        tt("dve", Sm[:, cs_], pf2[0][:, cs_], Eret[:].rearrange("p h i -> p (h i)")[:, cs_], ALU.mult,
                           [f"B{half}", "Eret"], [P_ + f"Sm{half}"])
                    chk("Rs_sc")
                    for d in range(2):
                        for hh in range(2):
                            tt("pool", qfm[d][hh][:], qT[:, :, tok], DRm[d][hh][:], ALU.mult, ["qT"], [P_ + f"qfm{d}{hh}"])
                    cp("act", Srf_bf[:], Srf[:], ["Srf"], ["Srfbf"])
                    for h in range(8):
                        t, hh = h // 2, h % 2
                        ps_ = slice(64 * hh, 64 * hh + 64)
                        o = pf2[1][:, h * 128:(h + 1) * 128]
                        wn_ = [f"B{2 + h // 4}"]
                        mm(o, Sm[:, h * 128:(h + 1) * 128], v_tok[:, cu, h * 128:(h + 1) * 128], True, False,
                           [P_ + f"Sm{h // 4}", f"vtok{cu}"], wn_)
                        mm(o, qfm[0][hh][:, t, :], Srf_bf[:, t, :], False, False, [P_ + f"qfm0{hh}", "Srfbf"], wn_)
                        mm(o, qfm[1][hh][:, t, :], Srb_all[:, cu, t * 128:(t + 1) * 128], False, True, [P_ + f"qfm1{hh}", f"Srball{cu}"], wn_)
                    chk("Rs_y")
                    yrf = yr[:].rearrange("p h v -> p (h v)")
                    sqf = sq[:].rearrange("p h v -> p (h v)")
                    for h in range(8):
                        half = h // 4
                        src_ = pf2[1][:, h * 128:(h + 1) * 128]
                        act(yr[:, h, :], src_, AF.Identity, [f"B{2 + half}"], [P_ + f"yr{half}", P_ + "gst0"], accum_out=gst[:, h:h + 1])
                        act(sq[:, h, :], src_, AF.Square, [f"B{2 + half}"], [P_ + "sq", P_ + "gst1"], accum_out=gst[:, 8 + h:9 + h])
                    ts("dve", gst[:, 16:24], gst[:, 0:8], 1.0 / 128, None, ALU.mult, None, [P_ + "gst0"], [P_ + "gst2"])
                    tt("dve", gst[:, 24:32], gst[:, 16:24], gst[:, 16:24], ALU.mult, [P_ + "gst2"], [P_ + "gst3"])
                    stt("dve", gst[:, 32:40], gst[:, 8:16], 1.0 / 128, gst[:, 24:32], ALU.mult, ALU.subtract, [P_ + "gst1", P_ + "gst3"], [P_ + "gst4"])
                    act(gst[:, 40:48], gst[:, 32:40], AF.Ln, [P_ + "gst4"], [P_ + "gst5"], bias=EPS)
                    act(gst[:, 40:48], gst[:, 40:48], AF.Exp, [P_ + "gst5"], [P_ + "gst5"], scale=-0.5)
                    tt("dve", yr[:], yr[:], bc_last(gst[:, 16:24], 128), ALU.subtract, [P_ + "yr0", P_ + "yr1", P_ + "gst2"], [P_ + "yr0", P_ + "yr1"])
                    tt("pool", yr[:], yr[:], bc_last(gst[:, 40:48], 128), ALU.mult, [P_ + "yr0", P_ + "yr1", P_ + "gst5"], [P_ + "yr0", P_ + "yr1"])
                    tt("dve", mixr[:], yrf, gs[:, cu, :], ALU.mult, [P_ + "yr0", P_ + "yr1", f"gs{cu}"], [P_ + "mixr"])
                    chk("Rs_gn")
                    for t in range(8):
                        tr(pb[1][:, t * 128:(t + 1) * 128], mixr[:, t * 128:(t + 1) * 128], ident_b[:], [P_ + "mixr", "identb"], ["B7"])
                    cp("act", mixT[:, 8:16, tok], pb[1][:].rearrange("p (t c) -> p t c", t=8), ["B7"], [f"mixTr{cu}"])
                    if c < nchs - 1 or U["sout"]:
                        ret_update(Srf, "Srf", cu, 0, 0)
                if U["sout"]:
                    ret_state_io(Srf, nsr_d[s, 0], False, "Srf")
            S.barrier()
            A.release(m2)
            dumps[f"Rscan{ui}"] = lambda: dump(mixT[:, 8:16, :].rearrange("p a b -> p (a b)"), "mixTr")
            chk(f"Rscan{ui}")

            m4 = A.mark()
            xbcT = A.alloc("xbcT", [128, 16, T], BF16)
            zs = A.alloc("zs", [128, nch, 1024], BF16)
            dtraw = A.alloc("dtraw", [128, nch, 32], F32)
            dtv = A.alloc("dtv", [128, nch, 32], F32)
            la = A.alloc("la", [128, nch, 32], F32)
            decs = A.alloc("decs", [128, nch, 96], F32)
            wts = A.alloc("wts", [128, nch, 32], F32)
            snw_b = A.alloc("snwb", [128, 1024], F32)
            dma("sp", snw_b[:], snw_d.partition_broadcast(128), (), ["snwb"], "cst")
            Sb = A.alloc("Sb", [128, 16, 64], F32)
            Sb_all = A.alloc("Sball", [128, nch, 1024], BF16)
            xwm2 = [A.alloc(f"xwm{i}", [128, 16, 64], BF16) for i in range(2)]
            Btok2 = [A.alloc(f"Btok{i}", [128, 4, 128], BF16) for i in range(2)]
            tmpa = A.alloc("tmpa", [128, nch, 32], F32)
            tmpb = A.alloc("tmpb", [128, nch, 32], F32)
            step = [0]
            SfN = ["Sf0", "Sf1", "Sf2", "Sf3"]

            def ssd_state_io(S_t, dram3, load, snames):
                dv = dram3.rearrange("h n p -> n h p")
                if load:
                    dma("sp", S_t[:], dv, (), snames, "sio")
                else:
                    dma("sp", dv, S_t[:], snames, (), "out")

            def xs_transposes(cu):
                tok = slice(cu * 128, (cu + 1) * 128)
                for t in range(8):
                    tr(pb[0][:, t * 128:(t + 1) * 128], xbcT[:, t, tok], ident_b[:], [f"xbcT{t}", "identb"], ["B6"])
                return pb[0][:].rearrange("p (h d) -> p h d", h=16)

            def b_transposes(cu, Btok, bname):
                tok = slice(cu * 128, (cu + 1) * 128)
                for g in range(4):
                    tr(pb[1][:, g * 128:(g + 1) * 128], xbcT[:, 8 + g, tok], ident_b[:], [f"xbcT{8 + g}", "identb"], ["B7"])
                cp("act", Btok[:].rearrange("p g n -> p (g n)"), pb[1][:, 0:512], ["B7"], [bname])

            def ssd_prep():
                ts("dve", tmpa[:], dtraw[:], -1.0, None, ALU.mult, None, ["dtraw"], ["tmpa"])
                tt("dve", tmpa[:], tmpa[:], dtraw[:], ALU.min, ["tmpa", "dtraw"], ["tmpa"])
                act(tmpa[:], tmpa[:], AF.Exp, ["tmpa"], ["tmpa"])
                act(tmpa[:], tmpa[:], AF.Ln, ["tmpa"], ["tmpa"], bias=1.0)
                ts("dve", tmpb[:], dtraw[:], 0.0, None, ALU.max, None, ["dtraw"], ["tmpb"])
                tt("dve", dtv[:], tmpa[:], tmpb[:], ALU.add, ["tmpa", "tmpb"], ["dtv"])
                tt("dve", la[:], dtv[:], bc_mid(negA[:], nch), ALU.mult, ["dtv", "negA"], ["la"])
                for c in range(nch):
                    o = pf2[0][:, c * 128:c * 128 + 96]
                    wn_ = [f"B{c // 4}"]
                    mm(o[:, 0:16], Uincl, la[:, c, 0:16], True, True, ["la"], wn_)
                    mm(o[:, 16:32], Lincl, la[:, c, 16:32], True, True, ["la"], wn_)
                    mm(o[:, 32:48], Lstr, la[:, c, 0:16], True, True, ["la"], wn_)
                    mm(o[:, 48:64], Ustr, la[:, c, 16:32], True, True, ["la"], wn_)
                    mm(o[:, 64:96], ones_f, la[:, c, 0:32], True, True, ["la"], wn_)
                for hb in range((nch + 3) // 4):
                    c_lo, c_hi = hb * 4, min(nch, hb * 4 + 4)
                    act(decs[:, c_lo:c_hi, :], pf2[0][:, c_lo * 128:c_hi * 128].rearrange("p (c x) -> p c x", x=128)[:, :, 0:96],
                        AF.Exp, [f"B{hb}"], ["decs"])
                tt("dve", wts[:], decs[:, :, 32:64], dtv[:], ALU.mult, ["decs", "dtv"], ["wts"])

            def ssd_bwd_sweep():
                for s in range(nseq):
                    if U["init"]:
                        ssd_state_io(Sb, s0s_d[1], True, ["Sb"])
                    else:
                        memset("dve", Sb[:], 0.0, ["Sb"])
                    for c in reversed(range(nchs)):
                        cu = s * nchs + c
                        cp("act", Sb_all[:, cu, :], Sb[:].rearrange("p h d -> p (h d)"), ["Sb"], [f"Sball{cu}"])
                        if c > 0 or U["sout"]:
                            par = step[0] % 2
                            step[0] += 1
                            xwm, xwn = xwm2[par], f"xwm{par}"
                            Btok, btn = Btok2[par], f"Btok{par}"
                            xsv = xs_transposes(cu)
                            tt("dve", xwm[:], xsv, bc_last(wts[:, cu, 16:32], 64), ALU.mult, ["B6", "wts"], [xwn])
                            b_transposes(cu, Btok, btn)
                            for g in range(4):
                                mm(pf2[2][:, g * 256:(g + 1) * 256], Btok[:, g, :], xwm[:, 4 * g:4 * g + 4, :].rearrange("p h d -> p (h d)"),
                                   True, True, [btn, xwn], [["B4"], ["B4"], ["B5"], ["B5"]][g])
                            tt("dve", Sb[:], Sb[:], bc_last(decs[:, cu, 80:96], 64), ALU.mult, ["Sb", "decs"], ["Sb"])
                            for half in range(2):
                                tt("dve", Sb[:, 8 * half:8 * half + 8, :], Sb[:, 8 * half:8 * half + 8, :],
                                   pf2[2][:, half * 512:(half + 1) * 512].rearrange("p (h d) -> p h d", h=8), ALU.add,
                                   ["Sb"] + [["B4"], ["B5"]][half], ["Sb"])
                    if U["sout"]:
                        ssd_state_io(Sb, nss_d[s, 1], False, ["Sb"])
            m5 = A.mark()
            wflat = [A.alloc(f"wbufs{i}", [128, 8448], BF16) for i in range(2)]
            raw = [A.alloc(f"raw{i}", [128, nseq, L + 2], F32) for i in range(2)]
            acc = [A.alloc(f"acc{i}", [128, nseq, L], F32) for i in range(2)]
            for i in range(2):
                memset("pool", raw[i][:], 0.0, [f"raw{i}"])
            nslot = 0
            for (c0, ncols, tile0) in [(C_XS, 1024, 0), (C_B, 1056, 8)]:
                wv, wn = load_slot(nslot % 2, [(c0, ncols, 0)], ncols, key=(f"Sxs{ui}" if tile0 == 0 else f"Sbc{ui}"))
                nslot += 1
                for ti in range(8):
                    gi = tile0 + ti
                    rw, ac = raw[gi % 2], acc[gi % 2]
                    rn, an = f"raw{gi % 2}", f"acc{gi % 2}"
                    for tg in range(ntg):
                        bk, bn = fm_tile(wv, wn, ti, tg)
                        if nseq == 1:
                            cp("act", rw[:, 0, 1 + tg * 512:1 + (tg + 1) * 512], bk, [bn], [rn])
                        else:
                            cp("act", rw[:, :, 1:L + 1], bk.rearrange("p (s l) -> p s l", s=nseq), [bn], [rn])
                    act(ac[:], rw[:, :, 1:L + 1], AF.Identity, [rn, "convw", "convb"], [an],
                        bias=convb[:, gi:gi + 1], scale=convw[:, gi, 1:2])
                    stt("dve", ac[:], rw[:, :, 0:L], convw[:, gi, 0:1], ac[:], ALU.mult, ALU.add, [rn, an], [an])
                    stt("dve", ac[:], rw[:, :, 2:L + 2], convw[:, gi, 2:3], ac[:], ALU.mult, ALU.add, [rn, an], [an])
                    act(xbcT[:, gi, :].rearrange("p (s l) -> p s l", s=nseq), ac[:], AF.Silu, [an], [f"xbcT{gi}"])
                if tile0 == 8:
                    for c in range(nch):
                        bk, bn = tm_tile(wv, wn, c, 1024, 32)
                        tt("dve", dtraw[:, c, :], bk[:, 0:32], vecs[:, 32:64], ALU.add, [bn], ["dtraw"])
            ssd_prep()
            ssd_bwd_sweep()
            wv, wn = load_slot(nslot % 2, [(C_Z, 1024, 0)], 1024, key=f"Sz{ui}")
            for c in range(nch):
                for cg in range(2):
                    bk, bn = tm_tile(wv, wn, c, cg * 512, 512)
                    act(zs[:, c, cg * 512:(cg + 1) * 512], bk, AF.Silu, [bn], [f"zs{c}"])
            S.barrier()
            A.release(m5)
            dumps[f"Sproj{ui}"] = lambda: (dump(xbcT[:].rearrange("p a b -> p (a b)"), "xbcT"), dump(zs[:].rearrange("p a b -> p (a b)"), "zs"),
                                          dump(dtraw[:].rearrange("p a b -> p (a b)"), "dtraw"))
            chk(f"Sproj{ui}")

            prefetch(f"O{ui}", PRE_H, [(0, 512, 0)], 512, kdim=16, src=wout_v)
            if ui == 1:
                prefetch(f"Ob{ui}", wpre1[:], [(512, 512, 0)], 512, kdim=16, src=wout_v)
            Sf = A.alloc("Sf", [128, 16, 64], F32)
            Sf_bf = A.alloc("Sfbf", [128, 1024], BF16)
            xfm2 = [A.alloc(f"xfm{i}", [128, 16, 64], BF16) for i in range(2)]
            xbm2 = [A.alloc(f"xbm{i}", [128, 16, 64], BF16) for i in range(2)]
            Gfm = A.alloc("Gfm", [128, 4, 128], BF16)
            Gbm = A.alloc("Gbm", [128, 4, 128], BF16)
            RFf2 = [A.alloc(f"RFf{i}", [128, 4, 128], F32) for i in range(2)]
            RFb2 = [A.alloc(f"RFb{i}", [128, 4, 128], F32) for i in range(2)]
            Ef2 = [A.alloc(f"Ef{i}", [128, 4, 128], BF16) for i in range(2)]
            Eb2 = [A.alloc(f"Eb{i}", [128, 4, 128], BF16) for i in range(2)]
            SSf2 = [A.alloc(f"SSf{i}", [128, 4, 128], BF16) for i in range(2)]
            SSb2 = [A.alloc(f"SSb{i}", [128, 4, 128], BF16) for i in range(2)]
            t12 = [A.alloc(f"t1{i}", [128, 4, 64], F32) for i in range(2)]
            t22 = [A.alloc(f"t2{i}", [128, 4, 64], F32) for i in range(2)]
            ys = A.alloc("ys", [128, 16, 64], F32)
            mixs = A.alloc("mixs", [128, 1024], BF16)
            junk2 = A.alloc("junk2", [128, 1024], BF16)
            dsk = vecs[:, 64:80]
            bankA = [(pf2[1][:, 0:512], "B2"), (pf2[2][:, 0:512], "B4")]
            bankB = [(pf2[1][:, 512:1024], "B3"), (pf2[2][:, 512:1024], "B5")]

            for s in range(nseq):
                if U["init"]:
                    ssd_state_io(Sf, s0s_d[0], True, SfN)
                else:
                    memset("dve", Sf[:], 0.0, SfN)
                for c in range(nchs):
                    cu = s * nchs + c
                    tok = slice(cu * 128, (cu + 1) * 128)
                    upd = (c < nchs - 1) or U["sout"]
                    par = step[0] % 2
                    step[0] += 1
                    xfm, xbm, xwm = xfm2[par], xbm2[par], xwm2[par]
                    xfn, xbn, xwn = f"xfm{par}", f"xbm{par}", f"xwm{par}"
                    Btok, btn = Btok2[par], f"Btok{par}"
                    xsv = xs_transposes(cu)
                    tt("dve", xfm[:], xsv, bc_last(dtv[:, cu, 0:16], 64), ALU.mult, ["B6", "dtv"], [xfn])
                    tt("dve", xbm[:], xsv, bc_last(dtv[:, cu, 16:32], 64), ALU.mult, ["B6", "dtv"], [xbn])
                    if upd:
                        tt("dve", xwm[:], xsv, bc_last(wts[:, cu, 0:16], 64), ALU.mult, ["B6", "wts"], [xwn])
                        b_transposes(cu, Btok, btn)
                    pG = pf2[0][:, 0:512]
                    for g in range(4):
                        mm(pG[:, g * 128:(g + 1) * 128], xbcT[:, 8 + g, tok], xbcT[:, 12 + g, tok], True, True,
                           [f"xbcT{8 + g}", f"xbcT{12 + g}"], ["B0"])
                    pG3 = pG.rearrange("p (g i) -> p g i", g=4)
                    tt("dve", Gfm[:], pG3, bc_mid(Uincl, 4), ALU.mult, ["B0"], ["Gfm"])
                    tt("dve", Gbm[:], pG3, bc_mid(Lincl, 4), ALU.mult, ["B0"], ["Gbm"])
                    cp("act", Sf_bf[:], Sf[:].rearrange("p h d -> p (h d)"), SfN, ["Sfbf"])
                    for g in range(4):
                        gp = g % 2
                        hs = slice(4 * g, 4 * g + 4)
                        RFf, RFb, Ef, Eb, SSf, SSb, t1, t2 = RFf2[gp], RFb2[gp], Ef2[gp], Eb2[gp], SSf2[gp], SSb2[gp], t12[gp], t22[gp]
                        nRFf, nRFb, nEf, nEb, nSSf, nSSb, nt1, nt2 = [f"{n}{gp}" for n in ("RFf", "RFb", "Ef", "Eb", "SSf", "SSb", "t1", "t2")]
                        tt("pool", RFf[:], bc_mid(Uincl, 4), bc_last(la[:, cu, 4 * g:4 * g + 4], 128), ALU.mult, ["la"], [nRFf])
                        tt("pool", RFb[:], bc_mid(Lincl, 4), bc_last(la[:, cu, 16 + 4 * g:16 + 4 * g + 4], 128), ALU.mult, ["la"], [nRFb])
                        pAf = pf2[0][:, 0:512]
                        pAb = pf2[0][:, 512:1024]
                        mm(pAf, Lstr, RFf[:].rearrange("p h i -> p (h i)"), True, True, [nRFf], ["B0"])
                        mm(pAb, Ustr, RFb[:].rearrange("p h i -> p (h i)"), True, True, [nRFb], ["B1"])
                        act(Ef[:].rearrange("p h i -> p (h i)"), pAf, AF.Exp, ["B0"], [nEf])
                        act(Eb[:].rearrange("p h i -> p (h i)"), pAb, AF.Exp, ["B1"], [nEb])
                        tt("dve", SSf[:], Ef[:], bc_mid(Gfm[:, g, :], 4), ALU.mult, [nEf, "Gfm"], [nSSf])
                        tt("pool", SSb[:], Eb[:], bc_mid(Gbm[:, g, :], 4), ALU.mult, [nEb, "Gbm"], [nSSb])
                        (bA, nA), (bB, nB) = bankA[gp], bankB[gp]
                        pY, pYf, pYb, pU = bA[:, 0:256], bA[:, 256:512], bB[:, 0:256], bB[:, 256:512]
                        for hl in range(4):
                            h = 4 * g + hl
                            o = pY[:, hl * 64:(hl + 1) * 64]
                            mm(o, SSf[:, hl, :], xfm[:, h, :], True, False, [nSSf, xfn], [nA])
                            mm(o, SSb[:, hl, :], xbm[:, h, :], False, False, [nSSb, xbn], [nA])
                            mm(o, xbcT[:, h // 2, tok], Ddiag[:, h // 2, (h % 2) * 64:(h % 2) * 64 + 64], False, True, [f"xbcT{h // 2}", "Ddiag"], [nA])
                        mm(pYf, xbcT[:, 12 + g, tok], Sf_bf[:, g * 256:(g + 1) * 256], True, True, [f"xbcT{12 + g}", "Sfbf"], [nA])
                        mm(pYb, xbcT[:, 12 + g, tok], Sb_all[:, cu, g * 256:(g + 1) * 256], True, True, [f"xbcT{12 + g}", f"Sball{cu}"], [nB])
                        if upd:
                            mm(pU, Btok[:, g, :], xwm[:, hs, :].rearrange("p h d -> p (h d)"), True, True, [btn, xwn], [nB])
                        tt("dve", t1[:], pYf.rearrange("p (h d) -> p h d", h=4), bc_last(decs[:, cu, 4 * g:4 * g + 4], 64), ALU.mult,
                           [nA, "decs"], [nt1])
                        tt("dve", t2[:], pYb.rearrange("p (h d) -> p h d", h=4), bc_last(decs[:, cu, 16 + 4 * g:16 + 4 * g + 4], 64), ALU.mult,
                           [nB, "decs"], [nt2])
                        tt("pool", t1[:], t1[:], t2[:], ALU.add, [nt1, nt2], [nt1])
                        tt("dve", ys[:, hs, :], pY.rearrange("p (h d) -> p h d", h=4), t1[:], ALU.add, [nA, nt1], [f"ys{g}"])
                        if upd:
                            for hl_ in range(4):
                                h_ = 4 * g + hl_
                                act(Sf[:, h_, :], Sf[:, h_, :], AF.Identity, [f"Sf{g}", "decs", "Sfbf"], [f"Sf{g}"],
                                    scale=decs[:, cu, 64 + h_:65 + h_])
                            tt("dve", Sf[:, hs, :], Sf[:, hs, :], pU.rearrange("p (h d) -> p h d", h=4), ALU.add, [f"Sf{g}", nB], [f"Sf{g}"])
                    ysf = ys[:].rearrange("p h d -> p (h d)")
                    ysn = [f"ys{g}" for g in range(4)]
                    tt("dve", ysf, ysf, zs[:, cu, :], ALU.mult, ysn + [f"zs{cu}"], ysn)
                    act(junk2[:], ysf, AF.Square, ysn, ["junk2", "sst"], accum_out=st[:, 16:17])
                    act(st[:, 17:18], st[:, 16:17], AF.Ln, ["sst"], ["sst"], scale=1.0 / 1024, bias=EPS)
                    act(st[:, 18:19], st[:, 17:18], AF.Exp, ["sst"], ["sst"], scale=-0.5)
                    stt("dve", mixs[:], ysf, st[:, 18:19], snw_b[:], ALU.mult, ALU.mult, ysn + ["sst", "snwb"], ["mixs"])
                    for t in range(8):
                        tr(pb[1][:, t * 128:(t + 1) * 128], mixs[:, t * 128:(t + 1) * 128], ident_b[:], ["mixs", "identb"], ["B7"])
                    cp("act", mixT[:, 0:8, tok], pb[1][:].rearrange("p (t c) -> p t c", t=8), ["B7"], [f"mixTs{cu}"])
                if U["sout"]:
                    ssd_state_io(Sf, nss_d[s, 0], False, SfN)
            S.barrier()
            A.release(m4)
            dumps[f"Sscan{ui}"] = lambda: dump(mixT[:, 0:8, :].rearrange("p a b -> p (a b)"), "mixTs")
            chk(f"Sscan{ui}")

            if ui == 0:
                wpre1 = A.alloc("wpre1", [128, 8192], BF16)
                wpreX = A.alloc("wpreX", [128, 8448], BF16)
                mx1 = A.mark()
                xbuf1 = [A.alloc(f"x1buf{i}", [128, 1024], F32) for i in range(2)]
                xnb1 = [A.alloc(f"x1nb{i}", [128, 1024], BF16) for i in range(4)]
                junk1 = A.alloc("junk1", [128, 1024], BF16)
                for c1 in range(4):
                    prenorm(c1, units[1]["tok0"], xbuf1, xnb1, junk1, "u1")
                pre_xn[1] = dict(mark=mx1, xnb=xnb1)
                prefetch("Rqk1", wpre1[:], [(C_Q, 512, 0), (C_K, 512, 512)], 1024)
                prefetch("Rv1", wpreX[:], [(C_V, 1024, 0)], 1024)
            m6 = A.mark()
            wflat_o = [A.alloc(f"wbufo{i}", [128, 8448], BF16) for i in range(2)]
            xbuf = [A.alloc(f"xbufo{i}", [128, 1024], F32) for i in range(2)]
            yo = [A.alloc(f"yo{i}", [128, 1024], F32) for i in range(2)]
            junk3 = A.alloc("junk3", [128, 1024], BF16)
            Gbu = A.alloc("Gbu", [128, 1024], F32)
            Dg = A.alloc("Dg", [128, 8, 128], F32)
            for kk in range(8):
                ts("dve", Dg[:, kk, :], ident_f, G_fm[:, kk, r:r + 1], None, ALU.mult, None, ["Gfm_"], ["Dg"])
            for half in range(2):
                bk, bn = next_bank()
                mm(bk, ones_f, Dg[:, 4 * half:4 * half + 4, :].rearrange("p a b -> p (a b)"), True, True, ["Dg"], [bn])
                cp("act", Gbu[:, half * 512:(half + 1) * 512], bk, [bn], ["Gbu"])
            wo = []
            for half in range(2):
                if half == 0 and f"O{ui}" in pre:
                    wo.append(pre.pop(f"O{ui}"))
                    continue
                if half == 1 and f"Ob{ui}" in pre:
                    wo.append(pre.pop(f"Ob{ui}"))
                    continue
                wv = wflat_o[half][:, 0:8192].rearrange("p (k c) -> p k c", k=16)
                dma("pool", wv, wout_v[:, :, half * 512:(half + 1) * 512], (), [f"wo{half}"], f"wb{half}")
                wo.append(wv)
            for c in range(nch):
                tok = slice(c * 128, (c + 1) * 128)
                xb, xr = xbuf[c % 2], f"xbo{c % 2}"
                yb, yn = yo[c % 2], f"yo{c % 2}"
                dma("sp", xb[:], x_d[tok0 + c * 128:tok0 + (c + 1) * 128, :], (), [xr], xr)
                pt = pf2[c % 2]
                for half in range(2):
                    for k in range(16):
                        mm(pt[:, half * 512:(half + 1) * 512], mixT[:, k, tok], wo[half][:, k, :], k == 0, k == 15,
                           [f"wo{half}"], [f"B{2 * (c % 2) + half}"])
                sc = st[:, 24 + 4 * (c % 2):28 + 4 * (c % 2)]
                sr = f"sto{c % 2}"
                for half in range(2):
                    act(junk3[:, half * 512:(half + 1) * 512], pt[:, half * 512:(half + 1) * 512], AF.Square,
                        [f"B{2 * (c % 2) + half}"], ["junk3", sr + str(half)], accum_out=sc[:, half:half + 1])
                tt("dve", sc[:, 2:3], sc[:, 0:1], sc[:, 1:2], ALU.add, [sr + "0", sr + "1"], [sr])
                act(sc[:, 3:4], sc[:, 2:3], AF.Ln, [sr], [sr], scale=1.0 / 1024, bias=EPS)
                act(sc[:, 3:4], sc[:, 3:4], AF.Exp, [sr], [sr], scale=-0.5)
                for half in range(2):
                    cs_ = slice(half * 512, (half + 1) * 512)
                    stt("dve", yb[:, cs_], pt[:, cs_], sc[:, 3:4], Gbu[:, cs_], ALU.mult, ALU.mult,
                        [f"B{2 * (c % 2) + half}", sr, "Gbu"], [yn + str(half)])
                tt("pool", yb[:], yb[:], xb[:], ALU.add, [yn + "0", yn + "1", xr], [yn + "0", yn + "1"])
                dma("sp", y_d[tok0 + c * 128:tok0 + (c + 1) * 128, :], yb[:], [yn + "0", yn + "1"], (), "out")
            S.barrier()
            A.release(m6)
            dumps[f"Oproj{ui}"] = lambda: dump(mixT[:].rearrange("p a b -> p (a b)"), "mixT")
            chk(f"Oproj{ui}")


    try:
        body()
    except _Stop:
        pass
    S.barrier()
    plan = S.finalize()
    run_plan(nc, plan, list(S.dma_totals.keys()))
    es.close()
    return nc


def _consts():
    t = np.arange(128)
    T_, I_ = t[:, None], t[None, :]
    mats = [
        (T_ <= I_), (T_ >= I_), (T_ < I_), (T_ > I_), np.ones((128, 128)), np.eye(128),
        np.maximum(I_ - T_, 0), np.maximum(T_ - I_, 0),
        np.broadcast_to(I_ + 1, (128, 128)), np.broadcast_to(128 - I_, (128, 128)),
    ]
    nn = np.arange(64)
    partner = np.where((nn % 32) < 16, nn + 16, nn - 16)
    Pm = np.zeros((128, 128), np.float32)
    for n2 in range(128):
        Pm[(n2 // 64) * 64 + partner[n2 % 64], n2] = 1.0
    mats.append(Pm)
    cm = np.concatenate([np.asarray(m, dtype=np.float32) for m in mats], axis=1)
    cnt = np.zeros((128, 16), np.float32)
    cnt[:, 0:8] = (127 - t)[:, None]
    cnt[:, 8:16] = t[:, None]
    sel = np.zeros((2, 256), np.float32)
    sel[0, 0:128] = 1.0
    sel[1, 128:256] = 1.0
    L = 1024
    half = 32
    inv = (10000.0 ** (-np.arange(0, half, 2, dtype=np.float32) / half)).astype(np.float32)
    row = (np.arange(L) // 64).astype(np.float32)
    col = (np.arange(L) % 64).astype(np.float32)
    ang_r = (row[:, None] * inv[None, :]).astype(np.float32)
    ang_c = (col[:, None] * inv[None, :]).astype(np.float32)
    cosT = np.zeros((128, L), np.float32)
    sinT = np.zeros((128, L), np.float32)
    for p in range(128):
        n = p % 64
        f = n % 16
        ang = ang_r[:, f] if n < 32 else ang_c[:, f]
        sign = -1.0 if (n % 32) < 16 else 1.0
        cosT[p] = np.cos(ang)
        sinT[p] = sign * np.sin(ang)
    return cm, cnt, sel, cosT, sinT


def _win_dev(w_in):
    z = w_in[:, 0:1024]
    xs = w_in[:, 1024:2048]
    B = w_in[:, 2048:2560]
    C = w_in[:, 2560:3072]
    dt = w_in[:, 3072:3104]
    q = w_in[:, 3104:3616]
    k = w_in[:, 3616:4128]
    v = w_in[:, 4128:5152]
    g = w_in[:, 5152:6176]
    n = np.arange(64)
    partner = np.where((n % 32) < 16, n + 16, n - 16)
    perm = (np.arange(8)[:, None] * 64 + partner[None, :]).reshape(-1)
    return np.ascontiguousarray(np.concatenate([q, q[:, perm], k, k[:, perm], v, g, xs, B, C, dt, z], axis=1))


_NC_CACHE = {}


def kernel(x_prompt, x_sample, state_ssd, state_ret, c, c_ctx, w_mod, b_mod, norm_pre_w,
           norm_post_w, w_in, conv_w, conv_b, ssd_A_log, ssd_dt_bias, ssd_D, ssd_norm_w,
           ret_decay, ret_norm_w, w_out):
    f = lambda a: np.ascontiguousarray(np.asarray(a, dtype=np.float32))
    x_prompt, x_sample, state_ssd, state_ret, c, c_ctx = map(f, (x_prompt, x_sample, state_ssd, state_ret, c, c_ctx))
    cm, cnt, sel, cosT, sinT = _consts()
    win = _win_dev(f(w_in)[0])
    cw = f(conv_w)[0]
    convw = np.ascontiguousarray(cw.reshape(3, 16, 128).transpose(2, 1, 0).reshape(128, 48))
    convb = np.ascontiguousarray(f(conv_b)[0].reshape(16, 128).T)
    vecs = np.concatenate([f(ssd_A_log)[0].reshape(-1), f(ssd_dt_bias)[0].reshape(-1), f(ssd_D)[0].reshape(-1),
                           f(ret_decay)[0].reshape(-1)])[None, :]
    shared = {
        "w_mod": f(w_mod)[0], "b_mod": f(b_mod)[0][None, :], "npre": f(norm_pre_w)[0][None, :],
        "npost": f(norm_post_w)[0][None, :], "w_in": win, "convw": convw, "convb": convb,
        "vecs": np.ascontiguousarray(vecs), "snw": f(ssd_norm_w)[0][None, :], "rnw": f(ret_norm_w)[0][None, :],
        "w_out": f(w_out)[0], "cm": cm, "cnt": cnt, "sel": sel, "cosT": cosT, "sinT": sinT,
    }
    in_maps = []
    for i in range(NCORES):
        xs_ = np.concatenate([x_sample[i], x_prompt[2 * i], x_prompt[2 * i + 1]], axis=0)
        cond = np.stack([c[i], c_ctx], axis=0)
        cond_fm = np.ascontiguousarray(cond.reshape(2, 8, 128).transpose(2, 1, 0).reshape(128, 16))
        m = dict(shared)
        m.update({"x": np.ascontiguousarray(xs_), "cond_fm": cond_fm,
                  "s0s": np.ascontiguousarray(state_ssd[i, 0]), "s0r": np.ascontiguousarray(state_ret[i, 0])})
        in_maps.append(m)
    if "nc" not in _NC_CACHE:
        _NC_CACHE["nc"] = build_nc()
    res = run_bass_kernel_spmd(_NC_CACHE["nc"], in_maps, core_ids=list(range(NCORES)))
    y_prompt = np.zeros((16, 256, 1024), np.float32)
    y_sample = np.zeros((8, 1024, 1024), np.float32)
    ns_s = np.zeros((16, 1, 2, 16, 128, 64), np.float32)
    ns_r = np.zeros((16, 1, 2, 8, 64, 128), np.float32)
    for i in range(NCORES):
        rr = res.results[i]
        y = rr["y"]
        y_sample[i] = y[0:1024]
        y_prompt[2 * i] = y[1024:1280]
        y_prompt[2 * i + 1] = y[1280:1536]
        ns_s[2 * i:2 * i + 2, 0] = rr["ns_s"]
        ns_r[2 * i:2 * i + 2, 0] = rr["ns_r"]
    return (y_prompt, y_sample, ns_s, ns_r)
```
